# Optimizing a Trainium2 kernel written in Bass

```python
import jax, jax.numpy as jnp
from jax import lax
import numpy as np

D_MODEL = 1024
BATCH = 4
SEQ = 8192
DEPTH = 1

CONV_WIDTH = D_MODEL
CONV_K = 3
HEAD_DIM = 64
N_Q_HEADS = 8
N_KV_HEADS = 2
GQA_GROUP = N_Q_HEADS // N_KV_HEADS
WINDOW = 128
BLOCK = WINDOW
ROPE_THETA = 10000.0
Q_WIDTH = N_Q_HEADS * HEAD_DIM
KV_WIDTH = N_KV_HEADS * HEAD_DIM
D_FF = 2816
FFN_CONV_K = 3
EPS = 1e-5

IN_SIZES = (CONV_WIDTH, CONV_WIDTH, CONV_WIDTH, Q_WIDTH, KV_WIDTH, KV_WIDTH, D_MODEL, D_MODEL)
IN_WIDTH = sum(IN_SIZES)
IN_SPLITS = tuple(int(c) for c in np.cumsum(IN_SIZES)[:-1])

kernel_name = "hybrid_shortconv_swa_sinks_convffn"


def rmsnorm(x, w):
    xf = x.astype(jnp.float32)
    xf = xf * lax.rsqrt(jnp.mean(xf * xf, axis=-1, keepdims=True) + EPS)
    return xf.astype(x.dtype) * w


def causal_dwconv(v, w, b=None):
    k_taps = w.shape[0]
    s = v.shape[1]
    vp = jnp.pad(v, ((0, 0), (k_taps - 1, 0), (0, 0)))
    y = sum(w[i] * vp[:, i:i + s] for i in range(k_taps))
    if b is not None:
        y = y + b
    return y


def rope_tables(positions, dtype):
    inv_freq = ROPE_THETA ** (-jnp.arange(0, HEAD_DIM, 2, dtype=jnp.float32) / HEAD_DIM)
    ang = positions.astype(jnp.float32)[..., None] * inv_freq
    return jnp.cos(ang)[:, :, None, :].astype(dtype), jnp.sin(ang)[:, :, None, :].astype(dtype)


def apply_rope(x, cos, sin):
    x1, x2 = jnp.split(x, 2, axis=-1)
    return jnp.concatenate([x1 * cos - x2 * sin, x2 * cos + x1 * sin], axis=-1)


def sliding_window_attention(q, k, v, sinks):
    b, s = q.shape[:2]
    nb = s // BLOCK
    qb = q.reshape(b, nb, BLOCK, N_KV_HEADS, GQA_GROUP, HEAD_DIM)

    def with_prev(t):
        tb = t.reshape(b, nb, BLOCK, N_KV_HEADS, HEAD_DIM)
        prev = jnp.pad(tb[:, :-1], ((0, 0), (1, 0), (0, 0), (0, 0), (0, 0)))
        return jnp.concatenate([prev, tb], axis=2)

    kk, vv = with_prev(k), with_prev(v)
    scale = HEAD_DIM ** -0.5
    scores = jnp.einsum('bnqhgd,bnkhd->bnhgqk', qb, kk).astype(jnp.float32) * scale

    q_pos = jnp.arange(BLOCK)[:, None] + BLOCK
    k_pos = jnp.arange(2 * BLOCK)[None, :]
    rel = q_pos - k_pos
    band = (rel >= 0) & (rel < WINDOW)
    not_pad = (jnp.arange(nb)[:, None, None] > 0) | (k_pos[None] >= BLOCK)
    valid = band[None] & not_pad
    scores = jnp.where(valid[None, :, None, None], scores, -jnp.inf)

    sink = sinks.astype(jnp.float32).reshape(N_KV_HEADS, GQA_GROUP)[None, None, :, :, None, None]
    m = jnp.maximum(jnp.max(scores, axis=-1, keepdims=True), sink)
    p = jnp.exp(scores - m)
    denom = jnp.sum(p, axis=-1, keepdims=True) + jnp.exp(sink - m)
    probs = (p / denom).astype(v.dtype)
    out = jnp.einsum('bnhgqk,bnkhd->bnqhgd', probs, vv)
    return out.reshape(b, s, Q_WIDTH)


def setup_inputs(seed: int = 0) -> dict:
    key = jax.random.key(seed)
    ks = jax.random.split(key, 20)
    f32 = jnp.float32

    def nrm(k, shape, scale):
        return jax.random.normal(k, shape, f32) * scale

    x = jax.random.normal(ks[0], (BATCH, SEQ, D_MODEL), f32)
    positions = jnp.broadcast_to(jnp.arange(SEQ, dtype=jnp.int32), (BATCH, SEQ))
    return {
        "x": x,
        "positions": positions,
        "norm_mix_w": 1.0 + nrm(ks[1], (DEPTH, D_MODEL), 0.02),
        "w_in": nrm(ks[2], (DEPTH, D_MODEL, IN_WIDTH), D_MODEL ** -0.5),
        "b_in": nrm(ks[3], (DEPTH, IN_WIDTH), 0.02),
        "conv_mix_w": nrm(ks[4], (DEPTH, CONV_K, CONV_WIDTH), CONV_K ** -0.5),
        "w_conv_out": nrm(ks[5], (DEPTH, CONV_WIDTH, D_MODEL), CONV_WIDTH ** -0.5),
        "w_attn_out": nrm(ks[6], (DEPTH, Q_WIDTH, D_MODEL), Q_WIDTH ** -0.5),
        "b_attn_out": nrm(ks[7], (DEPTH, D_MODEL), 0.02),
        "sinks": nrm(ks[8], (DEPTH, N_Q_HEADS), 1.0),
        "w_mix_out": nrm(ks[9], (DEPTH, D_MODEL, D_MODEL), D_MODEL ** -0.5),
        "norm_ffn_w": 1.0 + nrm(ks[10], (DEPTH, D_MODEL), 0.02),
        "w_ffn_up": nrm(ks[11], (DEPTH, D_MODEL, 2 * D_FF), D_MODEL ** -0.5),
        "ffn_conv_w": nrm(ks[12], (DEPTH, FFN_CONV_K, D_FF), FFN_CONV_K ** -0.5),
        "ffn_conv_b": nrm(ks[13], (DEPTH, D_FF), 0.02),
        "w_ffn_down": nrm(ks[14], (DEPTH, D_FF, D_MODEL), D_FF ** -0.5),
        "norm_final_w": 1.0 + nrm(ks[15], (D_MODEL,), 0.02),
    }


def reference(x, positions, norm_mix_w, w_in, b_in, conv_mix_w, w_conv_out, w_attn_out,
              b_attn_out, sinks, w_mix_out, norm_ffn_w, w_ffn_up, ffn_conv_w, ffn_conv_b,
              w_ffn_down, norm_final_w):
    b, s, _ = x.shape
    cos, sin = rope_tables(positions, x.dtype)
    for l in range(DEPTH):
        u = rmsnorm(x, norm_mix_w[l])
        proj = u @ w_in[l] + b_in[l]
        cb, cc, cx, q, k, v, g_conv, g_attn = jnp.split(proj, IN_SPLITS, axis=-1)

        y_conv = (cb * causal_dwconv(cc * cx, conv_mix_w[l])) @ w_conv_out[l]

        q = apply_rope(q.reshape(b, s, N_Q_HEADS, HEAD_DIM), cos, sin)
        k = apply_rope(k.reshape(b, s, N_KV_HEADS, HEAD_DIM), cos, sin)
        v = v.reshape(b, s, N_KV_HEADS, HEAD_DIM)
        y_attn = sliding_window_attention(q, k, v, sinks[l]) @ w_attn_out[l] + b_attn_out[l]

        merged = jax.nn.sigmoid(g_conv) * y_conv + jax.nn.sigmoid(g_attn) * y_attn
        x = x + merged @ w_mix_out[l]

        u = rmsnorm(x, norm_ffn_w[l])
        up, gate = jnp.split(u @ w_ffn_up[l], 2, axis=-1)
        a = causal_dwconv(up, ffn_conv_w[l], ffn_conv_b[l])
        x = x + (jax.nn.silu(a) * gate) @ w_ffn_down[l]
    return rmsnorm(x, norm_final_w)
```

```python
from contextlib import ExitStack
import math
import os
import numpy as np
import concourse.bass as bass
import concourse.mybir as mybir
from concourse.bass_utils import run_bass_kernel_spmd

F32 = mybir.dt.float32
BF16 = mybir.dt.bfloat16
I32 = mybir.dt.int32
AF = mybir.ActivationFunctionType
ALU = mybir.AluOpType

D = 1024
SEQ = 8192
BATCH = 4
DFF = 2816
NQ = 8
NKV = 2
HD = 64
EPS = 1e-5
NCORE = 8
TOK = 4096
HALO = 256
NTOK = TOK + HALO
INW = 5888
NFC = DFF // 128
TILES = [(0, 256)] + [(256 + 512 * i, 512) for i in range(8)]
NSLOT = 4
PFX = 8
NBANK = int(os.environ.get('KNBANK', 8))
SLOT_ELEMS = 4096
TWO_PI = 2.0 * math.pi
PI_SAFE = 3.1415925

CONV_ORDER = [("cc", 0), ("cx", 0)]
for _c in range(1, 8):
    CONV_ORDER += [("cc", _c), ("cx", _c), ("cb", _c - 1)]
CONV_ORDER += [("cb", 7)]
CONV_J = {k: 6 + i for i, k in enumerate(CONV_ORDER)}

C_NMW = 0
C_NFW = 8
C_NLW = 16
C_BIN = 24
C_BAO = 70
C_CVW = 78
C_FCW = 102
C_FCB = 168
C_INVF = 190
C_FLAG = 191
C_SINK = 192
C_BV = 200
NCOLS = 328

B_ONES = 0
B_ROT = 128
B_MPREV = 256
B_MCUR = 768
B_MFIRST = 1280
B_SEL = 1792
NBCOLS = 1920


ENGS = ("pe", "act", "dve", "pool", "sp")


class Op:
    __slots__ = ("eng", "fn", "deps", "signal", "sig_idx", "dma_sem", "dma_cnt", "pos")

    def __init__(self, eng, fn):
        self.eng = eng
        self.fn = fn
        self.deps = []
        self.signal = False
        self.sig_idx = 0
        self.dma_sem = None
        self.dma_cnt = 0


class Prog:
    def __init__(self, nc):
        self.nc = nc
        self.es = ExitStack()
        self.ops = {e: [] for e in ENGS}
        self.state = {}
        self.dma_counts = {}
        self.sems = {}

    def sbuf(self, name, shape, dtype):
        return self.es.enter_context(self.nc.sbuf_tensor("sb_" + name, list(shape), dtype))

    def psum(self, name, shape, dtype):
        return self.es.enter_context(self.nc.psum_tensor(name, list(shape), dtype))

    def sem(self, name):
        s = self.es.enter_context(self.nc.semaphore("sm_" + name))
        self.sems[name] = s
        self.dma_counts[name] = 0
        return name

    def op(self, eng, fn, r=(), w=(), dma=None):
        self.nrec = getattr(self, "nrec", 0) + 1
        if self.nrec > getattr(self, "limit", 10 ** 9) and fn[0] is not None:
            return None
        o = Op(eng, fn)
        deps = {}
        for k in r:
            st = self.state.get(k)
            if st is not None and st[0] is not None:
                deps[id(st[0])] = st[0]
        for k in w:
            st = self.state.get(k)
            if st is None:
                continue
            if st[0] is not None and (st[0].eng != eng or st[0].dma_sem is not None or eng != "pe"):
                deps[id(st[0])] = st[0]
            for rd in st[1]:
                if rd.eng != eng or rd.dma_sem is not None or eng != "pe":
                    deps[id(rd)] = rd
        for k in r:
            st = self.state.setdefault(k, [None, []])
            st[1].append(o)
        for k in w:
            self.state[k] = [o, []]
        best = {}
        for d in deps.values():
            if d.dma_sem is not None:
                key = ("d", d.dma_sem)
                if key not in best or d.dma_cnt > best[key].dma_cnt:
                    best[key] = d
            else:
                key = ("e", d.eng)
                if key not in best or d.pos > best[key].pos:
                    best[key] = d
        for d in best.values():
            if d.dma_sem is None:
                d.signal = True
        o.deps = list(best.values())
        o.pos = len(self.ops[eng])
        if dma is not None:
            o.dma_sem = dma
            self.dma_counts[dma] += 16
            o.dma_cnt = self.dma_counts[dma]
        self.ops[eng].append(o)
        return o

    def emit(self):
        nc = self.nc
        engsem = {e: self.es.enter_context(nc.semaphore("es_" + e)) for e in ENGS}
        for e in ENGS:
            c = 0
            for o in self.ops[e]:
                if o.signal and o.dma_sem is None:
                    c += 1
                    o.sig_idx = c
        block = self.es.enter_context(nc.Block())
        stats = {}

        def run(e, handle):
            waited = {}
            nw = 0
            for o in self.ops[e]:
                for d in o.deps:
                    if d.dma_sem is not None:
                        key, sem, val = "d_" + d.dma_sem, self.sems[d.dma_sem], d.dma_cnt
                    else:
                        key, sem, val = "e_" + d.eng, engsem[d.eng], d.sig_idx
                    if waited.get(key, 0) >= val:
                        continue
                    waited[key] = val
                    handle.wait_ge(sem, val)
                    nw += 1
                meth, kw = o.fn
                ins = getattr(handle, meth)(**kw) if meth is not None else None
                if o.dma_sem is not None:
                    ins.then_inc(self.sems[o.dma_sem], 16)
                elif o.signal:
                    ins.then_inc(engsem[e], 1)
            stats[e] = (len(self.ops[e]), nw)

        @block.sync
        def _(h):
            run("sp", h)

        @block.tensor
        def _(h):
            run("pe", h)

        @block.scalar
        def _(h):
            run("act", h)

        @block.vector
        def _(h):
            run("dve", h)

        @block.gpsimd
        def _(h):
            run("pool", h)

        self.stats = stats

    def close(self):
        self.es.close()


def build_program(tiles=TILES):
    nc = bass.Bass("TRN2", target_bir_lowering=False)
    XT = nc.dram_tensor("xT", [D, NTOK], F32, kind="ExternalInput").ap()
    POS = nc.dram_tensor("posb", [128, NTOK], I32, kind="ExternalInput").ap()
    COLS = nc.dram_tensor("cols", [128, NCOLS], F32, kind="ExternalInput").ap()
    CBF = nc.dram_tensor("cbf", [128, NBCOLS], F32, kind="ExternalInput").ap()
    WIN = nc.dram_tensor("win", [128, 8, INW], F32, kind="ExternalInput").ap()
    WCO = nc.dram_tensor("wco", [128, 8, D], F32, kind="ExternalInput").ap()
    WAO = nc.dram_tensor("wao", [128, 4, D], F32, kind="ExternalInput").ap()
    WMO = nc.dram_tensor("wmo", [128, 8, D], F32, kind="ExternalInput").ap()
    WUP = nc.dram_tensor("wup", [128, 8, 2 * DFF], F32, kind="ExternalInput").ap()
    WDN = nc.dram_tensor("wdn", [128, 8, NFC, 128], F32, kind="ExternalInput").ap()
    OUT = nc.dram_tensor("outT", [D, TOK], F32, kind="ExternalOutput").ap()
    NUNIT = 36
    SCR = nc.dram_tensor("wscr", [NUNIT, 128, SLOT_ELEMS], BF16, kind="Internal").ap()
    XT3 = XT.rearrange("(c p) t -> p c t", p=128)
    OUT3 = OUT.rearrange("(c p) t -> p c t", p=128)

    P = Prog(nc)
    P.limit = int(os.environ.get("KLIMIT", 10 ** 9))
    P.nrec = 0
    bank_touch = [0] * 8

    def E(eng, meth, r=(), w=(), dma=None, **kw):
        for k in list(r) + list(w):
            if isinstance(k, tuple) and k[0] == "ps":
                bank_touch[k[1]] = P.nrec + 1
        return P.op(eng, (meth, kw), r=r, w=w, dma=dma)

    TM = 512
    xs = [P.sbuf(f"xs{i}", [128, 8, TM], F32) for i in range(2)]
    cols = P.sbuf("cols", [128, NCOLS], F32)
    cbf = P.sbuf("cbf", [128, NBCOLS], BF16)
    ring = [P.sbuf(f"ring{i}", [128, SLOT_ELEMS], BF16) for i in range(NSLOT)]
    sq = [P.sbuf(f"sq{i}", [128, TM], BF16) for i in range(2)]
    rstd = [P.sbuf(f"rstd{i}", [128, TM], F32) for i in range(3)]
    un = P.sbuf("un", [128, 8, TM], BF16)
    NTMP = 8
    tmp = [P.sbuf(f"tmp{i}", [128, TM], F32) for i in range(NTMP)]
    tmpi = P.sbuf("tmpi", [128, TM], I32)
    cosb = P.sbuf("cosb", [128, TM], F32)
    sinb = P.sbuf("sinb", [128, TM], F32)
    posi = P.sbuf("posi", [128, TM], I32)
    qb = [P.sbuf(f"qb{i}", [128, TM], BF16) for i in range(2)]
    qr = P.sbuf("qr", [128, 4, TM], BF16)
    kr = P.sbuf("kr", [128, TM], BF16)
    kprev = P.sbuf("kprev", [128, 128], BF16)
    vcur = P.sbuf("vcur", [128, 4, 2, 128], BF16)
    vprev = P.sbuf("vprev", [128, 2, 128], BF16)
    cxb = [P.sbuf(f"cxb{i}", [128, TM + PFX], F32) for i in range(2)]
    cxcarry = P.sbuf("cxcarry", [128, 8, 2], F32)
    bc = P.sbuf("bc", [128, 8, TM], BF16)
    sgc = P.sbuf("sgc", [128, 8, TM], BF16)
    sga = P.sbuf("sga", [128, 8, TM], BF16)
    NPT = 12
    pt = [P.sbuf(f"pt{i}", [128, TM], BF16) for i in range(NPT)]
    mzero = P.sbuf("mzero", [128, TM], BF16)
    sinkrow = P.sbuf("sinkrow", [1, 2, TM], BF16)
    sinktmp = P.sbuf("sinktmp", [128, 8], F32)
    zf = P.sbuf("zf", [128, 128], F32)
    ao = P.sbuf("ao", [128, 4, TM], BF16)
    mg = P.sbuf("mg", [128, 8, TM], BF16)
    upb = [P.sbuf(f"upb{i}", [128, TM + PFX], F32) for i in range(3)]
    upcarry = P.sbuf("upcarry", [128, NFC, 2], F32)
    silub = [P.sbuf(f"silub{i}", [128, TM], F32) for i in range(2)]
    cva = silub
    hb = P.sbuf("hb", [128, NFC, TM], BF16)
    banks = [P.psum(f"ps{i}", [128, TM], F32) for i in range(8)]

    s_ring = [P.sem(f"ring{i}") for i in range(NSLOT)]
    s_wb = [P.sem(f"wb{i}") for i in range(NSLOT)]
    s_ld = [P.sem(f"ld{i}") for i in range(NSLOT)]
    s_x = [P.sem(f"x{i}") for i in range(2)]
    s_o = [P.sem(f"o{i}") for i in range(2)]
    s_pos = P.sem("pos")
    s_pos0 = P.sem("pos0")
    s_x0 = P.sem("x0first")
    s_cols = P.sem("cols")
    s_cbf = P.sem("cbf")

    ctr = {"bank": 0, "tmp": 0, "sq": 0, "qb": 0, "pt": 0, "cxb": 0, "upb": 0, "silu": 0, "cva": 0}

    def rot(name, n):
        i = ctr[name]
        ctr[name] = (i + 1) % n
        return i

    def bank():
        b = min(range(NBANK), key=lambda i: bank_touch[i])
        bank_touch[b] = P.nrec + 1
        return banks[b], ("ps", b)

    def tmpf():
        i = rot("tmp", NTMP)
        return tmp[i], ("tmp", i)

    units = []
    NHALO_UNITS = 0
    for _ti, (_tok0, _T) in enumerate(tiles):
        _halo = _tok0 < HALO
        for u in range(12):
            n = min(512, INW - u * 512)
            units.append((WIN[:, :, u * 512:u * 512 + n], (8, n)))
        units.append((WCO[:, :, 0:512], (8, 512)))
        units.append((WAO[:, :, :], (4, 1024)))
        units.append((WCO[:, :, 512:1024], (8, 512)))
        units.append((WMO[:, :, 0:512], (8, 512)))
        units.append((WMO[:, :, 512:1024], (8, 512)))
        for u in range(11):
            if _halo:
                units.append((WUP[:, :, u * 512:u * 512 + 256], (8, 256)))
            else:
                units.append((WUP[:, :, u * 512:(u + 1) * 512], (8, 512)))
        if not _halo:
            for c in range(8):
                units.append((WDN[:, c, :, :], (NFC, 128)))
        if _halo:
            NHALO_UNITS = len(units)
    ws = {"issued": 0, "acq": 0}

    def ws_issue():
        n = ws["issued"]
        if n >= len(units):
            return
        src, (a, b) = units[n]
        slot = n % NSLOT
        dst = ring[slot][:, 0:a * b].rearrange("p (a b) -> p a b", a=a)
        if n < NHALO_UNITS:
            t, u, g = 0, n, 0
        else:
            t, u = 1 + (n - NHALO_UNITS) // NUNIT, (n - NHALO_UNITS) % NUNIT
            g = u % 3
        if t >= g + 2:
            E("sp", "dma_start", r=[("scr", u)], w=[("ring", slot)], dma=s_ld[slot],
              out=ring[slot][:, 0:a * b], in_=SCR[u, :, 0:a * b])
        else:
            E("pool", "dma_start", w=[("ring", slot)], dma=s_ring[slot], out=dst, in_=src)
            if t == g + 1:
                E("sp", "dma_start", r=[("ring", slot)], w=[("scr", u)], dma=s_wb[slot],
                  out=SCR[u, :, 0:a * b], in_=ring[slot][:, 0:a * b])
        ws["issued"] = n + 1

    def ws_acquire():
        n = ws["acq"]
        assert n < ws["issued"], "weight ring underflow"
        ws["acq"] = n + 1
        _, (a, b) = units[n]
        slot = n % NSLOT
        return ring[slot][:, 0:a * b].rearrange("p (a b) -> p a b", a=a), ("ring", slot)

    def ws_release():
        ws_issue()

    E("sp", "dma_start", w=["cols"], dma=s_cols, out=cols[:], in_=COLS)
    E("pool", "dma_start", w=["cbf"], dma=s_cbf, out=cbf[:], in_=CBF)
    ones_m = cbf[:, B_ONES:B_ONES + 128]
    rot_m = cbf[:, B_ROT:B_ROT + 128]
    mprev = cbf[:, B_MPREV:B_MPREV + 512]
    mcur = cbf[:, B_MCUR:B_MCUR + 512]
    mfirst = cbf[:, B_MFIRST:B_MFIRST + 512]

    def col(i):
        return cols[:, i:i + 1]

    E("dve", "memset", w=["mzero"], ap=mzero[:], constant=0.0)
    E("dve", "memset", w=["kprev"], ap=kprev[:], constant=0.0)
    E("dve", "memset", w=["vprev"], ap=vprev[:], constant=0.0)
    E("dve", "memset", w=["vcur"], ap=vcur[:], constant=1.0)
    E("dve", "memset", w=["cxcarry"], ap=cxcarry[:], constant=0.0)
    E("dve", "memset", w=["upcarry"], ap=upcarry[:], constant=0.0)
    E("dve", "memset", w=["zf"], ap=zf[:], constant=0.0)
    E("act", "activation", r=["cols"], w=["sinktmp"], out=sinktmp[:], in_=cols[:, C_SINK:C_SINK + 8], func=AF.Exp)
    for hh in range(2):
        for j in range(4):
            E("dve", "tensor_scalar", r=["zf", "sinktmp"], w=["sinkrow"],
              out=sinkrow[0:1, hh, j * 128:(j + 1) * 128], in0=zf[0:1, :],
              scalar1=sinktmp[0:1, hh * 4 + j:hh * 4 + j + 1], scalar2=None, op0=ALU.add)

    def norm_stats(xb, xkey, T, ridx):
        pb, pk = bank()
        for c in range(8):
            si = rot("sq", 2)
            E("act", "activation", r=[(xkey, c)], w=[("sq", si)],
              out=sq[si][:, :T], in_=xb[:, c, :T], func=AF.Square, scale=1.0 / 32.0)
            E("pe", "matmul", r=[("sq", si), "cbf"], w=[pk],
              out=pb[:, :T], lhsT=ones_m, rhs=sq[si][:, :T], start=(c == 0), stop=(c == 7))
        tb, tk = tmpf()
        E("act", "activation", r=[pk], w=[tk], out=tb[:, :T], in_=pb[:, :T], func=AF.Ln, bias=EPS, scale=1.0)
        E("act", "activation", r=[tk], w=[("rstd", ridx)], out=rstd[ridx][:, :T], in_=tb[:, :T], func=AF.Exp, scale=-0.5)

    def norm_apply(xb, xkey, T, ridx, wc0, dst, dkey):
        for c in range(8):
            E("dve", "scalar_tensor_tensor", r=[(xkey, c), ("rstd", ridx), "cols"], w=[(dkey, c)],
              out=dst[:, c, :T], in0=xb[:, c, :T], scalar=col(wc0 + c), in1=rstd[ridx][:, :T],
              op0=ALU.mult, op1=ALU.mult)

    def proj(wap, wkey, nk, coff, rhs_fn, rkeys, T):
        pb, pk = bank()
        for k in range(nk):
            E("pe", "matmul", r=[wkey, rkeys(k)], w=[pk],
              out=pb[:, :T], lhsT=wap[:, k, coff:coff + 128], rhs=rhs_fn(k), start=(k == 0), stop=(k == nk - 1))
        return pb, pk

    def load_x(ti):
        tok0, T = tiles[ti]
        xb = xs[ti % 2]
        xkey = "x%d" % (ti % 2)
        E("sp" if ti == 0 else "pool", "dma_start", w=[(xkey, c) for c in range(8)], dma=s_x0 if ti == 0 else s_x[ti % 2],
          out=xb[:, :, :T], in_=XT3[:, :, tok0:tok0 + T])

    def prep(ti):
        tok0, T = tiles[ti]
        xb = xs[ti % 2]
        xkey = "x%d" % (ti % 2)
        if ti > 0:
            E("pool", "dma_start", w=["posi"], dma=s_pos, out=posi[:, :T], in_=POS[:, tok0:tok0 + T])

        norm_stats(xb, xkey, T, 0)
        norm_apply(xb, xkey, T, 0, C_NMW, un, "un")

        a_b, a_k = tmpf()
        E("dve", "tensor_copy", r=["posi"], w=[a_k], out=a_b[:, :T], in_=posi[:, :T])
        ang_b, ang_k = tmpf()
        E("dve", "tensor_scalar", r=[a_k, "cols"], w=[ang_k],
          out=ang_b[:, :T], in0=a_b[:, :T], scalar1=col(C_INVF), scalar2=None, op0=ALU.mult)
        for which in range(2):
            shift = 0.0 if which == 0 else 0.5 * math.pi
            dstb = sinb if which == 0 else cosb
            dk = "sinb" if which == 0 else "cosb"
            E("dve", "tensor_scalar", r=[ang_k], w=["tmpi"],
              out=tmpi[:, :T], in0=ang_b[:, :T], scalar1=shift, scalar2=1.0 / TWO_PI, op0=ALU.add, op1=ALU.mult)
            kf_b, kf_k = tmpf()
            E("dve", "tensor_copy", r=["tmpi"], w=[kf_k], out=kf_b[:, :T], in_=tmpi[:, :T])
            r_b, r_k = tmpf()
            E("dve", "scalar_tensor_tensor", r=[kf_k, ang_k], w=[r_k],
              out=r_b[:, :T], in0=kf_b[:, :T], scalar=-TWO_PI, in1=ang_b[:, :T], op0=ALU.mult, op1=ALU.add)
            E("dve", "tensor_scalar", r=[r_k], w=[r_k],
              out=r_b[:, :T], in0=r_b[:, :T], scalar1=PI_SAFE - shift, scalar2=-PI_SAFE - shift,
              op0=ALU.min, op1=ALU.max)
            E("act", "activation", r=[r_k], w=[dk], out=dstb[:, :T], in_=r_b[:, :T], func=AF.Sin, bias=shift, scale=1.0)


    def finish_sq(ti):
        tok0, T = tiles[ti]
        xb = xs[ti % 2]
        xkey = "x%d" % (ti % 2)
        for c in range(8):
            E("act", "activation", r=[(xkey, c)], w=[("hb", c)],
              out=hb[:, c, :T], in_=xb[:, c, :T], func=AF.Square, scale=1.0 / 32.0)

    def finish_tile(ti):
        tok0, T = tiles[ti]
        xb = xs[ti % 2]
        xkey = "x%d" % (ti % 2)
        xkeys = [(xkey, c) for c in range(8)]
        pb, pk = bank()
        for c in range(8):
            E("pe", "matmul", r=[("hb", c), "cbf"], w=[pk],
              out=pb[:, :T], lhsT=ones_m, rhs=hb[:, c, :T], start=(c == 0), stop=(c == 7))
        tb, tk = tmpf()
        E("act", "activation", r=[pk], w=[tk], out=tb[:, :T], in_=pb[:, :T], func=AF.Ln, bias=EPS, scale=1.0)
        E("act", "activation", r=[tk], w=[("rstd", 2)], out=rstd[2][:, :T], in_=tb[:, :T], func=AF.Exp, scale=-0.5)
        norm_apply(xb, xkey, T, 2, C_NLW, xb, xkey)
        o0 = tok0 - HALO
        E("pool", "dma_start", r=xkeys, dma=s_o[ti % 2], out=OUT3[:, :, o0:o0 + T], in_=xb[:, :, :T])

    load_x(0)
    E("sp", "dma_start", w=["posi"], dma=s_pos0, out=posi[:, :tiles[0][1]], in_=POS[:, tiles[0][0]:tiles[0][0] + tiles[0][1]])
    for _ in range(NSLOT):
        ws_issue()
    prep(0)

    for ti, (tok0, T) in enumerate(tiles):
        NB = T // 128
        gb0 = tok0 // 128
        is_halo = tok0 < HALO
        last_halo = is_halo and (tok0 + T == HALO)
        xb = xs[ti % 2]
        xkey = "x%d" % (ti % 2)
        xkeys = [(xkey, c) for c in range(8)]

        un_r = lambda k, T=T: un[:, k, :T]
        un_k = lambda k: ("un", k)
        cur = {"w": None, "k": None, "n": -1}

        def win_chunk(j):
            u = j // 4
            if cur["w"] is None or cur["n"] != u:
                if cur["w"] is not None:
                    ws_release()
                cur["w"], cur["k"] = ws_acquire()
                cur["n"] = u
            return cur["w"], cur["k"], (j % 4) * 128

        def rope_start(pb, pk, j, dst, dkey):
            qi = rot("qb", 2)
            qf, qfk = tmpf()
            E("act", "activation", r=[pk, "cols"], w=[qfk],
              out=qf[:, :T], in_=pb[:, :T], func=AF.Identity, bias=col(C_BIN + j), scale=1.0)
            E("act", "activation", r=[qfk], w=[("qb", qi)], out=qb[qi][:, :T], in_=qf[:, :T], func=AF.Copy)
            t1, k1 = tmpf()
            E("dve", "tensor_tensor", r=[qfk, "cosb"], w=[k1], out=t1[:, :T], in0=qf[:, :T], in1=cosb[:, :T], op=ALU.mult)
            return (qi, t1, k1, dst, dkey)

        def rope_finish(st):
            qi, t1, k1, dst, dkey = st
            rb, rk = bank()
            E("pe", "matmul", r=[("qb", qi), "cbf"], w=[rk],
              out=rb[:, :T], lhsT=rot_m, rhs=qb[qi][:, :T], start=True, stop=True)
            t2, k2 = tmpf()
            E("dve", "tensor_tensor", r=[rk, "sinb"], w=[k2], out=t2[:, :T], in0=rb[:, :T], in1=sinb[:, :T], op=ALU.mult)
            E("pool", "tensor_tensor", r=[k1, k2], w=[dkey], out=dst, in0=t1[:, :T], in1=t2[:, :T], op=ALU.add)

        if ti > 0 and not tiles[ti - 1][0] < HALO:
            finish_sq(ti - 1)
        pend = None
        for j in range(0, 5):
            wap, wkey, coff = win_chunk(j)
            pb, pk = proj(wap, wkey, 8, coff, un_r, un_k, T)
            if pend is not None:
                rope_finish(pend)
            if j == 0:
                pend = rope_start(pb, pk, 0, kr[:, :T], "kr")
            else:
                pend = rope_start(pb, pk, j, qr[:, j - 1, :T], ("qr", j - 1))
        wap, wkey, coff = win_chunk(5)
        vb, vk = bank()
        for i in range(NB):
            for k in range(8):
                E("pe", "matmul", r=[wkey, ("un", k)], w=[vk],
                  out=vb[:, i * 128:(i + 1) * 128], lhsT=un[:, k, i * 128:(i + 1) * 128],
                  rhs=wap[:, k, coff:coff + 128], start=(k == 0), stop=(k == 7))
        rope_finish(pend)
        for i in range(NB):
            E("dve", "tensor_tensor", r=[vk, "cols"], w=[("vcur", i)],
              out=vcur[:, i, :, 0:64], in0=vb[:, i * 128:(i + 1) * 128].rearrange("p (a b) -> p a b", a=2),
              in1=cols[:, C_BV:C_BV + 128].rearrange("p (a b) -> p a b", a=2), op=ALU.add)
        if ti > 0 and not tiles[ti - 1][0] < HALO:
            finish_tile(ti - 1)

        pts = {}

        def attn_A(i):
            gb = gb0 + i
            for hh in range(2):
                for kb in range(2):
                    if kb == 0:
                        if i == 0:
                            ksrc, kkey = kprev[hh * 64:(hh + 1) * 64, :], "kprev"
                        else:
                            ksrc, kkey = kr[hh * 64:(hh + 1) * 64, (i - 1) * 128:i * 128], "kr"
                        if gb == 0:
                            msk, mkey = mzero[:, :], "mzero"
                        elif gb == HALO // 128:
                            msk, mkey = mfirst, "cbf"
                        else:
                            msk, mkey = mprev, "cbf"
                    else:
                        ksrc, kkey = kr[hh * 64:(hh + 1) * 64, i * 128:(i + 1) * 128], "kr"
                        msk, mkey = mcur, "cbf"
                    sb, sk = bank()
                    E("pe", "matmul", r=[kkey] + [("qr", j) for j in range(4)], w=[sk],
                      out=sb[:, :], lhsT=ksrc, rhs=qr[hh * 64:(hh + 1) * 64, :, i * 128:(i + 1) * 128],
                      start=True, stop=True)
                    pi = rot("pt", NPT)
                    E("act", "activation", r=[sk], w=[("pt", pi)],
                      out=pt[pi][:, :], in_=sb[:, :], func=AF.Exp, scale=HD ** -0.5)
                    E("pool", "tensor_tensor", r=[("pt", pi), mkey], w=[("pt", pi)],
                      out=pt[pi][:, :], in0=pt[pi][:, :], in1=msk, op=ALU.mult)
                    pts[(i, hh, kb)] = pi

        def attn_B(i):
            for hh in range(2):
                ob, ok = bank()
                for kb in range(2):
                    if kb == 0:
                        if i == 0:
                            vsrc, vkey = vprev[:, hh, :], "vprev"
                        else:
                            vsrc, vkey = vcur[:, i - 1, hh, :], ("vcur", i - 1)
                    else:
                        vsrc, vkey = vcur[:, i, hh, :], ("vcur", i)
                    pi = pts[(i, hh, kb)]
                    E("pe", "matmul", r=[vkey, ("pt", pi)], w=[ok],
                      out=ob[:, :], lhsT=vsrc, rhs=pt[pi][:, :], start=(kb == 0), stop=False)
                E("pe", "matmul", r=["cbf", "sinkrow"], w=[ok],
                  out=ob[:, :], lhsT=cbf[0:1, B_SEL:B_SEL + 128], rhs=sinkrow[0:1, hh, :], start=False, stop=True)
                d1, dk1 = tmpf()
                E("act", "activation", r=[ok], w=[dk1], out=d1[0:64, :], in_=ob[64:128, :], func=AF.Ln)
                d2, dk2 = tmpf()
                E("act", "activation", r=[dk1], w=[dk2], out=d2[0:64, :], in_=d1[0:64, :], func=AF.Exp, scale=-1.0)
                E("dve", "tensor_tensor", r=[ok, dk2], w=[("ao", hh, i)],
                  out=ao[hh * 64:(hh + 1) * 64, :, i * 128:(i + 1) * 128],
                  in0=ob[0:64, :].rearrange("p (a b) -> p a b", a=4),
                  in1=d2[0:64, :].rearrange("p (a b) -> p a b", a=4), op=ALU.mult)

        work = [[] for _ in range(NB + 2)]
        for i in range(NB):
            work[i].append((attn_A, i))
            work[i + 2].append((attn_B, i))

        def conv_b_part(c, a2, k2):
            jb = CONV_J[("cb", c)]
            wap, wkey, coff = win_chunk(jb)
            pbb, pbk = proj(wap, wkey, 8, coff, un_r, un_k, T)
            E("dve", "scalar_tensor_tensor", r=[pbk, k2, "cols"], w=[("bc", c)],
              out=bc[:, c, :T], in0=pbb[:, :T], scalar=col(C_BIN + jb), in1=a2[:, :T], op0=ALU.add, op1=ALU.mult)

        pend_b = None
        for c in range(8):
            jc, jx = CONV_J[("cc", c)], CONV_J[("cx", c)]
            wap, wkey, coff = win_chunk(jc)
            pc, pck = proj(wap, wkey, 8, coff, un_r, un_k, T)
            cs, csk = tmpf()
            E("act", "activation", r=[pck, "cols"], w=[csk],
              out=cs[:, :T], in_=pc[:, :T], func=AF.Identity, bias=col(C_BIN + jc), scale=1.0)
            wap, wkey, coff = win_chunk(jx)
            px, pxk = proj(wap, wkey, 8, coff, un_r, un_k, T)
            xi = rot("cxb", 2)
            cx = cxb[xi]
            cxk = ("cxb", xi)
            E("pool", "tensor_copy", r=[("cxcarry", c)], w=[cxk], out=cx[:, PFX - 2:PFX], in_=cxcarry[:, c, :])
            E("dve", "scalar_tensor_tensor", r=[pxk, csk, "cols", cxk], w=[cxk],
              out=cx[:, PFX:PFX + T], in0=px[:, :T], scalar=col(C_BIN + jx), in1=cs[:, :T], op0=ALU.add, op1=ALU.mult)
            if last_halo:
                E("pool", "tensor_scalar", r=[cxk, "cols"], w=[("cxcarry", c)],
                  out=cxcarry[:, c, :], in0=cx[:, PFX + T - 2:PFX + T], scalar1=col(C_FLAG), scalar2=None, op0=ALU.mult)
            else:
                E("pool", "tensor_copy", r=[cxk], w=[("cxcarry", c)], out=cxcarry[:, c, :], in_=cx[:, PFX + T - 2:PFX + T])
            a0, k0 = tmpf()
            E("dve", "tensor_scalar", r=[cxk, "cols"], w=[k0],
              out=a0[:, :T], in0=cx[:, PFX:PFX + T], scalar1=col(C_CVW + 3 * c + 2), scalar2=None, op0=ALU.mult)
            a1, k1 = tmpf()
            E("dve", "scalar_tensor_tensor", r=[cxk, k0, "cols"], w=[k1],
              out=a1[:, :T], in0=cx[:, PFX - 1:PFX - 1 + T], scalar=col(C_CVW + 3 * c + 1), in1=a0[:, :T],
              op0=ALU.mult, op1=ALU.add)
            ci = rot("cva", 2)
            E("dve", "scalar_tensor_tensor", r=[cxk, k1, "cols"], w=[("silu", ci)],
              out=cva[ci][:, :T], in0=cx[:, PFX - 2:PFX - 2 + T], scalar=col(C_CVW + 3 * c), in1=a1[:, :T],
              op0=ALU.mult, op1=ALU.add)
            if pend_b is not None:
                conv_b_part(*pend_b)
            pend_b = (c, cva[ci], ("silu", ci))
            if work:
                for fn, arg in work.pop(0):
                    fn(arg)
        conv_b_part(*pend_b)
        while work:
            for fn, arg in work.pop(0):
                fn(arg)

        for c in range(8):
            for g, dst, dk in ((0, sgc, "sgc"), (1, sga, "sga")):
                j = 30 + 2 * c + g
                wap, wkey, coff = win_chunk(j)
                pg, pgk = proj(wap, wkey, 8, coff, un_r, un_k, T)
                E("act", "activation", r=[pgk, "cols"], w=[(dk, c)],
                  out=dst[:, c, :T], in_=pg[:, :T], func=AF.Sigmoid, bias=col(C_BIN + j), scale=1.0)
        ws_release()
        cur["w"] = None

        E("dve", "tensor_copy", r=["kr"], w=["kprev"], out=kprev[:, :], in_=kr[:, T - 128:T])
        E("dve", "tensor_copy", r=[("vcur", NB - 1)], w=["vprev"], out=vprev[:, :, :], in_=vcur[:, NB - 1, :, :])

        wco0, wco0k = ws_acquire()
        wao_, waok = ws_acquire()
        wco1 = wco1k = None
        aokeys = [("ao", hh, i) for hh in range(2) for i in range(NB)]
        for c in range(8):
            if c == 4:
                ws_release()
                wco1, wco1k = ws_acquire()
            wc, wck = (wco0, wco0k) if c < 4 else (wco1, wco1k)
            yc, yck = proj(wc, wck, 8, (c % 4) * 128, lambda k: bc[:, k, :T], lambda k: ("bc", k), T)
            ya, yak = bank()
            for j in range(4):
                E("pe", "matmul", r=[waok] + aokeys, w=[yak],
                  out=ya[:, :T], lhsT=wao_[:, j, c * 128:(c + 1) * 128], rhs=ao[:, j, :T],
                  start=(j == 0), stop=(j == 3))
            m1, mk1 = tmpf()
            E("dve", "tensor_tensor", r=[yck, ("sgc", c)], w=[mk1],
              out=m1[:, :T], in0=yc[:, :T], in1=sgc[:, c, :T], op=ALU.mult)
            m2, mk2 = tmpf()
            E("dve", "scalar_tensor_tensor", r=[yak, ("sga", c), "cols"], w=[mk2],
              out=m2[:, :T], in0=ya[:, :T], scalar=col(C_BAO + c), in1=sga[:, c, :T], op0=ALU.add, op1=ALU.mult)
            E("pool", "tensor_tensor", r=[mk1, mk2], w=[("mg", c)],
              out=mg[:, c, :T], in0=m1[:, :T], in1=m2[:, :T], op=ALU.add)
        ws_release()
        ws_release()

        wm = wmk = None
        for c in range(8):
            if c % 4 == 0:
                if c == 4:
                    ws_release()
                wm, wmk = ws_acquire()
            pm, pmk = proj(wm, wmk, 8, (c % 4) * 128, lambda k: mg[:, k, :T], lambda k: ("mg", k), T)
            E("dve", "tensor_tensor", r=[pmk, (xkey, c)], w=[(xkey, c)],
              out=xb[:, c, :T], in0=xb[:, c, :T], in1=pm[:, :T], op=ALU.add)
        ws_release()

        if ti + 1 < len(tiles):
            load_x(ti + 1)

        norm_stats(xb, xkey, T, 1)
        norm_apply(xb, xkey, T, 1, C_NFW, un, "un")

        wunits = {}
        pre = {}

        def ffn_up_part(f):
            g = f // 2
            if g not in wunits:
                wunits[g] = ws_acquire()
            wu, wuk = wunits[g]
            lo = (f % 2) * 128
            if ("up", f) in pre:
                pu, puk = pre[("up", f)]
            else:
                pu, puk = proj(wu, wuk, 8, lo, un_r, un_k, T)
            ui = rot("upb", 3)
            ub = upb[ui]
            ubk = ("upb", ui)
            E("pool", "tensor_copy", r=[("upcarry", f)], w=[ubk], out=ub[:, PFX - 2:PFX], in_=upcarry[:, f, :])
            E("act", "activation", r=[puk, ubk], w=[ubk], out=ub[:, PFX:PFX + T], in_=pu[:, :T], func=AF.Copy)
            a0, k0 = tmpf()
            E("act", "activation", r=[puk, "cols"], w=[k0], out=a0[:, :T], in_=pu[:, :T], func=AF.Identity,
              scale=col(C_FCW + 3 * f + 2), bias=col(C_FCB + f))
            if last_halo:
                E("pool", "tensor_scalar", r=[ubk, "cols"], w=[("upcarry", f)],
                  out=upcarry[:, f, :], in0=ub[:, PFX + T - 2:PFX + T], scalar1=col(C_FLAG), scalar2=None, op0=ALU.mult)
            else:
                E("pool", "tensor_copy", r=[ubk], w=[("upcarry", f)], out=upcarry[:, f, :], in_=ub[:, PFX + T - 2:PFX + T])
            a1, k1 = tmpf()
            E("dve", "scalar_tensor_tensor", r=[ubk, k0, "cols"], w=[k1],
              out=a1[:, :T], in0=ub[:, PFX - 1:PFX - 1 + T], scalar=col(C_FCW + 3 * f + 1), in1=a0[:, :T],
              op0=ALU.mult, op1=ALU.add)
            a2, k2 = tmpf()
            E("dve", "scalar_tensor_tensor", r=[ubk, k1, "cols"], w=[k2],
              out=a2[:, :T], in0=ub[:, PFX - 2:PFX - 2 + T], scalar=col(C_FCW + 3 * f), in1=a1[:, :T],
              op0=ALU.mult, op1=ALU.add)
            return (a2, k2)

        def ffn_gate_part(f, a2k):
            a2, k2 = a2k
            si = rot("silu", 2)
            E("act", "activation", r=[k2], w=[("silu", si)], out=silub[si][:, :T], in_=a2[:, :T], func=AF.Silu)
            wu, wuk = wunits[f // 2]
            lo = (f % 2) * 128
            if ("gate", f) in pre:
                pg, pgk = pre[("gate", f)]
            else:
                pg, pgk = proj(wu, wuk, 8, 256 + lo, un_r, un_k, T)
            E("dve", "tensor_tensor", r=[pgk, ("silu", si)], w=[("hb", f)],
              out=hb[:, f, :T], in0=pg[:, :T], in1=silub[si][:, :T], op=ALU.mult)
            if f % 2 == 1:
                ws_release()

        if is_halo:
            for f in range(NFC):
                g = f // 2
                if f % 2 == 0:
                    wu, wuk = ws_acquire()
                pu, puk = proj(wu, wuk, 8, (f % 2) * 128, un_r, un_k, T)
                ui = rot("upb", 3)
                ub = upb[ui]
                ubk = ("upb", ui)
                E("act", "activation", r=[puk], w=[ubk], out=ub[:, PFX:PFX + T], in_=pu[:, :T], func=AF.Copy)
                if last_halo:
                    E("pool", "tensor_scalar", r=[ubk, "cols"], w=[("upcarry", f)],
                      out=upcarry[:, f, :], in0=ub[:, PFX + T - 2:PFX + T], scalar1=col(C_FLAG), scalar2=None,
                      op0=ALU.mult)
                else:
                    E("pool", "tensor_copy", r=[ubk], w=[("upcarry", f)], out=upcarry[:, f, :],
                      in_=ub[:, PFX + T - 2:PFX + T])
                if f % 2 == 1:
                    ws_release()
            if ti + 1 < len(tiles):
                prep(ti + 1)
        else:
            wunits[0] = ws_acquire()
            wu0, wuk0 = wunits[0]
            pbanks = [bank() for _ in range(4)]
            for k in range(8):
                for j in range(4):
                    E("pe", "matmul", r=[wuk0, ("un", k)], w=[pbanks[j][1]],
                      out=pbanks[j][0][:, :T], lhsT=wu0[:, k, j * 128:(j + 1) * 128], rhs=un[:, k, :T],
                      start=(k == 0), stop=(k == 7))
            pre[("up", 0)], pre[("up", 1)], pre[("gate", 0)], pre[("gate", 1)] = pbanks
            prev = None
            for f in range(NFC):
                si = ffn_up_part(f)
                if prev is not None:
                    ffn_gate_part(*prev)
                prev = (f, si)
            ffn_gate_part(*prev)

            for c in range(8):
                wd, wdk = ws_acquire()
                pd, pdk = proj(wd, wdk, NFC, 0, lambda k: hb[:, k, :T], lambda k: ("hb", k), T)
                ws_release()
                if c == 2 and ti + 1 < len(tiles):
                    prep(ti + 1)
                E("dve", "tensor_tensor", r=[pdk, (xkey, c)], w=[(xkey, c)],
                  out=xb[:, c, :T], in0=xb[:, c, :T], in1=pd[:, :T], op=ALU.add)

        if ti + 1 == len(tiles) and not is_halo:
            finish_sq(ti)
            finish_tile(ti)

    E("sp", None, w=[("x0", c) for c in range(8)] + [("x1", c) for c in range(8)]
      + [("ring", i) for i in range(NSLOT)] + ["cbf", "posi", "cols"])
    P.emit()
    P.close()
    return nc, P


def _perm_in():
    cb0, cc0, cx0, q0, k0, v0, gc0, ga0 = 0, 1024, 2048, 3072, 3584, 3712, 3840, 4864
    perm = []
    perm += list(range(k0, k0 + 128))
    for j in range(4):
        perm += list(range(q0 + j * 64, q0 + (j + 1) * 64))
        perm += list(range(q0 + (4 + j) * 64, q0 + (5 + j) * 64))
    perm += list(range(v0, v0 + 128))
    base = {"cc": cc0, "cx": cx0, "cb": cb0}
    for kind, c in CONV_ORDER:
        perm += list(range(base[kind] + c * 128, base[kind] + (c + 1) * 128))
    for c in range(8):
        perm += list(range(gc0 + c * 128, gc0 + (c + 1) * 128))
        perm += list(range(ga0 + c * 128, ga0 + (c + 1) * 128))
    assert len(perm) == INW
    return np.array(perm, dtype=np.int64)


def _pkc(w):
    K, N = w.shape
    return np.ascontiguousarray(w.reshape(K // 128, 128, N).transpose(1, 0, 2))


def _colchunks(v):
    return np.ascontiguousarray(v.reshape(-1, 128).T)


def prepare_inputs(x, positions, norm_mix_w, w_in, b_in, conv_mix_w, w_conv_out, w_attn_out, b_attn_out, sinks,
                   w_mix_out, norm_ffn_w, w_ffn_up, ffn_conv_w, ffn_conv_b, w_ffn_down, norm_final_w):
    f32 = np.float32
    perm = _perm_in()
    win = _pkc(np.asarray(w_in[0], f32)[:, perm])
    wco = _pkc(np.asarray(w_conv_out[0], f32))
    arow = []
    for j in range(4):
        arow += list(range(j * 64, (j + 1) * 64)) + list(range((4 + j) * 64, (5 + j) * 64))
    wao = _pkc(np.asarray(w_attn_out[0], f32)[np.array(arow)])
    wmo = _pkc(np.asarray(w_mix_out[0], f32))
    uperm = []
    for g in range(NFC // 2):
        uperm += list(range(2 * g * 128, (2 * g + 2) * 128))
        uperm += list(range(DFF + 2 * g * 128, DFF + (2 * g + 2) * 128))
    wup = _pkc(np.asarray(w_ffn_up[0], f32)[:, np.array(uperm)])
    wd = np.asarray(w_ffn_down[0], f32).reshape(NFC, 128, 8, 128)
    wdn = np.ascontiguousarray(wd.transpose(1, 2, 0, 3))

    cols = np.zeros((128, NCOLS), f32)
    cols[:, C_NMW:C_NMW + 8] = _colchunks(np.asarray(norm_mix_w[0], f32))
    cols[:, C_NFW:C_NFW + 8] = _colchunks(np.asarray(norm_ffn_w[0], f32))
    cols[:, C_NLW:C_NLW + 8] = _colchunks(np.asarray(norm_final_w, f32))
    cols[:, C_BIN:C_BIN + 46] = _colchunks(np.asarray(b_in[0], f32)[perm])
    cols[:, C_BAO:C_BAO + 8] = _colchunks(np.asarray(b_attn_out[0], f32))
    cw = np.asarray(conv_mix_w[0], f32)
    cols[:, C_CVW:C_CVW + 24] = cw.reshape(3, 8, 128).transpose(2, 1, 0).reshape(128, 24)
    fw = np.asarray(ffn_conv_w[0], f32)
    cols[:, C_FCW:C_FCW + 66] = fw.reshape(3, NFC, 128).transpose(2, 1, 0).reshape(128, 66)
    cols[:, C_FCB:C_FCB + NFC] = _colchunks(np.asarray(ffn_conv_b[0], f32))
    inv_freq = (10000.0 ** (-np.arange(0, HD, 2, dtype=f32) / HD)).astype(f32)
    cols[:, C_INVF] = inv_freq[np.arange(128) % 32]
    cols[:, C_SINK:C_SINK + 8] = np.asarray(sinks[0], f32)[None, :]
    cols[:, C_BV:C_BV + 128] = np.asarray(b_in[0], f32)[3712:3840][None, :]

    cbf = np.zeros((128, NBCOLS), f32)
    cbf[:, B_ONES:B_ONES + 128] = 1.0
    rm = np.zeros((128, 128), f32)
    for m in range(128):
        if m % 64 < 32:
            rm[m + 32, m] = -1.0
        else:
            rm[m - 32, m] = 1.0
    cbf[:, B_ROT:B_ROT + 128] = rm
    kk = np.arange(128)[:, None]
    qq = np.arange(128)[None, :]
    mp = (kk > qq).astype(f32)
    mc = (kk <= qq).astype(f32)
    cbf[:, B_MPREV:B_MPREV + 512] = np.tile(mp, (1, 4))
    cbf[:, B_MCUR:B_MCUR + 512] = np.tile(mc, (1, 4))
    cbf[0, B_SEL + 64:B_SEL + 128] = 1.0

    x = np.asarray(x, f32)
    positions = np.asarray(positions, np.int32)
    in_maps = []
    for core in range(NCORE):
        b, half = core // 2, core % 2
        t0 = half * TOK
        xT = np.zeros((D, NTOK), f32)
        xT[:, HALO:] = x[b, t0:t0 + TOK, :].T
        pos = np.zeros((NTOK,), np.int32)
        pos[HALO:] = positions[b, t0:t0 + TOK]
        ccols = cols.copy()
        ccbf = cbf.copy()
        if half == 1:
            xT[:, :HALO] = x[b, t0 - HALO:t0, :].T
            pos[:HALO] = positions[b, t0 - HALO:t0]
            ccols[:, C_FLAG] = 1.0
            ccbf[:, B_MFIRST:B_MFIRST + 512] = np.tile(mp, (1, 4))
        in_maps.append({
            "xT": xT, "posb": np.ascontiguousarray(np.broadcast_to(pos[None, :], (128, NTOK))),
            "cols": ccols, "cbf": ccbf, "win": win, "wco": wco, "wao": wao, "wmo": wmo, "wup": wup, "wdn": wdn,
        })
    return in_maps


def kernel(**inputs):
    in_maps = prepare_inputs(**inputs)
    nc, _ = build_program()
    res = run_bass_kernel_spmd(nc, in_maps, core_ids=list(range(NCORE)))
    out = np.empty((BATCH, SEQ, D), np.float32)
    for core in range(NCORE):
        b, half = core // 2, core % 2
        out[b, half * TOK:(half + 1) * TOK, :] = res.results[core]["outT"].T
    return out
```

```python
from contextlib import ExitStack
import math
import os
import numpy as np
import concourse.bass as bass
import concourse.mybir as mybir
from concourse.bass_utils import run_bass_kernel_spmd

F32 = mybir.dt.float32
BF16 = mybir.dt.bfloat16
I32 = mybir.dt.int32
AF = mybir.ActivationFunctionType
ALU = mybir.AluOpType

D = 1024
SEQ = 8192
BATCH = 4
DFF = 2816
NQ = 8
NKV = 2
HD = 64
EPS = 1e-5
NCORE = 8
TOK = 4096
HALO = 256
NTOK = TOK + HALO
INW = 5888
NFC = DFF // 128
TILES = [(0, 256)] + [(256 + 512 * i, 512) for i in range(8)]
NSLOT = 4
PFX = 8
NBANK = int(os.environ.get('KNBANK', 8))
SLOT_ELEMS = 4096
TWO_PI = 2.0 * math.pi
PI_SAFE = 3.1415925

CONV_ORDER = [("cc", 0), ("cx", 0)]
for _c in range(1, 8):
    CONV_ORDER += [("cc", _c), ("cx", _c), ("cb", _c - 1)]
CONV_ORDER += [("cb", 7)]
CONV_J = {k: 6 + i for i, k in enumerate(CONV_ORDER)}

C_NMW = 0
C_NFW = 8
C_NLW = 16
C_BIN = 24
C_BAO = 70
C_CVW = 78
C_FCW = 102
C_FCB = 168
C_INVF = 190
C_FLAG = 191
C_SINK = 192
C_BV = 200
NCOLS = 328

B_ONES = 0
B_ROT = 128
B_MPREV = 256
B_MCUR = 768
B_MFIRST = 1280
B_SEL = 1792
NBCOLS = 1920


ENGS = ("pe", "act", "dve", "pool", "sp")


class Op:
    __slots__ = ("eng", "fn", "deps", "signal", "sig_idx", "dma_sem", "dma_cnt", "pos")

    def __init__(self, eng, fn):
        self.eng = eng
        self.fn = fn
        self.deps = []
        self.signal = False
        self.sig_idx = 0
        self.dma_sem = None
        self.dma_cnt = 0


class Prog:
    def __init__(self, nc):
        self.nc = nc
        self.es = ExitStack()
        self.ops = {e: [] for e in ENGS}
        self.state = {}
        self.dma_counts = {}
        self.sems = {}

    def sbuf(self, name, shape, dtype):
        return self.es.enter_context(self.nc.sbuf_tensor("sb_" + name, list(shape), dtype))

    def psum(self, name, shape, dtype):
        return self.es.enter_context(self.nc.psum_tensor(name, list(shape), dtype))

    def sem(self, name):
        s = self.es.enter_context(self.nc.semaphore("sm_" + name))
        self.sems[name] = s
        self.dma_counts[name] = 0
        return name

    def op(self, eng, fn, r=(), w=(), dma=None):
        self.nrec = getattr(self, "nrec", 0) + 1
        if self.nrec > getattr(self, "limit", 10 ** 9) and fn[0] is not None:
            return None
        o = Op(eng, fn)
        deps = {}
        for k in r:
            st = self.state.get(k)
            if st is not None and st[0] is not None:
                deps[id(st[0])] = st[0]
        for k in w:
            st = self.state.get(k)
            if st is None:
                continue
            if st[0] is not None and (st[0].eng != eng or st[0].dma_sem is not None or eng != "pe"):
                deps[id(st[0])] = st[0]
            for rd in st[1]:
                if rd.eng != eng or rd.dma_sem is not None or eng != "pe":
                    deps[id(rd)] = rd
        for k in r:
            st = self.state.setdefault(k, [None, []])
            st[1].append(o)
        for k in w:
            self.state[k] = [o, []]
        best = {}
        for d in deps.values():
            if d.dma_sem is not None:
                key = ("d", d.dma_sem)
                if key not in best or d.dma_cnt > best[key].dma_cnt:
                    best[key] = d
            else:
                key = ("e", d.eng)
                if key not in best or d.pos > best[key].pos:
                    best[key] = d
        for d in best.values():
            if d.dma_sem is None:
                d.signal = True
        o.deps = list(best.values())
        o.pos = len(self.ops[eng])
        if dma is not None:
            o.dma_sem = dma
            self.dma_counts[dma] += 16
            o.dma_cnt = self.dma_counts[dma]
        self.ops[eng].append(o)
        return o

    def emit(self):
        nc = self.nc
        engsem = {e: self.es.enter_context(nc.semaphore("es_" + e)) for e in ENGS}
        for e in ENGS:
            c = 0
            for o in self.ops[e]:
                if o.signal and o.dma_sem is None:
                    c += 1
                    o.sig_idx = c
        block = self.es.enter_context(nc.Block())
        stats = {}

        def run(e, handle):
            waited = {}
            nw = 0
            for o in self.ops[e]:
                for d in o.deps:
                    if d.dma_sem is not None:
                        key, sem, val = "d_" + d.dma_sem, self.sems[d.dma_sem], d.dma_cnt
                    else:
                        key, sem, val = "e_" + d.eng, engsem[d.eng], d.sig_idx
                    if waited.get(key, 0) >= val:
                        continue
                    waited[key] = val
                    handle.wait_ge(sem, val)
                    nw += 1
                meth, kw = o.fn
                ins = getattr(handle, meth)(**kw) if meth is not None else None
                if o.dma_sem is not None:
                    ins.then_inc(self.sems[o.dma_sem], 16)
                elif o.signal:
                    ins.then_inc(engsem[e], 1)
            stats[e] = (len(self.ops[e]), nw)

        @block.sync
        def _(h):
            run("sp", h)

        @block.tensor
        def _(h):
            run("pe", h)

        @block.scalar
        def _(h):
            run("act", h)

        @block.vector
        def _(h):
            run("dve", h)

        @block.gpsimd
        def _(h):
            run("pool", h)

        self.stats = stats

    def close(self):
        self.es.close()


def build_program(tiles=TILES):
    nc = bass.Bass("TRN2", target_bir_lowering=False)
    XT = nc.dram_tensor("xT", [D, NTOK], F32, kind="ExternalInput").ap()
    POS = nc.dram_tensor("posb", [128, NTOK], I32, kind="ExternalInput").ap()
    COLS = nc.dram_tensor("cols", [128, NCOLS], F32, kind="ExternalInput").ap()
    CBF = nc.dram_tensor("cbf", [128, NBCOLS], F32, kind="ExternalInput").ap()
    WIN = nc.dram_tensor("win", [128, 8, INW], F32, kind="ExternalInput").ap()
    WCO = nc.dram_tensor("wco", [128, 8, D], F32, kind="ExternalInput").ap()
    WAO = nc.dram_tensor("wao", [128, 4, D], F32, kind="ExternalInput").ap()
    WMO = nc.dram_tensor("wmo", [128, 8, D], F32, kind="ExternalInput").ap()
    WUP = nc.dram_tensor("wup", [128, 8, 2 * DFF], F32, kind="ExternalInput").ap()
    WDN = nc.dram_tensor("wdn", [128, 8, NFC, 128], F32, kind="ExternalInput").ap()
    OUT = nc.dram_tensor("outT", [D, TOK], F32, kind="ExternalOutput").ap()
    NUNIT = 36
    SCR = nc.dram_tensor("wscr", [NUNIT, 128, SLOT_ELEMS], BF16, kind="Internal").ap()
    XT3 = XT.rearrange("(c p) t -> p c t", p=128)
    OUT3 = OUT.rearrange("(c p) t -> p c t", p=128)

    P = Prog(nc)
    P.limit = int(os.environ.get("KLIMIT", 10 ** 9))
    P.nrec = 0
    bank_touch = [0] * 8

    def E(eng, meth, r=(), w=(), dma=None, **kw):
        for k in list(r) + list(w):
            if isinstance(k, tuple) and k[0] == "ps":
                bank_touch[k[1]] = P.nrec + 1
        return P.op(eng, (meth, kw), r=r, w=w, dma=dma)

    TM = 512
    xs = [P.sbuf(f"xs{i}", [128, 8, TM], F32) for i in range(2)]
    cols = P.sbuf("cols", [128, NCOLS], F32)
    cbf = P.sbuf("cbf", [128, NBCOLS], BF16)
    ring = [P.sbuf(f"ring{i}", [128, SLOT_ELEMS], BF16) for i in range(NSLOT)]
    sq = [P.sbuf(f"sq{i}", [128, TM], BF16) for i in range(2)]
    rstd = [P.sbuf(f"rstd{i}", [128, TM], F32) for i in range(3)]
    un = P.sbuf("un", [128, 8, TM], BF16)
    NTMP = 8
    tmp = [P.sbuf(f"tmp{i}", [128, TM], F32) for i in range(NTMP)]
    tmpi = P.sbuf("tmpi", [128, TM], I32)
    cosb = P.sbuf("cosb", [128, TM], F32)
    sinb = P.sbuf("sinb", [128, TM], F32)
    posi = P.sbuf("posi", [128, TM], I32)
    qb = [P.sbuf(f"qb{i}", [128, TM], BF16) for i in range(2)]
    qr = P.sbuf("qr", [128, 4, TM], BF16)
    kr = P.sbuf("kr", [128, TM], BF16)
    kprev = P.sbuf("kprev", [128, 128], BF16)
    vcur = P.sbuf("vcur", [128, 4, 2, 128], BF16)
    vprev = P.sbuf("vprev", [128, 2, 128], BF16)
    cxb = [P.sbuf(f"cxb{i}", [128, TM + PFX], F32) for i in range(2)]
    cxcarry = P.sbuf("cxcarry", [128, 8, 2], F32)
    bc = P.sbuf("bc", [128, 8, TM], BF16)
    sgc = P.sbuf("sgc", [128, 8, TM], BF16)
    sga = P.sbuf("sga", [128, 8, TM], BF16)
    NPT = 12
    pt = [P.sbuf(f"pt{i}", [128, TM], BF16) for i in range(NPT)]
    mzero = P.sbuf("mzero", [128, TM], BF16)
    sinkrow = P.sbuf("sinkrow", [1, 2, TM], BF16)
    sinktmp = P.sbuf("sinktmp", [128, 8], F32)
    zf = P.sbuf("zf", [128, 128], F32)
    ao = P.sbuf("ao", [128, 4, TM], BF16)
    mg = P.sbuf("mg", [128, 8, TM], BF16)
    upb = [P.sbuf(f"upb{i}", [128, TM + PFX], F32) for i in range(3)]
    upcarry = P.sbuf("upcarry", [128, NFC, 2], F32)
    silub = [P.sbuf(f"silub{i}", [128, TM], F32) for i in range(2)]
    cva = silub
    hb = P.sbuf("hb", [128, NFC, TM], BF16)
    banks = [P.psum(f"ps{i}", [128, TM], F32) for i in range(8)]

    s_ring = [P.sem(f"ring{i}") for i in range(NSLOT)]
    s_wb = [P.sem(f"wb{i}") for i in range(NSLOT)]
    s_ld = [P.sem(f"ld{i}") for i in range(NSLOT)]
    s_x = [P.sem(f"x{i}") for i in range(2)]
    s_o = [P.sem(f"o{i}") for i in range(2)]
    s_pos = P.sem("pos")
    s_pos0 = P.sem("pos0")
    s_x0 = P.sem("x0first")
    s_cols = P.sem("cols")
    s_cbf = P.sem("cbf")

    ctr = {"bank": 0, "tmp": 0, "sq": 0, "qb": 0, "pt": 0, "cxb": 0, "upb": 0, "silu": 0, "cva": 0}

    def rot(name, n):
        i = ctr[name]
        ctr[name] = (i + 1) % n
        return i

    def bank():
        b = min(range(NBANK), key=lambda i: bank_touch[i])
        bank_touch[b] = P.nrec + 1
        return banks[b], ("ps", b)

    def tmpf():
        i = rot("tmp", NTMP)
        return tmp[i], ("tmp", i)

    units = []
    NHALO_UNITS = 0
    for _ti, (_tok0, _T) in enumerate(tiles):
        _halo = _tok0 < HALO
        for u in range(12):
            n = min(512, INW - u * 512)
            units.append((WIN[:, :, u * 512:u * 512 + n], (8, n)))
        units.append((WCO[:, :, 0:512], (8, 512)))
        units.append((WAO[:, :, :], (4, 1024)))
        units.append((WCO[:, :, 512:1024], (8, 512)))
        units.append((WMO[:, :, 0:512], (8, 512)))
        units.append((WMO[:, :, 512:1024], (8, 512)))
        for u in range(11):
            if _halo:
                units.append((WUP[:, :, u * 512:u * 512 + 256], (8, 256)))
            else:
                units.append((WUP[:, :, u * 512:(u + 1) * 512], (8, 512)))
        if not _halo:
            for c in range(8):
                units.append((WDN[:, c, :, :], (NFC, 128)))
        if _halo:
            NHALO_UNITS = len(units)
    ws = {"issued": 0, "acq": 0}

    def ws_issue():
        n = ws["issued"]
        if n >= len(units):
            return
        src, (a, b) = units[n]
        slot = n % NSLOT
        dst = ring[slot][:, 0:a * b].rearrange("p (a b) -> p a b", a=a)
        if n < NHALO_UNITS:
            t, u, g = 0, n, 0
        else:
            t, u = 1 + (n - NHALO_UNITS) // NUNIT, (n - NHALO_UNITS) % NUNIT
            g = u % 3
        if t >= g + 2:
            E("sp", "dma_start", r=[("scr", u)], w=[("ring", slot)], dma=s_ld[slot],
              out=ring[slot][:, 0:a * b], in_=SCR[u, :, 0:a * b])
        else:
            E("pool", "dma_start", w=[("ring", slot)], dma=s_ring[slot], out=dst, in_=src)
            if t == g + 1:
                E("sp", "dma_start", r=[("ring", slot)], w=[("scr", u)], dma=s_wb[slot],
                  out=SCR[u, :, 0:a * b], in_=ring[slot][:, 0:a * b])
        ws["issued"] = n + 1

    def ws_acquire():
        n = ws["acq"]
        assert n < ws["issued"], "weight ring underflow"
        ws["acq"] = n + 1
        _, (a, b) = units[n]
        slot = n % NSLOT
        return ring[slot][:, 0:a * b].rearrange("p (a b) -> p a b", a=a), ("ring", slot)

    def ws_release():
        ws_issue()

    E("sp", "dma_start", w=["cols"], dma=s_cols, out=cols[:], in_=COLS)
    E("pool", "dma_start", w=["cbf"], dma=s_cbf, out=cbf[:], in_=CBF)
    ones_m = cbf[:, B_ONES:B_ONES + 128]
    rot_m = cbf[:, B_ROT:B_ROT + 128]
    mprev = cbf[:, B_MPREV:B_MPREV + 512]
    mcur = cbf[:, B_MCUR:B_MCUR + 512]
    mfirst = cbf[:, B_MFIRST:B_MFIRST + 512]

    def col(i):
        return cols[:, i:i + 1]

    E("dve", "memset", w=["mzero"], ap=mzero[:], constant=0.0)
    E("dve", "memset", w=["kprev"], ap=kprev[:], constant=0.0)
    E("dve", "memset", w=["vprev"], ap=vprev[:], constant=0.0)
    E("dve", "memset", w=["vcur"], ap=vcur[:], constant=1.0)
    E("dve", "memset", w=["cxcarry"], ap=cxcarry[:], constant=0.0)
    E("dve", "memset", w=["upcarry"], ap=upcarry[:], constant=0.0)
    E("dve", "memset", w=["zf"], ap=zf[:], constant=0.0)
    E("act", "activation", r=["cols"], w=["sinktmp"], out=sinktmp[:], in_=cols[:, C_SINK:C_SINK + 8], func=AF.Exp)
    for hh in range(2):
        for j in range(4):
            E("dve", "tensor_scalar", r=["zf", "sinktmp"], w=["sinkrow"],
              out=sinkrow[0:1, hh, j * 128:(j + 1) * 128], in0=zf[0:1, :],
              scalar1=sinktmp[0:1, hh * 4 + j:hh * 4 + j + 1], scalar2=None, op0=ALU.add)

    def norm_stats(xb, xkey, T, ridx):
        pb, pk = bank()
        for c in range(8):
            si = rot("sq", 2)
            E("act", "activation", r=[(xkey, c)], w=[("sq", si)],
              out=sq[si][:, :T], in_=xb[:, c, :T], func=AF.Square, scale=1.0 / 32.0)
            E("pe", "matmul", r=[("sq", si), "cbf"], w=[pk],
              out=pb[:, :T], lhsT=ones_m, rhs=sq[si][:, :T], start=(c == 0), stop=(c == 7))
        tb, tk = tmpf()
        E("act", "activation", r=[pk], w=[tk], out=tb[:, :T], in_=pb[:, :T], func=AF.Ln, bias=EPS, scale=1.0)
        E("act", "activation", r=[tk], w=[("rstd", ridx)], out=rstd[ridx][:, :T], in_=tb[:, :T], func=AF.Exp, scale=-0.5)

    def norm_apply(xb, xkey, T, ridx, wc0, dst, dkey):
        for c in range(8):
            E("dve", "scalar_tensor_tensor", r=[(xkey, c), ("rstd", ridx), "cols"], w=[(dkey, c)],
              out=dst[:, c, :T], in0=xb[:, c, :T], scalar=col(wc0 + c), in1=rstd[ridx][:, :T],
              op0=ALU.mult, op1=ALU.mult)

    def proj(wap, wkey, nk, coff, rhs_fn, rkeys, T):
        pb, pk = bank()
        for k in range(nk):
            E("pe", "matmul", r=[wkey, rkeys(k)], w=[pk],
              out=pb[:, :T], lhsT=wap[:, k, coff:coff + 128], rhs=rhs_fn(k), start=(k == 0), stop=(k == nk - 1))
        return pb, pk

    def load_x(ti):
        tok0, T = tiles[ti]
        xb = xs[ti % 2]
        xkey = "x%d" % (ti % 2)
        E("sp" if ti == 0 else "pool", "dma_start", w=[(xkey, c) for c in range(8)], dma=s_x0 if ti == 0 else s_x[ti % 2],
          out=xb[:, :, :T], in_=XT3[:, :, tok0:tok0 + T])

    def prep(ti):
        tok0, T = tiles[ti]
        xb = xs[ti % 2]
        xkey = "x%d" % (ti % 2)
        if ti > 0:
            E("pool", "dma_start", w=["posi"], dma=s_pos, out=posi[:, :T], in_=POS[:, tok0:tok0 + T])

        norm_stats(xb, xkey, T, 0)
        norm_apply(xb, xkey, T, 0, C_NMW, un, "un")

        a_b, a_k = tmpf()
        E("dve", "tensor_copy", r=["posi"], w=[a_k], out=a_b[:, :T], in_=posi[:, :T])
        ang_b, ang_k = tmpf()
        E("dve", "tensor_scalar", r=[a_k, "cols"], w=[ang_k],
          out=ang_b[:, :T], in0=a_b[:, :T], scalar1=col(C_INVF), scalar2=None, op0=ALU.mult)
        for which in range(2):
            shift = 0.0 if which == 0 else 0.5 * math.pi
            dstb = sinb if which == 0 else cosb
            dk = "sinb" if which == 0 else "cosb"
            E("dve", "tensor_scalar", r=[ang_k], w=["tmpi"],
              out=tmpi[:, :T], in0=ang_b[:, :T], scalar1=shift, scalar2=1.0 / TWO_PI, op0=ALU.add, op1=ALU.mult)
            kf_b, kf_k = tmpf()
            E("dve", "tensor_copy", r=["tmpi"], w=[kf_k], out=kf_b[:, :T], in_=tmpi[:, :T])
            r_b, r_k = tmpf()
            E("dve", "scalar_tensor_tensor", r=[kf_k, ang_k], w=[r_k],
              out=r_b[:, :T], in0=kf_b[:, :T], scalar=-TWO_PI, in1=ang_b[:, :T], op0=ALU.mult, op1=ALU.add)
            E("dve", "tensor_scalar", r=[r_k], w=[r_k],
              out=r_b[:, :T], in0=r_b[:, :T], scalar1=PI_SAFE - shift, scalar2=-PI_SAFE - shift,
              op0=ALU.min, op1=ALU.max)
            E("act", "activation", r=[r_k], w=[dk], out=dstb[:, :T], in_=r_b[:, :T], func=AF.Sin, bias=shift, scale=1.0)


    def finish_sq(ti):
        tok0, T = tiles[ti]
        xb = xs[ti % 2]
        xkey = "x%d" % (ti % 2)
        for c in range(8):
            E("act", "activation", r=[(xkey, c)], w=[("hb", c)],
              out=hb[:, c, :T], in_=xb[:, c, :T], func=AF.Square, scale=1.0 / 32.0)

    def finish_tile(ti):
        tok0, T = tiles[ti]
        xb = xs[ti % 2]
        xkey = "x%d" % (ti % 2)
        xkeys = [(xkey, c) for c in range(8)]
        pb, pk = bank()
        for c in range(8):
            E("pe", "matmul", r=[("hb", c), "cbf"], w=[pk],
              out=pb[:, :T], lhsT=ones_m, rhs=hb[:, c, :T], start=(c == 0), stop=(c == 7))
        tb, tk = tmpf()
        E("act", "activation", r=[pk], w=[tk], out=tb[:, :T], in_=pb[:, :T], func=AF.Ln, bias=EPS, scale=1.0)
        E("act", "activation", r=[tk], w=[("rstd", 2)], out=rstd[2][:, :T], in_=tb[:, :T], func=AF.Exp, scale=-0.5)
        norm_apply(xb, xkey, T, 2, C_NLW, xb, xkey)
        o0 = tok0 - HALO
        E("pool", "dma_start", r=xkeys, dma=s_o[ti % 2], out=OUT3[:, :, o0:o0 + T], in_=xb[:, :, :T])

    load_x(0)
    E("sp", "dma_start", w=["posi"], dma=s_pos0, out=posi[:, :tiles[0][1]], in_=POS[:, tiles[0][0]:tiles[0][0] + tiles[0][1]])
    for _ in range(NSLOT):
        ws_issue()
    prep(0)

    for ti, (tok0, T) in enumerate(tiles):
        NB = T // 128
        gb0 = tok0 // 128
        is_halo = tok0 < HALO
        last_halo = is_halo and (tok0 + T == HALO)
        xb = xs[ti % 2]
        xkey = "x%d" % (ti % 2)
        xkeys = [(xkey, c) for c in range(8)]

        un_r = lambda k, T=T: un[:, k, :T]
        un_k = lambda k: ("un", k)
        cur = {"w": None, "k": None, "n": -1}

        def win_chunk(j):
            u = j // 4
            if cur["w"] is None or cur["n"] != u:
                if cur["w"] is not None:
                    ws_release()
                cur["w"], cur["k"] = ws_acquire()
                cur["n"] = u
            return cur["w"], cur["k"], (j % 4) * 128

        def rope_start(pb, pk, j, dst, dkey):
            qi = rot("qb", 2)
            qf, qfk = tmpf()
            E("act", "activation", r=[pk, "cols"], w=[qfk],
              out=qf[:, :T], in_=pb[:, :T], func=AF.Identity, bias=col(C_BIN + j), scale=1.0)
            E("act", "activation", r=[qfk], w=[("qb", qi)], out=qb[qi][:, :T], in_=qf[:, :T], func=AF.Copy)
            t1, k1 = tmpf()
            E("dve", "tensor_tensor", r=[qfk, "cosb"], w=[k1], out=t1[:, :T], in0=qf[:, :T], in1=cosb[:, :T], op=ALU.mult)
            return (qi, t1, k1, dst, dkey)

        def rope_finish(st):
            qi, t1, k1, dst, dkey = st
            rb, rk = bank()
            E("pe", "matmul", r=[("qb", qi), "cbf"], w=[rk],
              out=rb[:, :T], lhsT=rot_m, rhs=qb[qi][:, :T], start=True, stop=True)
            t2, k2 = tmpf()
            E("dve", "tensor_tensor", r=[rk, "sinb"], w=[k2], out=t2[:, :T], in0=rb[:, :T], in1=sinb[:, :T], op=ALU.mult)
            E("dve", "tensor_tensor", r=[k1, k2], w=[dkey], out=dst, in0=t1[:, :T], in1=t2[:, :T], op=ALU.add)

        if ti > 0 and not tiles[ti - 1][0] < HALO:
            finish_sq(ti - 1)
        pend = None
        for j in range(0, 5):
            wap, wkey, coff = win_chunk(j)
            pb, pk = proj(wap, wkey, 8, coff, un_r, un_k, T)
            if pend is not None:
                rope_finish(pend)
            if j == 0:
                pend = rope_start(pb, pk, 0, kr[:, :T], "kr")
            else:
                pend = rope_start(pb, pk, j, qr[:, j - 1, :T], ("qr", j - 1))
        wap, wkey, coff = win_chunk(5)
        vb, vk = bank()
        for i in range(NB):
            for k in range(8):
                E("pe", "matmul", r=[wkey, ("un", k)], w=[vk],
                  out=vb[:, i * 128:(i + 1) * 128], lhsT=un[:, k, i * 128:(i + 1) * 128],
                  rhs=wap[:, k, coff:coff + 128], start=(k == 0), stop=(k == 7))
        rope_finish(pend)
        for i in range(NB):
            E("dve", "tensor_tensor", r=[vk, "cols"], w=[("vcur", i)],
              out=vcur[:, i, :, 0:64], in0=vb[:, i * 128:(i + 1) * 128].rearrange("p (a b) -> p a b", a=2),
              in1=cols[:, C_BV:C_BV + 128].rearrange("p (a b) -> p a b", a=2), op=ALU.add)
        if ti > 0 and not tiles[ti - 1][0] < HALO:
            finish_tile(ti - 1)

        pts = {}

        def attn_A(i):
            gb = gb0 + i
            for hh in range(2):
                for kb in range(2):
                    if kb == 0:
                        if i == 0:
                            ksrc, kkey = kprev[hh * 64:(hh + 1) * 64, :], "kprev"
                        else:
                            ksrc, kkey = kr[hh * 64:(hh + 1) * 64, (i - 1) * 128:i * 128], "kr"
                        if gb == 0:
                            msk, mkey = mzero[:, :], "mzero"
                        elif gb == HALO // 128:
                            msk, mkey = mfirst, "cbf"
                        else:
                            msk, mkey = mprev, "cbf"
                    else:
                        ksrc, kkey = kr[hh * 64:(hh + 1) * 64, i * 128:(i + 1) * 128], "kr"
                        msk, mkey = mcur, "cbf"
                    sb, sk = bank()
                    E("pe", "matmul", r=[kkey] + [("qr", j) for j in range(4)], w=[sk],
                      out=sb[:, :], lhsT=ksrc, rhs=qr[hh * 64:(hh + 1) * 64, :, i * 128:(i + 1) * 128],
                      start=True, stop=True)
                    pi = rot("pt", NPT)
                    E("act", "activation", r=[sk], w=[("pt", pi)],
                      out=pt[pi][:, :], in_=sb[:, :], func=AF.Exp, scale=HD ** -0.5)
                    E("pool", "tensor_tensor", r=[("pt", pi), mkey], w=[("pt", pi)],
                      out=pt[pi][:, :], in0=pt[pi][:, :], in1=msk, op=ALU.mult)
                    pts[(i, hh, kb)] = pi

        def attn_B(i):
            for hh in range(2):
                ob, ok = bank()
                for kb in range(2):
                    if kb == 0:
                        if i == 0:
                            vsrc, vkey = vprev[:, hh, :], "vprev"
                        else:
                            vsrc, vkey = vcur[:, i - 1, hh, :], ("vcur", i - 1)
                    else:
                        vsrc, vkey = vcur[:, i, hh, :], ("vcur", i)
                    pi = pts[(i, hh, kb)]
                    E("pe", "matmul", r=[vkey, ("pt", pi)], w=[ok],
                      out=ob[:, :], lhsT=vsrc, rhs=pt[pi][:, :], start=(kb == 0), stop=False)
                E("pe", "matmul", r=["cbf", "sinkrow"], w=[ok],
                  out=ob[:, :], lhsT=cbf[0:1, B_SEL:B_SEL + 128], rhs=sinkrow[0:1, hh, :], start=False, stop=True)
                d1, dk1 = tmpf()
                E("act", "activation", r=[ok], w=[dk1], out=d1[0:64, :], in_=ob[64:128, :], func=AF.Ln)
                d2, dk2 = tmpf()
                E("act", "activation", r=[dk1], w=[dk2], out=d2[0:64, :], in_=d1[0:64, :], func=AF.Exp, scale=-1.0)
                E("dve", "tensor_tensor", r=[ok, dk2], w=[("ao", hh, i)],
                  out=ao[hh * 64:(hh + 1) * 64, :, i * 128:(i + 1) * 128],
                  in0=ob[0:64, :].rearrange("p (a b) -> p a b", a=4),
                  in1=d2[0:64, :].rearrange("p (a b) -> p a b", a=4), op=ALU.mult)

        work = [[] for _ in range(NB + 2)]
        for i in range(NB):
            work[i].append((attn_A, i))
            work[i + 2].append((attn_B, i))

        def conv_b_part(c, a2, k2):
            jb = CONV_J[("cb", c)]
            wap, wkey, coff = win_chunk(jb)
            pbb, pbk = proj(wap, wkey, 8, coff, un_r, un_k, T)
            E("dve", "scalar_tensor_tensor", r=[pbk, k2, "cols"], w=[("bc", c)],
              out=bc[:, c, :T], in0=pbb[:, :T], scalar=col(C_BIN + jb), in1=a2[:, :T], op0=ALU.add, op1=ALU.mult)

        pend_b = None
        for c in range(8):
            jc, jx = CONV_J[("cc", c)], CONV_J[("cx", c)]
            wap, wkey, coff = win_chunk(jc)
            pc, pck = proj(wap, wkey, 8, coff, un_r, un_k, T)
            cs, csk = tmpf()
            E("act", "activation", r=[pck, "cols"], w=[csk],
              out=cs[:, :T], in_=pc[:, :T], func=AF.Identity, bias=col(C_BIN + jc), scale=1.0)
            wap, wkey, coff = win_chunk(jx)
            px, pxk = proj(wap, wkey, 8, coff, un_r, un_k, T)
            xi = rot("cxb", 2)
            cx = cxb[xi]
            cxk = ("cxb", xi)
            E("pool", "tensor_copy", r=[("cxcarry", c)], w=[cxk], out=cx[:, PFX - 2:PFX], in_=cxcarry[:, c, :])
            E("dve", "scalar_tensor_tensor", r=[pxk, csk, "cols", cxk], w=[cxk],
              out=cx[:, PFX:PFX + T], in0=px[:, :T], scalar=col(C_BIN + jx), in1=cs[:, :T], op0=ALU.add, op1=ALU.mult)
            if last_halo:
                E("pool", "tensor_scalar", r=[cxk, "cols"], w=[("cxcarry", c)],
                  out=cxcarry[:, c, :], in0=cx[:, PFX + T - 2:PFX + T], scalar1=col(C_FLAG), scalar2=None, op0=ALU.mult)
            else:
                E("pool", "tensor_copy", r=[cxk], w=[("cxcarry", c)], out=cxcarry[:, c, :], in_=cx[:, PFX + T - 2:PFX + T])
            a0, k0 = tmpf()
            E("dve", "tensor_scalar", r=[cxk, "cols"], w=[k0],
              out=a0[:, :T], in0=cx[:, PFX:PFX + T], scalar1=col(C_CVW + 3 * c + 2), scalar2=None, op0=ALU.mult)
            a1, k1 = tmpf()
            E("dve", "scalar_tensor_tensor", r=[cxk, k0, "cols"], w=[k1],
              out=a1[:, :T], in0=cx[:, PFX - 1:PFX - 1 + T], scalar=col(C_CVW + 3 * c + 1), in1=a0[:, :T],
              op0=ALU.mult, op1=ALU.add)
            ci = rot("cva", 2)
            E("dve", "scalar_tensor_tensor", r=[cxk, k1, "cols"], w=[("silu", ci)],
              out=cva[ci][:, :T], in0=cx[:, PFX - 2:PFX - 2 + T], scalar=col(C_CVW + 3 * c), in1=a1[:, :T],
              op0=ALU.mult, op1=ALU.add)
            if pend_b is not None:
                conv_b_part(*pend_b)
            pend_b = (c, cva[ci], ("silu", ci))
            if work:
                for fn, arg in work.pop(0):
                    fn(arg)
        conv_b_part(*pend_b)
        while work:
            for fn, arg in work.pop(0):
                fn(arg)

        for c in range(8):
            for g, dst, dk in ((0, sgc, "sgc"), (1, sga, "sga")):
                j = 30 + 2 * c + g
                wap, wkey, coff = win_chunk(j)
                pg, pgk = proj(wap, wkey, 8, coff, un_r, un_k, T)
                E("act", "activation", r=[pgk, "cols"], w=[(dk, c)],
                  out=dst[:, c, :T], in_=pg[:, :T], func=AF.Sigmoid, bias=col(C_BIN + j), scale=1.0)
        ws_release()
        cur["w"] = None

        E("dve", "tensor_copy", r=["kr"], w=["kprev"], out=kprev[:, :], in_=kr[:, T - 128:T])
        E("dve", "tensor_copy", r=[("vcur", NB - 1)], w=["vprev"], out=vprev[:, :, :], in_=vcur[:, NB - 1, :, :])

        wco0, wco0k = ws_acquire()
        wao_, waok = ws_acquire()
        wco1 = wco1k = None
        aokeys = [("ao", hh, i) for hh in range(2) for i in range(NB)]
        for c in range(8):
            if c == 4:
                ws_release()
                wco1, wco1k = ws_acquire()
            wc, wck = (wco0, wco0k) if c < 4 else (wco1, wco1k)
            yc, yck = proj(wc, wck, 8, (c % 4) * 128, lambda k: bc[:, k, :T], lambda k: ("bc", k), T)
            ya, yak = bank()
            for j in range(4):
                E("pe", "matmul", r=[waok] + aokeys, w=[yak],
                  out=ya[:, :T], lhsT=wao_[:, j, c * 128:(c + 1) * 128], rhs=ao[:, j, :T],
                  start=(j == 0), stop=(j == 3))
            m1, mk1 = tmpf()
            E("dve", "tensor_tensor", r=[yck, ("sgc", c)], w=[mk1],
              out=m1[:, :T], in0=yc[:, :T], in1=sgc[:, c, :T], op=ALU.mult)
            m2, mk2 = tmpf()
            E("dve", "scalar_tensor_tensor", r=[yak, ("sga", c), "cols"], w=[mk2],
              out=m2[:, :T], in0=ya[:, :T], scalar=col(C_BAO + c), in1=sga[:, c, :T], op0=ALU.add, op1=ALU.mult)
            E("dve", "tensor_tensor", r=[mk1, mk2], w=[("mg", c)],
              out=mg[:, c, :T], in0=m1[:, :T], in1=m2[:, :T], op=ALU.add)
        ws_release()
        ws_release()

        wm = wmk = None
        for c in range(8):
            if c % 4 == 0:
                if c == 4:
                    ws_release()
                wm, wmk = ws_acquire()
            pm, pmk = proj(wm, wmk, 8, (c % 4) * 128, lambda k: mg[:, k, :T], lambda k: ("mg", k), T)
            E("dve", "tensor_tensor", r=[pmk, (xkey, c)], w=[(xkey, c)],
              out=xb[:, c, :T], in0=xb[:, c, :T], in1=pm[:, :T], op=ALU.add)
        ws_release()

        if ti + 1 < len(tiles):
            load_x(ti + 1)

        norm_stats(xb, xkey, T, 1)
        norm_apply(xb, xkey, T, 1, C_NFW, un, "un")

        wunits = {}
        pre = {}

        def ffn_up_part(f):
            g = f // 2
            if g not in wunits:
                wunits[g] = ws_acquire()
            wu, wuk = wunits[g]
            lo = (f % 2) * 128
            if ("up", f) in pre:
                pu, puk = pre[("up", f)]
            else:
                pu, puk = proj(wu, wuk, 8, lo, un_r, un_k, T)
            ui = rot("upb", 3)
            ub = upb[ui]
            ubk = ("upb", ui)
            E("pool", "tensor_copy", r=[("upcarry", f)], w=[ubk], out=ub[:, PFX - 2:PFX], in_=upcarry[:, f, :])
            E("act", "activation", r=[puk, ubk], w=[ubk], out=ub[:, PFX:PFX + T], in_=pu[:, :T], func=AF.Copy)
            a0, k0 = tmpf()
            E("act", "activation", r=[puk, "cols"], w=[k0], out=a0[:, :T], in_=pu[:, :T], func=AF.Identity,
              scale=col(C_FCW + 3 * f + 2), bias=col(C_FCB + f))
            if last_halo:
                E("pool", "tensor_scalar", r=[ubk, "cols"], w=[("upcarry", f)],
                  out=upcarry[:, f, :], in0=ub[:, PFX + T - 2:PFX + T], scalar1=col(C_FLAG), scalar2=None, op0=ALU.mult)
            else:
                E("pool", "tensor_copy", r=[ubk], w=[("upcarry", f)], out=upcarry[:, f, :], in_=ub[:, PFX + T - 2:PFX + T])
            a1, k1 = tmpf()
            E("dve", "scalar_tensor_tensor", r=[ubk, k0, "cols"], w=[k1],
              out=a1[:, :T], in0=ub[:, PFX - 1:PFX - 1 + T], scalar=col(C_FCW + 3 * f + 1), in1=a0[:, :T],
              op0=ALU.mult, op1=ALU.add)
            a2, k2 = tmpf()
            E("dve", "scalar_tensor_tensor", r=[ubk, k1, "cols"], w=[k2],
              out=a2[:, :T], in0=ub[:, PFX - 2:PFX - 2 + T], scalar=col(C_FCW + 3 * f), in1=a1[:, :T],
              op0=ALU.mult, op1=ALU.add)
            return (a2, k2)

        def ffn_gate_part(f, a2k):
            a2, k2 = a2k
            si = rot("silu", 2)
            E("act", "activation", r=[k2], w=[("silu", si)], out=silub[si][:, :T], in_=a2[:, :T], func=AF.Silu)
            wu, wuk = wunits[f // 2]
            lo = (f % 2) * 128
            if ("gate", f) in pre:
                pg, pgk = pre[("gate", f)]
            else:
                pg, pgk = proj(wu, wuk, 8, 256 + lo, un_r, un_k, T)
            E("dve", "tensor_tensor", r=[pgk, ("silu", si)], w=[("hb", f)],
              out=hb[:, f, :T], in0=pg[:, :T], in1=silub[si][:, :T], op=ALU.mult)
            if f % 2 == 1:
                ws_release()

        if is_halo:
            for f in range(NFC):
                g = f // 2
                if f % 2 == 0:
                    wu, wuk = ws_acquire()
                pu, puk = proj(wu, wuk, 8, (f % 2) * 128, un_r, un_k, T)
                ui = rot("upb", 3)
                ub = upb[ui]
                ubk = ("upb", ui)
                E("act", "activation", r=[puk], w=[ubk], out=ub[:, PFX:PFX + T], in_=pu[:, :T], func=AF.Copy)
                if last_halo:
                    E("pool", "tensor_scalar", r=[ubk, "cols"], w=[("upcarry", f)],
                      out=upcarry[:, f, :], in0=ub[:, PFX + T - 2:PFX + T], scalar1=col(C_FLAG), scalar2=None,
                      op0=ALU.mult)
                else:
                    E("pool", "tensor_copy", r=[ubk], w=[("upcarry", f)], out=upcarry[:, f, :],
                      in_=ub[:, PFX + T - 2:PFX + T])
                if f % 2 == 1:
                    ws_release()
            if ti + 1 < len(tiles):
                prep(ti + 1)
        else:
            wunits[0] = ws_acquire()
            wu0, wuk0 = wunits[0]
            pbanks = [bank() for _ in range(4)]
            for k in range(8):
                for j in range(4):
                    E("pe", "matmul", r=[wuk0, ("un", k)], w=[pbanks[j][1]],
                      out=pbanks[j][0][:, :T], lhsT=wu0[:, k, j * 128:(j + 1) * 128], rhs=un[:, k, :T],
                      start=(k == 0), stop=(k == 7))
            pre[("up", 0)], pre[("up", 1)], pre[("gate", 0)], pre[("gate", 1)] = pbanks
            prev = None
            for f in range(NFC):
                si = ffn_up_part(f)
                if prev is not None:
                    ffn_gate_part(*prev)
                prev = (f, si)
            ffn_gate_part(*prev)

            for c in range(8):
                wd, wdk = ws_acquire()
                pd, pdk = proj(wd, wdk, NFC, 0, lambda k: hb[:, k, :T], lambda k: ("hb", k), T)
                ws_release()
                if c == 2 and ti + 1 < len(tiles):
                    prep(ti + 1)
                E("dve", "tensor_tensor", r=[pdk, (xkey, c)], w=[(xkey, c)],
                  out=xb[:, c, :T], in0=xb[:, c, :T], in1=pd[:, :T], op=ALU.add)

        if ti + 1 == len(tiles) and not is_halo:
            finish_sq(ti)
            finish_tile(ti)

    E("sp", None, w=[("x0", c) for c in range(8)] + [("x1", c) for c in range(8)]
      + [("ring", i) for i in range(NSLOT)] + ["cbf", "posi", "cols"])
    P.emit()
    P.close()
    return nc, P


def _perm_in():
    cb0, cc0, cx0, q0, k0, v0, gc0, ga0 = 0, 1024, 2048, 3072, 3584, 3712, 3840, 4864
    perm = []
    perm += list(range(k0, k0 + 128))
    for j in range(4):
        perm += list(range(q0 + j * 64, q0 + (j + 1) * 64))
        perm += list(range(q0 + (4 + j) * 64, q0 + (5 + j) * 64))
    perm += list(range(v0, v0 + 128))
    base = {"cc": cc0, "cx": cx0, "cb": cb0}
    for kind, c in CONV_ORDER:
        perm += list(range(base[kind] + c * 128, base[kind] + (c + 1) * 128))
    for c in range(8):
        perm += list(range(gc0 + c * 128, gc0 + (c + 1) * 128))
        perm += list(range(ga0 + c * 128, ga0 + (c + 1) * 128))
    assert len(perm) == INW
    return np.array(perm, dtype=np.int64)


def _pkc(w):
    K, N = w.shape
    return np.ascontiguousarray(w.reshape(K // 128, 128, N).transpose(1, 0, 2))


def _colchunks(v):
    return np.ascontiguousarray(v.reshape(-1, 128).T)


def prepare_inputs(x, positions, norm_mix_w, w_in, b_in, conv_mix_w, w_conv_out, w_attn_out, b_attn_out, sinks,
                   w_mix_out, norm_ffn_w, w_ffn_up, ffn_conv_w, ffn_conv_b, w_ffn_down, norm_final_w):
    f32 = np.float32
    perm = _perm_in()
    win = _pkc(np.asarray(w_in[0], f32)[:, perm])
    wco = _pkc(np.asarray(w_conv_out[0], f32))
    arow = []
    for j in range(4):
        arow += list(range(j * 64, (j + 1) * 64)) + list(range((4 + j) * 64, (5 + j) * 64))
    wao = _pkc(np.asarray(w_attn_out[0], f32)[np.array(arow)])
    wmo = _pkc(np.asarray(w_mix_out[0], f32))
    uperm = []
    for g in range(NFC // 2):
        uperm += list(range(2 * g * 128, (2 * g + 2) * 128))
        uperm += list(range(DFF + 2 * g * 128, DFF + (2 * g + 2) * 128))
    wup = _pkc(np.asarray(w_ffn_up[0], f32)[:, np.array(uperm)])
    wd = np.asarray(w_ffn_down[0], f32).reshape(NFC, 128, 8, 128)
    wdn = np.ascontiguousarray(wd.transpose(1, 2, 0, 3))

    cols = np.zeros((128, NCOLS), f32)
    cols[:, C_NMW:C_NMW + 8] = _colchunks(np.asarray(norm_mix_w[0], f32))
    cols[:, C_NFW:C_NFW + 8] = _colchunks(np.asarray(norm_ffn_w[0], f32))
    cols[:, C_NLW:C_NLW + 8] = _colchunks(np.asarray(norm_final_w, f32))
    cols[:, C_BIN:C_BIN + 46] = _colchunks(np.asarray(b_in[0], f32)[perm])
    cols[:, C_BAO:C_BAO + 8] = _colchunks(np.asarray(b_attn_out[0], f32))
    cw = np.asarray(conv_mix_w[0], f32)
    cols[:, C_CVW:C_CVW + 24] = cw.reshape(3, 8, 128).transpose(2, 1, 0).reshape(128, 24)
    fw = np.asarray(ffn_conv_w[0], f32)
    cols[:, C_FCW:C_FCW + 66] = fw.reshape(3, NFC, 128).transpose(2, 1, 0).reshape(128, 66)
    cols[:, C_FCB:C_FCB + NFC] = _colchunks(np.asarray(ffn_conv_b[0], f32))
    inv_freq = (10000.0 ** (-np.arange(0, HD, 2, dtype=f32) / HD)).astype(f32)
    cols[:, C_INVF] = inv_freq[np.arange(128) % 32]
    cols[:, C_SINK:C_SINK + 8] = np.asarray(sinks[0], f32)[None, :]
    cols[:, C_BV:C_BV + 128] = np.asarray(b_in[0], f32)[3712:3840][None, :]

    cbf = np.zeros((128, NBCOLS), f32)
    cbf[:, B_ONES:B_ONES + 128] = 1.0
    rm = np.zeros((128, 128), f32)
    for m in range(128):
        if m % 64 < 32:
            rm[m + 32, m] = -1.0
        else:
            rm[m - 32, m] = 1.0
    cbf[:, B_ROT:B_ROT + 128] = rm
    kk = np.arange(128)[:, None]
    qq = np.arange(128)[None, :]
    mp = (kk > qq).astype(f32)
    mc = (kk <= qq).astype(f32)
    cbf[:, B_MPREV:B_MPREV + 512] = np.tile(mp, (1, 4))
    cbf[:, B_MCUR:B_MCUR + 512] = np.tile(mc, (1, 4))
    cbf[0, B_SEL + 64:B_SEL + 128] = 1.0

    x = np.asarray(x, f32)
    positions = np.asarray(positions, np.int32)
    in_maps = []
    for core in range(NCORE):
        b, half = core // 2, core % 2
        t0 = half * TOK
        xT = np.zeros((D, NTOK), f32)
        xT[:, HALO:] = x[b, t0:t0 + TOK, :].T
        pos = np.zeros((NTOK,), np.int32)
        pos[HALO:] = positions[b, t0:t0 + TOK]
        ccols = cols.copy()
        ccbf = cbf.copy()
        if half == 1:
            xT[:, :HALO] = x[b, t0 - HALO:t0, :].T
            pos[:HALO] = positions[b, t0 - HALO:t0]
            ccols[:, C_FLAG] = 1.0
            ccbf[:, B_MFIRST:B_MFIRST + 512] = np.tile(mp, (1, 4))
        in_maps.append({
            "xT": xT, "posb": np.ascontiguousarray(np.broadcast_to(pos[None, :], (128, NTOK))),
            "cols": ccols, "cbf": ccbf, "win": win, "wco": wco, "wao": wao, "wmo": wmo, "wup": wup, "wdn": wdn,
        })
    return in_maps


def kernel(**inputs):
    in_maps = prepare_inputs(**inputs)
    nc, _ = build_program()
    res = run_bass_kernel_spmd(nc, in_maps, core_ids=list(range(NCORE)))
    out = np.empty((BATCH, SEQ, D), np.float32)
    for core in range(NCORE):
        b, half = core // 2, core % 2
        out[b, half * TOK:(half + 1) * TOK, :] = res.results[core]["outT"].T
    return out
```

```python
from contextlib import ExitStack
import math
import os
import numpy as np
import concourse.bass as bass
import concourse.mybir as mybir
from concourse.bass_utils import run_bass_kernel_spmd

F32 = mybir.dt.float32
BF16 = mybir.dt.bfloat16
I32 = mybir.dt.int32
AF = mybir.ActivationFunctionType
ALU = mybir.AluOpType

D = 1024
SEQ = 8192
BATCH = 4
DFF = 2816
NQ = 8
NKV = 2
HD = 64
EPS = 1e-5
NCORE = 8
TOK = 4096
HALO = 256
NTOK = TOK + HALO
INW = 5888
NFC = DFF // 128
TILES = [(0, 256)] + [(256 + 512 * i, 512) for i in range(8)]
NSLOT = 4
NWBG = 6
PFX = 8
NBANK = int(os.environ.get('KNBANK', 8))
SLOT_ELEMS = 4096
TWO_PI = 2.0 * math.pi
PI_SAFE = 3.1415925

CONV_ORDER = [("cc", 0), ("cx", 0)]
for _c in range(1, 8):
    CONV_ORDER += [("cc", _c), ("cx", _c), ("cb", _c - 1)]
CONV_ORDER += [("cb", 7)]
CONV_J = {k: 6 + i for i, k in enumerate(CONV_ORDER)}

C_NMW = 0
C_NFW = 8
C_NLW = 16
C_BIN = 24
C_BAO = 70
C_CVW = 78
C_FCW = 102
C_FCB = 168
C_INVF = 190
C_FLAG = 191
C_SINK = 192
C_BV = 200
NCOLS = 328

B_ONES = 0
B_ROT = 128
B_MPREV = 256
B_MCUR = 768
B_MFIRST = 1280
B_SEL = 1792
NBCOLS = 1920


ENGS = ("pe", "act", "dve", "pool", "sp")


class Op:
    __slots__ = ("eng", "fn", "deps", "signal", "sig_idx", "dma_sem", "dma_cnt", "pos")

    def __init__(self, eng, fn):
        self.eng = eng
        self.fn = fn
        self.deps = []
        self.signal = False
        self.sig_idx = 0
        self.dma_sem = None
        self.dma_cnt = 0


class Prog:
    def __init__(self, nc):
        self.nc = nc
        self.es = ExitStack()
        self.ops = {e: [] for e in ENGS}
        self.state = {}
        self.dma_counts = {}
        self.sems = {}

    def sbuf(self, name, shape, dtype):
        return self.es.enter_context(self.nc.sbuf_tensor("sb_" + name, list(shape), dtype))

    def psum(self, name, shape, dtype):
        return self.es.enter_context(self.nc.psum_tensor(name, list(shape), dtype))

    def sem(self, name):
        s = self.es.enter_context(self.nc.semaphore("sm_" + name))
        self.sems[name] = s
        self.dma_counts[name] = 0
        return name

    def op(self, eng, fn, r=(), w=(), dma=None):
        self.nrec = getattr(self, "nrec", 0) + 1
        if self.nrec > getattr(self, "limit", 10 ** 9) and fn[0] is not None:
            return None
        o = Op(eng, fn)
        deps = {}
        for k in r:
            st = self.state.get(k)
            if st is not None and st[0] is not None:
                deps[id(st[0])] = st[0]
        for k in w:
            st = self.state.get(k)
            if st is None:
                continue
            if st[0] is not None and (st[0].eng != eng or st[0].dma_sem is not None or eng != "pe"):
                deps[id(st[0])] = st[0]
            for rd in st[1]:
                if rd.eng != eng or rd.dma_sem is not None or eng != "pe":
                    deps[id(rd)] = rd
        for k in r:
            st = self.state.setdefault(k, [None, []])
            st[1].append(o)
        for k in w:
            self.state[k] = [o, []]
        best = {}
        for d in deps.values():
            if d.dma_sem is not None:
                key = ("d", d.dma_sem)
                if key not in best or d.dma_cnt > best[key].dma_cnt:
                    best[key] = d
            else:
                key = ("e", d.eng)
                if key not in best or d.pos > best[key].pos:
                    best[key] = d
        for d in best.values():
            if d.dma_sem is None:
                d.signal = True
        o.deps = list(best.values())
        o.pos = len(self.ops[eng])
        if dma is not None:
            o.dma_sem = dma
            self.dma_counts[dma] += 16
            o.dma_cnt = self.dma_counts[dma]
        self.ops[eng].append(o)
        return o

    def emit(self):
        nc = self.nc
        engsem = {e: self.es.enter_context(nc.semaphore("es_" + e)) for e in ENGS}
        for e in ENGS:
            c = 0
            for o in self.ops[e]:
                if o.signal and o.dma_sem is None:
                    c += 1
                    o.sig_idx = c
        block = self.es.enter_context(nc.Block())
        stats = {}

        def run(e, handle):
            waited = {}
            nw = 0
            for o in self.ops[e]:
                for d in o.deps:
                    if d.dma_sem is not None:
                        key, sem, val = "d_" + d.dma_sem, self.sems[d.dma_sem], d.dma_cnt
                    else:
                        key, sem, val = "e_" + d.eng, engsem[d.eng], d.sig_idx
                    if waited.get(key, 0) >= val:
                        continue
                    waited[key] = val
                    handle.wait_ge(sem, val)
                    nw += 1
                meth, kw = o.fn
                ins = getattr(handle, meth)(**kw) if meth is not None else None
                if o.dma_sem is not None:
                    ins.then_inc(self.sems[o.dma_sem], 16)
                elif o.signal:
                    ins.then_inc(engsem[e], 1)
            stats[e] = (len(self.ops[e]), nw)

        @block.sync
        def _(h):
            run("sp", h)

        @block.tensor
        def _(h):
            run("pe", h)

        @block.scalar
        def _(h):
            run("act", h)

        @block.vector
        def _(h):
            run("dve", h)

        @block.gpsimd
        def _(h):
            run("pool", h)

        self.stats = stats

    def close(self):
        self.es.close()


def build_program(tiles=TILES):
    nc = bass.Bass("TRN2", target_bir_lowering=False)
    XT = nc.dram_tensor("xT", [D, NTOK], F32, kind="ExternalInput").ap()
    POS = nc.dram_tensor("posb", [128, NTOK], I32, kind="ExternalInput").ap()
    COLS = nc.dram_tensor("cols", [128, NCOLS], F32, kind="ExternalInput").ap()
    CBF = nc.dram_tensor("cbf", [128, NBCOLS], F32, kind="ExternalInput").ap()
    WIN = nc.dram_tensor("win", [128, 8, INW], F32, kind="ExternalInput").ap()
    WCO = nc.dram_tensor("wco", [128, 8, D], F32, kind="ExternalInput").ap()
    WAO = nc.dram_tensor("wao", [128, 4, D], F32, kind="ExternalInput").ap()
    WMO = nc.dram_tensor("wmo", [128, 8, D], F32, kind="ExternalInput").ap()
    WUP = nc.dram_tensor("wup", [128, 8, 2 * DFF], F32, kind="ExternalInput").ap()
    WDN = nc.dram_tensor("wdn", [128, 8, NFC, 128], F32, kind="ExternalInput").ap()
    OUT = nc.dram_tensor("outT", [D, TOK], F32, kind="ExternalOutput").ap()
    NUNIT = 36
    SCR = nc.dram_tensor("wscr", [NUNIT, 128, SLOT_ELEMS], BF16, kind="Internal").ap()
    XT3 = XT.rearrange("(c p) t -> p c t", p=128)
    OUT3 = OUT.rearrange("(c p) t -> p c t", p=128)

    P = Prog(nc)
    P.limit = int(os.environ.get("KLIMIT", 10 ** 9))
    P.nrec = 0
    bank_touch = [0] * 8

    def E(eng, meth, r=(), w=(), dma=None, **kw):
        for k in list(r) + list(w):
            if isinstance(k, tuple) and k[0] == "ps":
                bank_touch[k[1]] = P.nrec + 1
        return P.op(eng, (meth, kw), r=r, w=w, dma=dma)

    TM = 512
    xs = [P.sbuf(f"xs{i}", [128, 8, TM], F32) for i in range(2)]
    cols = P.sbuf("cols", [128, NCOLS], F32)
    cbf = P.sbuf("cbf", [128, NBCOLS], BF16)
    ring = [P.sbuf(f"ring{i}", [128, SLOT_ELEMS], BF16) for i in range(NSLOT)]
    sq = [P.sbuf(f"sq{i}", [128, TM], BF16) for i in range(2)]
    rstd = [P.sbuf(f"rstd{i}", [128, TM], F32) for i in range(3)]
    un = P.sbuf("un", [128, 8, TM], BF16)
    NTMP = 8
    tmp = [P.sbuf(f"tmp{i}", [128, TM], F32) for i in range(NTMP)]
    tmpi = P.sbuf("tmpi", [128, TM], I32)
    cosb = P.sbuf("cosb", [128, TM], F32)
    sinb = P.sbuf("sinb", [128, TM], F32)
    posi = P.sbuf("posi", [128, TM], I32)
    qb = [P.sbuf(f"qb{i}", [128, TM], BF16) for i in range(2)]
    qr = P.sbuf("qr", [128, 4, TM], BF16)
    kr = P.sbuf("kr", [128, TM], BF16)
    kprev = P.sbuf("kprev", [128, 128], BF16)
    vcur = P.sbuf("vcur", [128, 4, 2, 128], BF16)
    vprev = P.sbuf("vprev", [128, 2, 128], BF16)
    cxb = [P.sbuf(f"cxb{i}", [128, TM + PFX], F32) for i in range(2)]
    cxcarry = P.sbuf("cxcarry", [128, 8, 2], F32)
    bc = P.sbuf("bc", [128, 8, TM], BF16)
    sgc = P.sbuf("sgc", [128, 8, TM], BF16)
    sga = P.sbuf("sga", [128, 8, TM], BF16)
    NPT = 12
    pt = [P.sbuf(f"pt{i}", [128, TM], BF16) for i in range(NPT)]
    mzero = P.sbuf("mzero", [128, TM], BF16)
    sinkrow = P.sbuf("sinkrow", [1, 2, TM], BF16)
    sinktmp = P.sbuf("sinktmp", [128, 8], F32)
    zf = P.sbuf("zf", [128, 128], F32)
    ao = P.sbuf("ao", [128, 4, TM], BF16)
    mg = P.sbuf("mg", [128, 8, TM], BF16)
    upb = [P.sbuf(f"upb{i}", [128, TM + PFX], F32) for i in range(3)]
    upcarry = P.sbuf("upcarry", [128, NFC, 2], F32)
    silub = [P.sbuf(f"silub{i}", [128, TM], F32) for i in range(2)]
    cva = silub
    hb = P.sbuf("hb", [128, NFC, TM], BF16)
    banks = [P.psum(f"ps{i}", [128, TM], F32) for i in range(8)]

    s_ring = [P.sem(f"ring{i}") for i in range(NSLOT)]
    s_wb = [P.sem(f"wb{i}") for i in range(NSLOT)]
    s_ld = [P.sem(f"ld{i}") for i in range(NSLOT)]
    s_x = [P.sem(f"x{i}") for i in range(2)]
    s_o = [P.sem(f"o{i}") for i in range(2)]
    s_pos = P.sem("pos")
    s_pos0 = P.sem("pos0")
    s_x0 = P.sem("x0first")
    s_cols = P.sem("cols")
    s_cbf = P.sem("cbf")

    ctr = {"bank": 0, "tmp": 0, "sq": 0, "qb": 0, "pt": 0, "cxb": 0, "upb": 0, "silu": 0, "cva": 0}

    def rot(name, n):
        i = ctr[name]
        ctr[name] = (i + 1) % n
        return i

    def bank():
        b = min(range(NBANK), key=lambda i: bank_touch[i])
        bank_touch[b] = P.nrec + 1
        return banks[b], ("ps", b)

    def tmpf():
        i = rot("tmp", NTMP)
        return tmp[i], ("tmp", i)

    units = []
    NHALO_UNITS = 0
    for _ti, (_tok0, _T) in enumerate(tiles):
        _halo = _tok0 < HALO
        for u in range(12):
            n = min(512, INW - u * 512)
            units.append((WIN[:, :, u * 512:u * 512 + n], (8, n)))
        units.append((WCO[:, :, 0:512], (8, 512)))
        units.append((WAO[:, :, :], (4, 1024)))
        units.append((WCO[:, :, 512:1024], (8, 512)))
        units.append((WMO[:, :, 0:512], (8, 512)))
        units.append((WMO[:, :, 512:1024], (8, 512)))
        for u in range(11):
            if _halo:
                units.append((WUP[:, :, u * 512:u * 512 + 256], (8, 256)))
            else:
                units.append((WUP[:, :, u * 512:(u + 1) * 512], (8, 512)))
        if not _halo:
            for c in range(8):
                units.append((WDN[:, c, :, :], (NFC, 128)))
        if _halo:
            NHALO_UNITS = len(units)
    ws = {"issued": 0, "acq": 0}

    def ws_issue():
        n = ws["issued"]
        if n >= len(units):
            return
        src, (a, b) = units[n]
        slot = n % NSLOT
        dst = ring[slot][:, 0:a * b].rearrange("p (a b) -> p a b", a=a)
        if n < NHALO_UNITS:
            t, u, g = 0, n, 0
        else:
            t, u = 1 + (n - NHALO_UNITS) // NUNIT, (n - NHALO_UNITS) % NUNIT
            g = u % NWBG
        if t >= g + 2:
            E("sp", "dma_start", r=[("scr", u)], w=[("ring", slot)], dma=s_ld[slot],
              out=ring[slot][:, 0:a * b], in_=SCR[u, :, 0:a * b])
        else:
            E("pool", "dma_start", w=[("ring", slot)], dma=s_ring[slot], out=dst, in_=src)
            if t == g + 1:
                E("sp", "dma_start", r=[("ring", slot)], w=[("scr", u)], dma=s_wb[slot],
                  out=SCR[u, :, 0:a * b], in_=ring[slot][:, 0:a * b])
        ws["issued"] = n + 1

    def ws_acquire():
        n = ws["acq"]
        assert n < ws["issued"], "weight ring underflow"
        ws["acq"] = n + 1
        _, (a, b) = units[n]
        slot = n % NSLOT
        return ring[slot][:, 0:a * b].rearrange("p (a b) -> p a b", a=a), ("ring", slot)

    def ws_release():
        ws_issue()

    E("sp", "dma_start", w=["cols"], dma=s_cols, out=cols[:], in_=COLS)
    E("pool", "dma_start", w=["cbf"], dma=s_cbf, out=cbf[:], in_=CBF)
    ones_m = cbf[:, B_ONES:B_ONES + 128]
    rot_m = cbf[:, B_ROT:B_ROT + 128]
    mprev = cbf[:, B_MPREV:B_MPREV + 512]
    mcur = cbf[:, B_MCUR:B_MCUR + 512]
    mfirst = cbf[:, B_MFIRST:B_MFIRST + 512]

    def col(i):
        return cols[:, i:i + 1]

    E("dve", "memset", w=["mzero"], ap=mzero[:], constant=0.0)
    E("dve", "memset", w=["kprev"], ap=kprev[:], constant=0.0)
    E("dve", "memset", w=["vprev"], ap=vprev[:], constant=0.0)
    E("dve", "memset", w=["vcur"], ap=vcur[:], constant=1.0)
    E("dve", "memset", w=["cxcarry"], ap=cxcarry[:], constant=0.0)
    E("dve", "memset", w=["upcarry"], ap=upcarry[:], constant=0.0)
    E("dve", "memset", w=["zf"], ap=zf[:], constant=0.0)
    E("act", "activation", r=["cols"], w=["sinktmp"], out=sinktmp[:], in_=cols[:, C_SINK:C_SINK + 8], func=AF.Exp)
    for hh in range(2):
        for j in range(4):
            E("dve", "tensor_scalar", r=["zf", "sinktmp"], w=["sinkrow"],
              out=sinkrow[0:1, hh, j * 128:(j + 1) * 128], in0=zf[0:1, :],
              scalar1=sinktmp[0:1, hh * 4 + j:hh * 4 + j + 1], scalar2=None, op0=ALU.add)

    def norm_stats(xb, xkey, T, ridx):
        pb, pk = bank()
        for c in range(8):
            si = rot("sq", 2)
            E("act", "activation", r=[(xkey, c)], w=[("sq", si)],
              out=sq[si][:, :T], in_=xb[:, c, :T], func=AF.Square, scale=1.0 / 32.0)
            E("pe", "matmul", r=[("sq", si), "cbf"], w=[pk],
              out=pb[:, :T], lhsT=ones_m, rhs=sq[si][:, :T], start=(c == 0), stop=(c == 7))
        tb, tk = tmpf()
        E("act", "activation", r=[pk], w=[tk], out=tb[:, :T], in_=pb[:, :T], func=AF.Ln, bias=EPS, scale=1.0)
        E("act", "activation", r=[tk], w=[("rstd", ridx)], out=rstd[ridx][:, :T], in_=tb[:, :T], func=AF.Exp, scale=-0.5)

    def norm_apply(xb, xkey, T, ridx, wc0, dst, dkey):
        for c in range(8):
            E("dve", "scalar_tensor_tensor", r=[(xkey, c), ("rstd", ridx), "cols"], w=[(dkey, c)],
              out=dst[:, c, :T], in0=xb[:, c, :T], scalar=col(wc0 + c), in1=rstd[ridx][:, :T],
              op0=ALU.mult, op1=ALU.mult)

    def proj(wap, wkey, nk, coff, rhs_fn, rkeys, T):
        pb, pk = bank()
        for k in range(nk):
            E("pe", "matmul", r=[wkey, rkeys(k)], w=[pk],
              out=pb[:, :T], lhsT=wap[:, k, coff:coff + 128], rhs=rhs_fn(k), start=(k == 0), stop=(k == nk - 1))
        return pb, pk

    def load_x(ti):
        tok0, T = tiles[ti]
        xb = xs[ti % 2]
        xkey = "x%d" % (ti % 2)
        E("sp" if ti == 0 else "pool", "dma_start", w=[(xkey, c) for c in range(8)], dma=s_x0 if ti == 0 else s_x[ti % 2],
          out=xb[:, :, :T], in_=XT3[:, :, tok0:tok0 + T])

    def prep(ti):
        tok0, T = tiles[ti]
        xb = xs[ti % 2]
        xkey = "x%d" % (ti % 2)
        if ti > 0:
            E("pool", "dma_start", w=["posi"], dma=s_pos, out=posi[:, :T], in_=POS[:, tok0:tok0 + T])

        norm_stats(xb, xkey, T, 0)
        norm_apply(xb, xkey, T, 0, C_NMW, un, "un")

        a_b, a_k = tmpf()
        E("dve", "tensor_copy", r=["posi"], w=[a_k], out=a_b[:, :T], in_=posi[:, :T])
        ang_b, ang_k = tmpf()
        E("dve", "tensor_scalar", r=[a_k, "cols"], w=[ang_k],
          out=ang_b[:, :T], in0=a_b[:, :T], scalar1=col(C_INVF), scalar2=None, op0=ALU.mult)
        for which in range(2):
            shift = 0.0 if which == 0 else 0.5 * math.pi
            dstb = sinb if which == 0 else cosb
            dk = "sinb" if which == 0 else "cosb"
            E("dve", "tensor_scalar", r=[ang_k], w=["tmpi"],
              out=tmpi[:, :T], in0=ang_b[:, :T], scalar1=shift, scalar2=1.0 / TWO_PI, op0=ALU.add, op1=ALU.mult)
            kf_b, kf_k = tmpf()
            E("dve", "tensor_copy", r=["tmpi"], w=[kf_k], out=kf_b[:, :T], in_=tmpi[:, :T])
            r_b, r_k = tmpf()
            E("dve", "scalar_tensor_tensor", r=[kf_k, ang_k], w=[r_k],
              out=r_b[:, :T], in0=kf_b[:, :T], scalar=-TWO_PI, in1=ang_b[:, :T], op0=ALU.mult, op1=ALU.add)
            E("dve", "tensor_scalar", r=[r_k], w=[r_k],
              out=r_b[:, :T], in0=r_b[:, :T], scalar1=PI_SAFE - shift, scalar2=-PI_SAFE - shift,
              op0=ALU.min, op1=ALU.max)
            E("act", "activation", r=[r_k], w=[dk], out=dstb[:, :T], in_=r_b[:, :T], func=AF.Sin, bias=shift, scale=1.0)


    def finish_sq(ti):
        tok0, T = tiles[ti]
        xb = xs[ti % 2]
        xkey = "x%d" % (ti % 2)
        for c in range(8):
            E("act", "activation", r=[(xkey, c)], w=[("hb", c)],
              out=hb[:, c, :T], in_=xb[:, c, :T], func=AF.Square, scale=1.0 / 32.0)

    def finish_tile(ti):
        tok0, T = tiles[ti]
        xb = xs[ti % 2]
        xkey = "x%d" % (ti % 2)
        xkeys = [(xkey, c) for c in range(8)]
        pb, pk = bank()
        for c in range(8):
            E("pe", "matmul", r=[("hb", c), "cbf"], w=[pk],
              out=pb[:, :T], lhsT=ones_m, rhs=hb[:, c, :T], start=(c == 0), stop=(c == 7))
        tb, tk = tmpf()
        E("act", "activation", r=[pk], w=[tk], out=tb[:, :T], in_=pb[:, :T], func=AF.Ln, bias=EPS, scale=1.0)
        E("act", "activation", r=[tk], w=[("rstd", 2)], out=rstd[2][:, :T], in_=tb[:, :T], func=AF.Exp, scale=-0.5)
        norm_apply(xb, xkey, T, 2, C_NLW, xb, xkey)
        o0 = tok0 - HALO
        E("pool", "dma_start", r=xkeys, dma=s_o[ti % 2], out=OUT3[:, :, o0:o0 + T], in_=xb[:, :, :T])

    load_x(0)
    E("sp", "dma_start", w=["posi"], dma=s_pos0, out=posi[:, :tiles[0][1]], in_=POS[:, tiles[0][0]:tiles[0][0] + tiles[0][1]])
    for _ in range(NSLOT):
        ws_issue()
    prep(0)

    for ti, (tok0, T) in enumerate(tiles):
        NB = T // 128
        gb0 = tok0 // 128
        is_halo = tok0 < HALO
        last_halo = is_halo and (tok0 + T == HALO)
        xb = xs[ti % 2]
        xkey = "x%d" % (ti % 2)
        xkeys = [(xkey, c) for c in range(8)]

        un_r = lambda k, T=T: un[:, k, :T]
        un_k = lambda k: ("un", k)
        cur = {"w": None, "k": None, "n": -1}

        def win_chunk(j):
            u = j // 4
            if cur["w"] is None or cur["n"] != u:
                if cur["w"] is not None:
                    ws_release()
                cur["w"], cur["k"] = ws_acquire()
                cur["n"] = u
            return cur["w"], cur["k"], (j % 4) * 128

        def rope_start(pb, pk, j, dst, dkey):
            qi = rot("qb", 2)
            qf, qfk = tmpf()
            E("act", "activation", r=[pk, "cols"], w=[qfk],
              out=qf[:, :T], in_=pb[:, :T], func=AF.Identity, bias=col(C_BIN + j), scale=1.0)
            E("act", "activation", r=[qfk], w=[("qb", qi)], out=qb[qi][:, :T], in_=qf[:, :T], func=AF.Copy)
            t1, k1 = tmpf()
            E("dve", "tensor_tensor", r=[qfk, "cosb"], w=[k1], out=t1[:, :T], in0=qf[:, :T], in1=cosb[:, :T], op=ALU.mult)
            return (qi, t1, k1, dst, dkey)

        def rope_finish(st):
            qi, t1, k1, dst, dkey = st
            rb, rk = bank()
            E("pe", "matmul", r=[("qb", qi), "cbf"], w=[rk],
              out=rb[:, :T], lhsT=rot_m, rhs=qb[qi][:, :T], start=True, stop=True)
            t2, k2 = tmpf()
            E("dve", "tensor_tensor", r=[rk, "sinb"], w=[k2], out=t2[:, :T], in0=rb[:, :T], in1=sinb[:, :T], op=ALU.mult)
            E("dve", "tensor_tensor", r=[k1, k2], w=[dkey], out=dst, in0=t1[:, :T], in1=t2[:, :T], op=ALU.add)

        if ti > 0 and not tiles[ti - 1][0] < HALO:
            finish_sq(ti - 1)
        pend = None
        for j in range(0, 5):
            wap, wkey, coff = win_chunk(j)
            pb, pk = proj(wap, wkey, 8, coff, un_r, un_k, T)
            if pend is not None:
                rope_finish(pend)
            if j == 0:
                pend = rope_start(pb, pk, 0, kr[:, :T], "kr")
            else:
                pend = rope_start(pb, pk, j, qr[:, j - 1, :T], ("qr", j - 1))
        wap, wkey, coff = win_chunk(5)
        vb, vk = bank()
        for i in range(NB):
            for k in range(8):
                E("pe", "matmul", r=[wkey, ("un", k)], w=[vk],
                  out=vb[:, i * 128:(i + 1) * 128], lhsT=un[:, k, i * 128:(i + 1) * 128],
                  rhs=wap[:, k, coff:coff + 128], start=(k == 0), stop=(k == 7))
        rope_finish(pend)
        for i in range(NB):
            E("dve", "tensor_tensor", r=[vk, "cols"], w=[("vcur", i)],
              out=vcur[:, i, :, 0:64], in0=vb[:, i * 128:(i + 1) * 128].rearrange("p (a b) -> p a b", a=2),
              in1=cols[:, C_BV:C_BV + 128].rearrange("p (a b) -> p a b", a=2), op=ALU.add)
        if ti > 0 and not tiles[ti - 1][0] < HALO:
            finish_tile(ti - 1)

        pts = {}

        def attn_A(i):
            gb = gb0 + i
            for hh in range(2):
                for kb in range(2):
                    if kb == 0:
                        if i == 0:
                            ksrc, kkey = kprev[hh * 64:(hh + 1) * 64, :], "kprev"
                        else:
                            ksrc, kkey = kr[hh * 64:(hh + 1) * 64, (i - 1) * 128:i * 128], "kr"
                        if gb == 0:
                            msk, mkey = mzero[:, :], "mzero"
                        elif gb == HALO // 128:
                            msk, mkey = mfirst, "cbf"
                        else:
                            msk, mkey = mprev, "cbf"
                    else:
                        ksrc, kkey = kr[hh * 64:(hh + 1) * 64, i * 128:(i + 1) * 128], "kr"
                        msk, mkey = mcur, "cbf"
                    sb, sk = bank()
                    E("pe", "matmul", r=[kkey] + [("qr", j) for j in range(4)], w=[sk],
                      out=sb[:, :], lhsT=ksrc, rhs=qr[hh * 64:(hh + 1) * 64, :, i * 128:(i + 1) * 128],
                      start=True, stop=True)
                    pi = rot("pt", NPT)
                    E("act", "activation", r=[sk], w=[("pt", pi)],
                      out=pt[pi][:, :], in_=sb[:, :], func=AF.Exp, scale=HD ** -0.5)
                    E("pool", "tensor_tensor", r=[("pt", pi), mkey], w=[("pt", pi)],
                      out=pt[pi][:, :], in0=pt[pi][:, :], in1=msk, op=ALU.mult)
                    pts[(i, hh, kb)] = pi

        def attn_B(i):
            for hh in range(2):
                ob, ok = bank()
                for kb in range(2):
                    if kb == 0:
                        if i == 0:
                            vsrc, vkey = vprev[:, hh, :], "vprev"
                        else:
                            vsrc, vkey = vcur[:, i - 1, hh, :], ("vcur", i - 1)
                    else:
                        vsrc, vkey = vcur[:, i, hh, :], ("vcur", i)
                    pi = pts[(i, hh, kb)]
                    E("pe", "matmul", r=[vkey, ("pt", pi)], w=[ok],
                      out=ob[:, :], lhsT=vsrc, rhs=pt[pi][:, :], start=(kb == 0), stop=False)
                E("pe", "matmul", r=["cbf", "sinkrow"], w=[ok],
                  out=ob[:, :], lhsT=cbf[0:1, B_SEL:B_SEL + 128], rhs=sinkrow[0:1, hh, :], start=False, stop=True)
                d1, dk1 = tmpf()
                E("act", "activation", r=[ok], w=[dk1], out=d1[0:64, :], in_=ob[64:128, :], func=AF.Ln)
                d2, dk2 = tmpf()
                E("act", "activation", r=[dk1], w=[dk2], out=d2[0:64, :], in_=d1[0:64, :], func=AF.Exp, scale=-1.0)
                E("dve", "tensor_tensor", r=[ok, dk2], w=[("ao", hh, i)],
                  out=ao[hh * 64:(hh + 1) * 64, :, i * 128:(i + 1) * 128],
                  in0=ob[0:64, :].rearrange("p (a b) -> p a b", a=4),
                  in1=d2[0:64, :].rearrange("p (a b) -> p a b", a=4), op=ALU.mult)

        work = [[] for _ in range(NB + 2)]
        for i in range(NB):
            work[i].append((attn_A, i))
            work[i + 2].append((attn_B, i))

        def conv_b_part(c, a2, k2):
            jb = CONV_J[("cb", c)]
            wap, wkey, coff = win_chunk(jb)
            pbb, pbk = proj(wap, wkey, 8, coff, un_r, un_k, T)
            E("dve", "scalar_tensor_tensor", r=[pbk, k2, "cols"], w=[("bc", c)],
              out=bc[:, c, :T], in0=pbb[:, :T], scalar=col(C_BIN + jb), in1=a2[:, :T], op0=ALU.add, op1=ALU.mult)

        pend_b = None
        for c in range(8):
            jc, jx = CONV_J[("cc", c)], CONV_J[("cx", c)]
            wap, wkey, coff = win_chunk(jc)
            pc, pck = proj(wap, wkey, 8, coff, un_r, un_k, T)
            cs, csk = tmpf()
            E("act", "activation", r=[pck, "cols"], w=[csk],
              out=cs[:, :T], in_=pc[:, :T], func=AF.Identity, bias=col(C_BIN + jc), scale=1.0)
            wap, wkey, coff = win_chunk(jx)
            px, pxk = proj(wap, wkey, 8, coff, un_r, un_k, T)
            xi = rot("cxb", 2)
            cx = cxb[xi]
            cxk = ("cxb", xi)
            E("pool", "tensor_copy", r=[("cxcarry", c)], w=[cxk], out=cx[:, PFX - 2:PFX], in_=cxcarry[:, c, :])
            E("dve", "scalar_tensor_tensor", r=[pxk, csk, "cols", cxk], w=[cxk],
              out=cx[:, PFX:PFX + T], in0=px[:, :T], scalar=col(C_BIN + jx), in1=cs[:, :T], op0=ALU.add, op1=ALU.mult)
            if last_halo:
                E("pool", "tensor_scalar", r=[cxk, "cols"], w=[("cxcarry", c)],
                  out=cxcarry[:, c, :], in0=cx[:, PFX + T - 2:PFX + T], scalar1=col(C_FLAG), scalar2=None, op0=ALU.mult)
            else:
                E("pool", "tensor_copy", r=[cxk], w=[("cxcarry", c)], out=cxcarry[:, c, :], in_=cx[:, PFX + T - 2:PFX + T])
            a0, k0 = tmpf()
            E("dve", "tensor_scalar", r=[cxk, "cols"], w=[k0],
              out=a0[:, :T], in0=cx[:, PFX:PFX + T], scalar1=col(C_CVW + 3 * c + 2), scalar2=None, op0=ALU.mult)
            a1, k1 = tmpf()
            E("dve", "scalar_tensor_tensor", r=[cxk, k0, "cols"], w=[k1],
              out=a1[:, :T], in0=cx[:, PFX - 1:PFX - 1 + T], scalar=col(C_CVW + 3 * c + 1), in1=a0[:, :T],
              op0=ALU.mult, op1=ALU.add)
            ci = rot("cva", 2)
            E("dve", "scalar_tensor_tensor", r=[cxk, k1, "cols"], w=[("silu", ci)],
              out=cva[ci][:, :T], in0=cx[:, PFX - 2:PFX - 2 + T], scalar=col(C_CVW + 3 * c), in1=a1[:, :T],
              op0=ALU.mult, op1=ALU.add)
            if pend_b is not None:
                conv_b_part(*pend_b)
            pend_b = (c, cva[ci], ("silu", ci))
            if work:
                for fn, arg in work.pop(0):
                    fn(arg)
        conv_b_part(*pend_b)
        while work:
            for fn, arg in work.pop(0):
                fn(arg)

        for c in range(8):
            for g, dst, dk in ((0, sgc, "sgc"), (1, sga, "sga")):
                j = 30 + 2 * c + g
                wap, wkey, coff = win_chunk(j)
                pg, pgk = proj(wap, wkey, 8, coff, un_r, un_k, T)
                E("act", "activation", r=[pgk, "cols"], w=[(dk, c)],
                  out=dst[:, c, :T], in_=pg[:, :T], func=AF.Sigmoid, bias=col(C_BIN + j), scale=1.0)
        ws_release()
        cur["w"] = None

        E("dve", "tensor_copy", r=["kr"], w=["kprev"], out=kprev[:, :], in_=kr[:, T - 128:T])
        E("dve", "tensor_copy", r=[("vcur", NB - 1)], w=["vprev"], out=vprev[:, :, :], in_=vcur[:, NB - 1, :, :])

        wco0, wco0k = ws_acquire()
        wao_, waok = ws_acquire()
        wco1 = wco1k = None
        aokeys = [("ao", hh, i) for hh in range(2) for i in range(NB)]
        for c in range(8):
            if c == 4:
                ws_release()
                wco1, wco1k = ws_acquire()
            wc, wck = (wco0, wco0k) if c < 4 else (wco1, wco1k)
            yc, yck = proj(wc, wck, 8, (c % 4) * 128, lambda k: bc[:, k, :T], lambda k: ("bc", k), T)
            ya, yak = bank()
            for j in range(4):
                E("pe", "matmul", r=[waok] + aokeys, w=[yak],
                  out=ya[:, :T], lhsT=wao_[:, j, c * 128:(c + 1) * 128], rhs=ao[:, j, :T],
                  start=(j == 0), stop=(j == 3))
            m1, mk1 = tmpf()
            E("dve", "tensor_tensor", r=[yck, ("sgc", c)], w=[mk1],
              out=m1[:, :T], in0=yc[:, :T], in1=sgc[:, c, :T], op=ALU.mult)
            m2, mk2 = tmpf()
            E("dve", "scalar_tensor_tensor", r=[yak, ("sga", c), "cols"], w=[mk2],
              out=m2[:, :T], in0=ya[:, :T], scalar=col(C_BAO + c), in1=sga[:, c, :T], op0=ALU.add, op1=ALU.mult)
            E("dve", "tensor_tensor", r=[mk1, mk2], w=[("mg", c)],
              out=mg[:, c, :T], in0=m1[:, :T], in1=m2[:, :T], op=ALU.add)
        ws_release()
        ws_release()

        wm = wmk = None
        for c in range(8):
            if c % 4 == 0:
                if c == 4:
                    ws_release()
                wm, wmk = ws_acquire()
            pm, pmk = proj(wm, wmk, 8, (c % 4) * 128, lambda k: mg[:, k, :T], lambda k: ("mg", k), T)
            E("dve", "tensor_tensor", r=[pmk, (xkey, c)], w=[(xkey, c)],
              out=xb[:, c, :T], in0=xb[:, c, :T], in1=pm[:, :T], op=ALU.add)
        ws_release()

        if ti + 1 < len(tiles):
            load_x(ti + 1)

        norm_stats(xb, xkey, T, 1)
        norm_apply(xb, xkey, T, 1, C_NFW, un, "un")

        wunits = {}
        pre = {}

        def ffn_up_part(f):
            g = f // 2
            if g not in wunits:
                wunits[g] = ws_acquire()
            wu, wuk = wunits[g]
            lo = (f % 2) * 128
            if ("up", f) in pre:
                pu, puk = pre[("up", f)]
            else:
                pu, puk = proj(wu, wuk, 8, lo, un_r, un_k, T)
            ui = rot("upb", 3)
            ub = upb[ui]
            ubk = ("upb", ui)
            E("pool", "tensor_copy", r=[("upcarry", f)], w=[ubk], out=ub[:, PFX - 2:PFX], in_=upcarry[:, f, :])
            E("act", "activation", r=[puk, ubk], w=[ubk], out=ub[:, PFX:PFX + T], in_=pu[:, :T], func=AF.Copy)
            a0, k0 = tmpf()
            E("act", "activation", r=[puk, "cols"], w=[k0], out=a0[:, :T], in_=pu[:, :T], func=AF.Identity,
              scale=col(C_FCW + 3 * f + 2), bias=col(C_FCB + f))
            if last_halo:
                E("pool", "tensor_scalar", r=[ubk, "cols"], w=[("upcarry", f)],
                  out=upcarry[:, f, :], in0=ub[:, PFX + T - 2:PFX + T], scalar1=col(C_FLAG), scalar2=None, op0=ALU.mult)
            else:
                E("pool", "tensor_copy", r=[ubk], w=[("upcarry", f)], out=upcarry[:, f, :], in_=ub[:, PFX + T - 2:PFX + T])
            a1, k1 = tmpf()
            E("dve", "scalar_tensor_tensor", r=[ubk, k0, "cols"], w=[k1],
              out=a1[:, :T], in0=ub[:, PFX - 1:PFX - 1 + T], scalar=col(C_FCW + 3 * f + 1), in1=a0[:, :T],
              op0=ALU.mult, op1=ALU.add)
            a2, k2 = tmpf()
            E("dve", "scalar_tensor_tensor", r=[ubk, k1, "cols"], w=[k2],
              out=a2[:, :T], in0=ub[:, PFX - 2:PFX - 2 + T], scalar=col(C_FCW + 3 * f), in1=a1[:, :T],
              op0=ALU.mult, op1=ALU.add)
            return (a2, k2)

        def ffn_gate_part(f, a2k):
            a2, k2 = a2k
            si = rot("silu", 2)
            E("act", "activation", r=[k2], w=[("silu", si)], out=silub[si][:, :T], in_=a2[:, :T], func=AF.Silu)
            wu, wuk = wunits[f // 2]
            lo = (f % 2) * 128
            if ("gate", f) in pre:
                pg, pgk = pre[("gate", f)]
            else:
                pg, pgk = proj(wu, wuk, 8, 256 + lo, un_r, un_k, T)
            E("dve", "tensor_tensor", r=[pgk, ("silu", si)], w=[("hb", f)],
              out=hb[:, f, :T], in0=pg[:, :T], in1=silub[si][:, :T], op=ALU.mult)
            if f % 2 == 1:
                ws_release()

        if is_halo:
            for f in range(NFC):
                g = f // 2
                if f % 2 == 0:
                    wu, wuk = ws_acquire()
                pu, puk = proj(wu, wuk, 8, (f % 2) * 128, un_r, un_k, T)
                ui = rot("upb", 3)
                ub = upb[ui]
                ubk = ("upb", ui)
                E("act", "activation", r=[puk], w=[ubk], out=ub[:, PFX:PFX + T], in_=pu[:, :T], func=AF.Copy)
                if last_halo:
                    E("pool", "tensor_scalar", r=[ubk, "cols"], w=[("upcarry", f)],
                      out=upcarry[:, f, :], in0=ub[:, PFX + T - 2:PFX + T], scalar1=col(C_FLAG), scalar2=None,
                      op0=ALU.mult)
                else:
                    E("pool", "tensor_copy", r=[ubk], w=[("upcarry", f)], out=upcarry[:, f, :],
                      in_=ub[:, PFX + T - 2:PFX + T])
                if f % 2 == 1:
                    ws_release()
            if ti + 1 < len(tiles):
                prep(ti + 1)
        else:
            wunits[0] = ws_acquire()
            wu0, wuk0 = wunits[0]
            pbanks = [bank() for _ in range(4)]
            for k in range(8):
                for j in range(4):
                    E("pe", "matmul", r=[wuk0, ("un", k)], w=[pbanks[j][1]],
                      out=pbanks[j][0][:, :T], lhsT=wu0[:, k, j * 128:(j + 1) * 128], rhs=un[:, k, :T],
                      start=(k == 0), stop=(k == 7))
            pre[("up", 0)], pre[("up", 1)], pre[("gate", 0)], pre[("gate", 1)] = pbanks
            prev = None
            for f in range(NFC):
                si = ffn_up_part(f)
                if prev is not None:
                    ffn_gate_part(*prev)
                prev = (f, si)
            ffn_gate_part(*prev)

            for c in range(8):
                wd, wdk = ws_acquire()
                pd, pdk = proj(wd, wdk, NFC, 0, lambda k: hb[:, k, :T], lambda k: ("hb", k), T)
                ws_release()
                if c == 2 and ti + 1 < len(tiles):
                    prep(ti + 1)
                E("dve", "tensor_tensor", r=[pdk, (xkey, c)], w=[(xkey, c)],
                  out=xb[:, c, :T], in0=xb[:, c, :T], in1=pd[:, :T], op=ALU.add)

        if ti + 1 == len(tiles) and not is_halo:
            finish_sq(ti)
            finish_tile(ti)

    E("sp", None, w=[("x0", c) for c in range(8)] + [("x1", c) for c in range(8)]
      + [("ring", i) for i in range(NSLOT)] + ["cbf", "posi", "cols"])
    P.emit()
    P.close()
    return nc, P


def _perm_in():
    cb0, cc0, cx0, q0, k0, v0, gc0, ga0 = 0, 1024, 2048, 3072, 3584, 3712, 3840, 4864
    perm = []
    perm += list(range(k0, k0 + 128))
    for j in range(4):
        perm += list(range(q0 + j * 64, q0 + (j + 1) * 64))
        perm += list(range(q0 + (4 + j) * 64, q0 + (5 + j) * 64))
    perm += list(range(v0, v0 + 128))
    base = {"cc": cc0, "cx": cx0, "cb": cb0}
    for kind, c in CONV_ORDER:
        perm += list(range(base[kind] + c * 128, base[kind] + (c + 1) * 128))
    for c in range(8):
        perm += list(range(gc0 + c * 128, gc0 + (c + 1) * 128))
        perm += list(range(ga0 + c * 128, ga0 + (c + 1) * 128))
    assert len(perm) == INW
    return np.array(perm, dtype=np.int64)


def _pkc(w):
    K, N = w.shape
    return np.ascontiguousarray(w.reshape(K // 128, 128, N).transpose(1, 0, 2))


def _colchunks(v):
    return np.ascontiguousarray(v.reshape(-1, 128).T)


def prepare_inputs(x, positions, norm_mix_w, w_in, b_in, conv_mix_w, w_conv_out, w_attn_out, b_attn_out, sinks,
                   w_mix_out, norm_ffn_w, w_ffn_up, ffn_conv_w, ffn_conv_b, w_ffn_down, norm_final_w):
    f32 = np.float32
    perm = _perm_in()
    win = _pkc(np.asarray(w_in[0], f32)[:, perm])
    wco = _pkc(np.asarray(w_conv_out[0], f32))
    arow = []
    for j in range(4):
        arow += list(range(j * 64, (j + 1) * 64)) + list(range((4 + j) * 64, (5 + j) * 64))
    wao = _pkc(np.asarray(w_attn_out[0], f32)[np.array(arow)])
    wmo = _pkc(np.asarray(w_mix_out[0], f32))
    uperm = []
    for g in range(NFC // 2):
        uperm += list(range(2 * g * 128, (2 * g + 2) * 128))
        uperm += list(range(DFF + 2 * g * 128, DFF + (2 * g + 2) * 128))
    wup = _pkc(np.asarray(w_ffn_up[0], f32)[:, np.array(uperm)])
    wd = np.asarray(w_ffn_down[0], f32).reshape(NFC, 128, 8, 128)
    wdn = np.ascontiguousarray(wd.transpose(1, 2, 0, 3))

    cols = np.zeros((128, NCOLS), f32)
    cols[:, C_NMW:C_NMW + 8] = _colchunks(np.asarray(norm_mix_w[0], f32))
    cols[:, C_NFW:C_NFW + 8] = _colchunks(np.asarray(norm_ffn_w[0], f32))
    cols[:, C_NLW:C_NLW + 8] = _colchunks(np.asarray(norm_final_w, f32))
    cols[:, C_BIN:C_BIN + 46] = _colchunks(np.asarray(b_in[0], f32)[perm])
    cols[:, C_BAO:C_BAO + 8] = _colchunks(np.asarray(b_attn_out[0], f32))
    cw = np.asarray(conv_mix_w[0], f32)
    cols[:, C_CVW:C_CVW + 24] = cw.reshape(3, 8, 128).transpose(2, 1, 0).reshape(128, 24)
    fw = np.asarray(ffn_conv_w[0], f32)
    cols[:, C_FCW:C_FCW + 66] = fw.reshape(3, NFC, 128).transpose(2, 1, 0).reshape(128, 66)
    cols[:, C_FCB:C_FCB + NFC] = _colchunks(np.asarray(ffn_conv_b[0], f32))
    inv_freq = (10000.0 ** (-np.arange(0, HD, 2, dtype=f32) / HD)).astype(f32)
    cols[:, C_INVF] = inv_freq[np.arange(128) % 32]
    cols[:, C_SINK:C_SINK + 8] = np.asarray(sinks[0], f32)[None, :]
    cols[:, C_BV:C_BV + 128] = np.asarray(b_in[0], f32)[3712:3840][None, :]

    cbf = np.zeros((128, NBCOLS), f32)
    cbf[:, B_ONES:B_ONES + 128] = 1.0
    rm = np.zeros((128, 128), f32)
    for m in range(128):
        if m % 64 < 32:
            rm[m + 32, m] = -1.0
        else:
            rm[m - 32, m] = 1.0
    cbf[:, B_ROT:B_ROT + 128] = rm
    kk = np.arange(128)[:, None]
    qq = np.arange(128)[None, :]
    mp = (kk > qq).astype(f32)
    mc = (kk <= qq).astype(f32)
    cbf[:, B_MPREV:B_MPREV + 512] = np.tile(mp, (1, 4))
    cbf[:, B_MCUR:B_MCUR + 512] = np.tile(mc, (1, 4))
    cbf[0, B_SEL + 64:B_SEL + 128] = 1.0

    x = np.asarray(x, f32)
    positions = np.asarray(positions, np.int32)
    in_maps = []
    for core in range(NCORE):
        b, half = core // 2, core % 2
        t0 = half * TOK
        xT = np.zeros((D, NTOK), f32)
        xT[:, HALO:] = x[b, t0:t0 + TOK, :].T
        pos = np.zeros((NTOK,), np.int32)
        pos[HALO:] = positions[b, t0:t0 + TOK]
        ccols = cols.copy()
        ccbf = cbf.copy()
        if half == 1:
            xT[:, :HALO] = x[b, t0 - HALO:t0, :].T
            pos[:HALO] = positions[b, t0 - HALO:t0]
            ccols[:, C_FLAG] = 1.0
            ccbf[:, B_MFIRST:B_MFIRST + 512] = np.tile(mp, (1, 4))
        in_maps.append({
            "xT": xT, "posb": np.ascontiguousarray(np.broadcast_to(pos[None, :], (128, NTOK))),
            "cols": ccols, "cbf": ccbf, "win": win, "wco": wco, "wao": wao, "wmo": wmo, "wup": wup, "wdn": wdn,
        })
    return in_maps


def kernel(**inputs):
    in_maps = prepare_inputs(**inputs)
    nc, _ = build_program()
    res = run_bass_kernel_spmd(nc, in_maps, core_ids=list(range(NCORE)))
    out = np.empty((BATCH, SEQ, D), np.float32)
    for core in range(NCORE):
        b, half = core // 2, core % 2
        out[b, half * TOK:(half + 1) * TOK, :] = res.results[core]["outT"].T
    return out
```

```python
from contextlib import ExitStack
import math
import os
import numpy as np
import concourse.bass as bass
import concourse.mybir as mybir
from concourse.bass_utils import run_bass_kernel_spmd

F32 = mybir.dt.float32
BF16 = mybir.dt.bfloat16
I32 = mybir.dt.int32
AF = mybir.ActivationFunctionType
ALU = mybir.AluOpType

D = 1024
SEQ = 8192
BATCH = 4
DFF = 2816
NQ = 8
NKV = 2
HD = 64
EPS = 1e-5
NCORE = 8
TOK = 4096
HALO = 256
NTOK = TOK + HALO
INW = 5888
NFC = DFF // 128
TILES = [(0, 256)] + [(256 + 512 * i, 512) for i in range(8)]
NSLOT = 4
NWBG = 6
PFX = 8
NBANK = int(os.environ.get('KNBANK', 8))
SLOT_ELEMS = 4096
TWO_PI = 2.0 * math.pi
PI_SAFE = 3.1415925

CONV_ORDER = [("cc", 0), ("cx", 0)]
for _c in range(1, 8):
    CONV_ORDER += [("cc", _c), ("cx", _c), ("cb", _c - 1)]
CONV_ORDER += [("cb", 7)]
CONV_J = {k: 6 + i for i, k in enumerate(CONV_ORDER)}

C_NMW = 0
C_NFW = 8
C_NLW = 16
C_BIN = 24
C_BAO = 70
C_CVW = 78
C_FCW = 102
C_FCB = 168
C_INVF = 190
C_FLAG = 191
C_SINK = 192
C_BV = 200
NCOLS = 328

B_ONES = 0
B_ROT = 128
B_MPREV = 256
B_MCUR = 768
B_MFIRST = 1280
B_SEL = 1792
NBCOLS = 1920


ENGS = ("pe", "act", "dve", "pool", "sp")


class Op:
    __slots__ = ("eng", "fn", "deps", "signal", "sig_idx", "dma_sem", "dma_cnt", "pos")

    def __init__(self, eng, fn):
        self.eng = eng
        self.fn = fn
        self.deps = []
        self.signal = False
        self.sig_idx = 0
        self.dma_sem = None
        self.dma_cnt = 0


class Prog:
    def __init__(self, nc):
        self.nc = nc
        self.es = ExitStack()
        self.ops = {e: [] for e in ENGS}
        self.state = {}
        self.dma_counts = {}
        self.sems = {}

    def sbuf(self, name, shape, dtype):
        return self.es.enter_context(self.nc.sbuf_tensor("sb_" + name, list(shape), dtype))

    def psum(self, name, shape, dtype):
        return self.es.enter_context(self.nc.psum_tensor(name, list(shape), dtype))

    def sem(self, name):
        s = self.es.enter_context(self.nc.semaphore("sm_" + name))
        self.sems[name] = s
        self.dma_counts[name] = 0
        return name

    def op(self, eng, fn, r=(), w=(), dma=None):
        self.nrec = getattr(self, "nrec", 0) + 1
        if self.nrec > getattr(self, "limit", 10 ** 9) and fn[0] is not None:
            return None
        o = Op(eng, fn)
        deps = {}
        for k in r:
            st = self.state.get(k)
            if st is not None and st[0] is not None:
                deps[id(st[0])] = st[0]
        for k in w:
            st = self.state.get(k)
            if st is None:
                continue
            if st[0] is not None and (st[0].eng != eng or st[0].dma_sem is not None or eng != "pe"):
                deps[id(st[0])] = st[0]
            for rd in st[1]:
                if rd.eng != eng or rd.dma_sem is not None or eng != "pe":
                    deps[id(rd)] = rd
        for k in r:
            st = self.state.setdefault(k, [None, []])
            st[1].append(o)
        for k in w:
            self.state[k] = [o, []]
        best = {}
        for d in deps.values():
            if d.dma_sem is not None:
                key = ("d", d.dma_sem)
                if key not in best or d.dma_cnt > best[key].dma_cnt:
                    best[key] = d
            else:
                key = ("e", d.eng)
                if key not in best or d.pos > best[key].pos:
                    best[key] = d
        for d in best.values():
            if d.dma_sem is None:
                d.signal = True
        o.deps = list(best.values())
        o.pos = len(self.ops[eng])
        if dma is not None:
            o.dma_sem = dma
            self.dma_counts[dma] += 16
            o.dma_cnt = self.dma_counts[dma]
        self.ops[eng].append(o)
        return o

    def emit(self):
        nc = self.nc
        engsem = {e: self.es.enter_context(nc.semaphore("es_" + e)) for e in ENGS}
        for e in ENGS:
            c = 0
            for o in self.ops[e]:
                if o.signal and o.dma_sem is None:
                    c += 1
                    o.sig_idx = c
        block = self.es.enter_context(nc.Block())
        stats = {}

        def run(e, handle):
            waited = {}
            nw = 0
            for o in self.ops[e]:
                for d in o.deps:
                    if d.dma_sem is not None:
                        key, sem, val = "d_" + d.dma_sem, self.sems[d.dma_sem], d.dma_cnt
                    else:
                        key, sem, val = "e_" + d.eng, engsem[d.eng], d.sig_idx
                    if waited.get(key, 0) >= val:
                        continue
                    waited[key] = val
                    handle.wait_ge(sem, val)
                    nw += 1
                meth, kw = o.fn
                ins = getattr(handle, meth)(**kw) if meth is not None else None
                if o.dma_sem is not None:
                    ins.then_inc(self.sems[o.dma_sem], 16)
                elif o.signal:
                    ins.then_inc(engsem[e], 1)
            stats[e] = (len(self.ops[e]), nw)

        @block.sync
        def _(h):
            run("sp", h)

        @block.tensor
        def _(h):
            run("pe", h)

        @block.scalar
        def _(h):
            run("act", h)

        @block.vector
        def _(h):
            run("dve", h)

        @block.gpsimd
        def _(h):
            run("pool", h)

        self.stats = stats

    def close(self):
        self.es.close()


def build_program(tiles=TILES):
    nc = bass.Bass("TRN2", target_bir_lowering=False)
    XT = nc.dram_tensor("xT", [D, NTOK], F32, kind="ExternalInput").ap()
    POS = nc.dram_tensor("posb", [128, NTOK], I32, kind="ExternalInput").ap()
    COLS = nc.dram_tensor("cols", [128, NCOLS], F32, kind="ExternalInput").ap()
    CBF = nc.dram_tensor("cbf", [128, NBCOLS], F32, kind="ExternalInput").ap()
    WIN = nc.dram_tensor("win", [128, 8, INW], F32, kind="ExternalInput").ap()
    WCO = nc.dram_tensor("wco", [128, 8, D], F32, kind="ExternalInput").ap()
    WAO = nc.dram_tensor("wao", [128, 4, D], F32, kind="ExternalInput").ap()
    WMO = nc.dram_tensor("wmo", [128, 8, D], F32, kind="ExternalInput").ap()
    WUP = nc.dram_tensor("wup", [128, 8, 2 * DFF], F32, kind="ExternalInput").ap()
    WDN = nc.dram_tensor("wdn", [128, 8, NFC, 128], F32, kind="ExternalInput").ap()
    OUT = nc.dram_tensor("outT", [D, TOK], F32, kind="ExternalOutput").ap()
    NUNIT = 36
    SCR = nc.dram_tensor("wscr", [NUNIT, 128, SLOT_ELEMS], BF16, kind="Internal").ap()
    XT3 = XT.rearrange("(c p) t -> p c t", p=128)
    OUT3 = OUT.rearrange("(c p) t -> p c t", p=128)

    P = Prog(nc)
    P.limit = int(os.environ.get("KLIMIT", 10 ** 9))
    P.nrec = 0
    bank_touch = [0] * 8

    def E(eng, meth, r=(), w=(), dma=None, **kw):
        for k in list(r) + list(w):
            if isinstance(k, tuple) and k[0] == "ps":
                bank_touch[k[1]] = P.nrec + 1
        return P.op(eng, (meth, kw), r=r, w=w, dma=dma)

    TM = 512
    xs = [P.sbuf(f"xs{i}", [128, 8, TM], F32) for i in range(2)]
    cols = P.sbuf("cols", [128, NCOLS], F32)
    cbf = P.sbuf("cbf", [128, NBCOLS], BF16)
    ring = [P.sbuf(f"ring{i}", [128, SLOT_ELEMS], BF16) for i in range(NSLOT)]
    sq = [P.sbuf(f"sq{i}", [128, TM], BF16) for i in range(2)]
    rstd = [P.sbuf(f"rstd{i}", [128, TM], F32) for i in range(3)]
    un = P.sbuf("un", [128, 8, TM], BF16)
    NTMP = 8
    tmp = [P.sbuf(f"tmp{i}", [128, TM], F32) for i in range(NTMP)]
    tmpi = P.sbuf("tmpi", [128, TM], I32)
    cosb = P.sbuf("cosb", [128, TM], F32)
    sinb = P.sbuf("sinb", [128, TM], F32)
    posi = P.sbuf("posi", [128, TM], I32)
    qb = [P.sbuf(f"qb{i}", [128, TM], BF16) for i in range(2)]
    qr = P.sbuf("qr", [128, 4, TM], BF16)
    kr = P.sbuf("kr", [128, TM], BF16)
    kprev = P.sbuf("kprev", [128, 128], BF16)
    vcur = P.sbuf("vcur", [128, 4, 2, 128], BF16)
    vprev = P.sbuf("vprev", [128, 2, 128], BF16)
    cxb = [P.sbuf(f"cxb{i}", [128, TM + PFX], F32) for i in range(2)]
    cxcarry = P.sbuf("cxcarry", [128, 8, 2], F32)
    bc = P.sbuf("bc", [128, 8, TM], BF16)
    sgc = P.sbuf("sgc", [128, 8, TM], BF16)
    sga = P.sbuf("sga", [128, 8, TM], BF16)
    NPT = 12
    pt = [P.sbuf(f"pt{i}", [128, TM], BF16) for i in range(NPT)]
    mzero = P.sbuf("mzero", [128, TM], BF16)
    sinkrow = P.sbuf("sinkrow", [1, 2, TM], BF16)
    sinktmp = P.sbuf("sinktmp", [128, 8], F32)
    zf = P.sbuf("zf", [128, 128], F32)
    ao = P.sbuf("ao", [128, 4, TM], BF16)
    mg = P.sbuf("mg", [128, 8, TM], BF16)
    upb = [P.sbuf(f"upb{i}", [128, TM + PFX], F32) for i in range(3)]
    upcarry = P.sbuf("upcarry", [128, NFC, 2], F32)
    silub = [P.sbuf(f"silub{i}", [128, TM], F32) for i in range(2)]
    cva = silub
    hb = P.sbuf("hb", [128, NFC, TM], BF16)
    banks = [P.psum(f"ps{i}", [128, TM], F32) for i in range(8)]

    s_ring = [P.sem(f"ring{i}") for i in range(NSLOT)]
    s_wb = [P.sem(f"wb{i}") for i in range(NSLOT)]
    s_ld = [P.sem(f"ld{i}") for i in range(NSLOT)]
    s_x = [P.sem(f"x{i}") for i in range(2)]
    s_o = [P.sem(f"o{i}") for i in range(2)]
    s_pos = P.sem("pos")
    s_pos0 = P.sem("pos0")
    s_x0 = P.sem("x0first")
    s_cols = P.sem("cols")
    s_cbf = P.sem("cbf")

    ctr = {"bank": 0, "tmp": 0, "sq": 0, "qb": 0, "pt": 0, "cxb": 0, "upb": 0, "silu": 0, "cva": 0}

    def rot(name, n):
        i = ctr[name]
        ctr[name] = (i + 1) % n
        return i

    def bank():
        b = min(range(NBANK), key=lambda i: bank_touch[i])
        bank_touch[b] = P.nrec + 1
        return banks[b], ("ps", b)

    def tmpf():
        i = rot("tmp", NTMP)
        return tmp[i], ("tmp", i)

    units = []
    NHALO_UNITS = 0
    for _ti, (_tok0, _T) in enumerate(tiles):
        _halo = _tok0 < HALO
        for u in range(12):
            n = min(512, INW - u * 512)
            units.append((WIN[:, :, u * 512:u * 512 + n], (8, n)))
        units.append((WCO[:, :, 0:512], (8, 512)))
        units.append((WAO[:, :, :], (4, 1024)))
        units.append((WCO[:, :, 512:1024], (8, 512)))
        units.append((WMO[:, :, 0:512], (8, 512)))
        units.append((WMO[:, :, 512:1024], (8, 512)))
        for u in range(11):
            if _halo:
                units.append((WUP[:, :, u * 512:u * 512 + 256], (8, 256)))
            else:
                units.append((WUP[:, :, u * 512:(u + 1) * 512], (8, 512)))
        if not _halo:
            for c in range(8):
                units.append((WDN[:, c, :, :], (NFC, 128)))
        if _halo:
            NHALO_UNITS = len(units)
    ws = {"issued": 0, "acq": 0}

    def ws_issue():
        n = ws["issued"]
        if n >= len(units):
            return
        src, (a, b) = units[n]
        slot = n % NSLOT
        dst = ring[slot][:, 0:a * b].rearrange("p (a b) -> p a b", a=a)
        if n < NHALO_UNITS:
            t, u, g = 0, n, 0
        else:
            t, u = 1 + (n - NHALO_UNITS) // NUNIT, (n - NHALO_UNITS) % NUNIT
            g = u % NWBG
        if t >= g + 2:
            E("sp", "dma_start", r=[("scr", u)], w=[("ring", slot)], dma=s_ld[slot],
              out=ring[slot][:, 0:a * b], in_=SCR[u, :, 0:a * b])
        else:
            E("pool", "dma_start", w=[("ring", slot)], dma=s_ring[slot], out=dst, in_=src)
            if t == g + 1:
                E("sp", "dma_start", r=[("ring", slot)], w=[("scr", u)], dma=s_wb[slot],
                  out=SCR[u, :, 0:a * b], in_=ring[slot][:, 0:a * b])
        ws["issued"] = n + 1

    def ws_acquire():
        n = ws["acq"]
        assert n < ws["issued"], "weight ring underflow"
        ws["acq"] = n + 1
        _, (a, b) = units[n]
        slot = n % NSLOT
        return ring[slot][:, 0:a * b].rearrange("p (a b) -> p a b", a=a), ("ring", slot)

    def ws_release():
        ws_issue()

    E("sp", "dma_start", w=["cols"], dma=s_cols, out=cols[:], in_=COLS)
    E("pool", "dma_start", w=["cbf"], dma=s_cbf, out=cbf[:], in_=CBF)
    ones_m = cbf[:, B_ONES:B_ONES + 128]
    rot_m = cbf[:, B_ROT:B_ROT + 128]
    mprev = cbf[:, B_MPREV:B_MPREV + 512]
    mcur = cbf[:, B_MCUR:B_MCUR + 512]
    mfirst = cbf[:, B_MFIRST:B_MFIRST + 512]

    def col(i):
        return cols[:, i:i + 1]

    E("dve", "memset", w=["mzero"], ap=mzero[:], constant=0.0)
    E("dve", "memset", w=["kprev"], ap=kprev[:], constant=0.0)
    E("dve", "memset", w=["vprev"], ap=vprev[:], constant=0.0)
    E("dve", "memset", w=["vcur"], ap=vcur[:], constant=1.0)
    E("dve", "memset", w=["cxcarry"], ap=cxcarry[:], constant=0.0)
    E("dve", "memset", w=["upcarry"], ap=upcarry[:], constant=0.0)
    E("dve", "memset", w=["zf"], ap=zf[:], constant=0.0)
    E("act", "activation", r=["cols"], w=["sinktmp"], out=sinktmp[:], in_=cols[:, C_SINK:C_SINK + 8], func=AF.Exp)
    for hh in range(2):
        for j in range(4):
            E("dve", "tensor_scalar", r=["zf", "sinktmp"], w=["sinkrow"],
              out=sinkrow[0:1, hh, j * 128:(j + 1) * 128], in0=zf[0:1, :],
              scalar1=sinktmp[0:1, hh * 4 + j:hh * 4 + j + 1], scalar2=None, op0=ALU.add)

    def norm_stats(xb, xkey, T, ridx):
        pb, pk = bank()
        for c in range(8):
            si = rot("sq", 2)
            E("act", "activation", r=[(xkey, c)], w=[("sq", si)],
              out=sq[si][:, :T], in_=xb[:, c, :T], func=AF.Square, scale=1.0 / 32.0)
            E("pe", "matmul", r=[("sq", si), "cbf"], w=[pk],
              out=pb[:, :T], lhsT=ones_m, rhs=sq[si][:, :T], start=(c == 0), stop=(c == 7))
        tb, tk = tmpf()
        E("act", "activation", r=[pk], w=[tk], out=tb[:, :T], in_=pb[:, :T], func=AF.Ln, bias=EPS, scale=1.0)
        E("act", "activation", r=[tk], w=[("rstd", ridx)], out=rstd[ridx][:, :T], in_=tb[:, :T], func=AF.Exp, scale=-0.5)

    def norm_apply(xb, xkey, T, ridx, wc0, dst, dkey):
        for c in range(8):
            E("dve", "scalar_tensor_tensor", r=[(xkey, c), ("rstd", ridx), "cols"], w=[(dkey, c)],
              out=dst[:, c, :T], in0=xb[:, c, :T], scalar=col(wc0 + c), in1=rstd[ridx][:, :T],
              op0=ALU.mult, op1=ALU.mult)

    def proj(wap, wkey, nk, coff, rhs_fn, rkeys, T):
        pb, pk = bank()
        for k in range(nk):
            E("pe", "matmul", r=[wkey, rkeys(k)], w=[pk],
              out=pb[:, :T], lhsT=wap[:, k, coff:coff + 128], rhs=rhs_fn(k), start=(k == 0), stop=(k == nk - 1))
        return pb, pk

    def load_x(ti):
        tok0, T = tiles[ti]
        xb = xs[ti % 2]
        xkey = "x%d" % (ti % 2)
        E("sp" if ti == 0 else "pool", "dma_start", w=[(xkey, c) for c in range(8)], dma=s_x0 if ti == 0 else s_x[ti % 2],
          out=xb[:, :, :T], in_=XT3[:, :, tok0:tok0 + T])

    def prep(ti):
        tok0, T = tiles[ti]
        xb = xs[ti % 2]
        xkey = "x%d" % (ti % 2)
        if ti > 0:
            E("pool", "dma_start", w=["posi"], dma=s_pos, out=posi[:, :T], in_=POS[:, tok0:tok0 + T])

        norm_stats(xb, xkey, T, 0)
        norm_apply(xb, xkey, T, 0, C_NMW, un, "un")

        a_b, a_k = tmpf()
        E("dve", "tensor_copy", r=["posi"], w=[a_k], out=a_b[:, :T], in_=posi[:, :T])
        ang_b, ang_k = tmpf()
        E("dve", "tensor_scalar", r=[a_k, "cols"], w=[ang_k],
          out=ang_b[:, :T], in0=a_b[:, :T], scalar1=col(C_INVF), scalar2=None, op0=ALU.mult)
        for which in range(2):
            shift = 0.0 if which == 0 else 0.5 * math.pi
            dstb = sinb if which == 0 else cosb
            dk = "sinb" if which == 0 else "cosb"
            E("dve", "tensor_scalar", r=[ang_k], w=["tmpi"],
              out=tmpi[:, :T], in0=ang_b[:, :T], scalar1=shift, scalar2=1.0 / TWO_PI, op0=ALU.add, op1=ALU.mult)
            kf_b, kf_k = tmpf()
            E("dve", "tensor_copy", r=["tmpi"], w=[kf_k], out=kf_b[:, :T], in_=tmpi[:, :T])
            r_b, r_k = tmpf()
            E("dve", "scalar_tensor_tensor", r=[kf_k, ang_k], w=[r_k],
              out=r_b[:, :T], in0=kf_b[:, :T], scalar=-TWO_PI, in1=ang_b[:, :T], op0=ALU.mult, op1=ALU.add)
            E("dve", "tensor_scalar", r=[r_k], w=[r_k],
              out=r_b[:, :T], in0=r_b[:, :T], scalar1=PI_SAFE - shift, scalar2=-PI_SAFE - shift,
              op0=ALU.min, op1=ALU.max)
            E("act", "activation", r=[r_k], w=[dk], out=dstb[:, :T], in_=r_b[:, :T], func=AF.Sin, bias=shift, scale=1.0)


    def finish_sq(ti):
        tok0, T = tiles[ti]
        xb = xs[ti % 2]
        xkey = "x%d" % (ti % 2)
        for c in range(8):
            E("act", "activation", r=[(xkey, c)], w=[("hb", c)],
              out=hb[:, c, :T], in_=xb[:, c, :T], func=AF.Square, scale=1.0 / 32.0)

    def finish_tile(ti):
        tok0, T = tiles[ti]
        xb = xs[ti % 2]
        xkey = "x%d" % (ti % 2)
        xkeys = [(xkey, c) for c in range(8)]
        pb, pk = bank()
        for c in range(8):
            E("pe", "matmul", r=[("hb", c), "cbf"], w=[pk],
              out=pb[:, :T], lhsT=ones_m, rhs=hb[:, c, :T], start=(c == 0), stop=(c == 7))
        tb, tk = tmpf()
        E("act", "activation", r=[pk], w=[tk], out=tb[:, :T], in_=pb[:, :T], func=AF.Ln, bias=EPS, scale=1.0)
        E("act", "activation", r=[tk], w=[("rstd", 2)], out=rstd[2][:, :T], in_=tb[:, :T], func=AF.Exp, scale=-0.5)
        norm_apply(xb, xkey, T, 2, C_NLW, xb, xkey)
        o0 = tok0 - HALO
        E("pool", "dma_start", r=xkeys, dma=s_o[ti % 2], out=OUT3[:, :, o0:o0 + T], in_=xb[:, :, :T])

    load_x(0)
    E("sp", "dma_start", w=["posi"], dma=s_pos0, out=posi[:, :tiles[0][1]], in_=POS[:, tiles[0][0]:tiles[0][0] + tiles[0][1]])
    for _ in range(NSLOT):
        ws_issue()
    prep(0)

    for ti, (tok0, T) in enumerate(tiles):
        NB = T // 128
        gb0 = tok0 // 128
        is_halo = tok0 < HALO
        last_halo = is_halo and (tok0 + T == HALO)
        xb = xs[ti % 2]
        xkey = "x%d" % (ti % 2)
        xkeys = [(xkey, c) for c in range(8)]

        un_r = lambda k, T=T: un[:, k, :T]
        un_k = lambda k: ("un", k)
        cur = {"w": None, "k": None, "n": -1}

        def win_chunk(j):
            u = j // 4
            if cur["w"] is None or cur["n"] != u:
                if cur["w"] is not None:
                    ws_release()
                cur["w"], cur["k"] = ws_acquire()
                cur["n"] = u
            return cur["w"], cur["k"], (j % 4) * 128

        def rope_start(pb, pk, j, dst, dkey):
            qi = rot("qb", 2)
            qf, qfk = tmpf()
            E("act", "activation", r=[pk, "cols"], w=[qfk],
              out=qf[:, :T], in_=pb[:, :T], func=AF.Identity, bias=col(C_BIN + j), scale=1.0)
            E("act", "activation", r=[qfk], w=[("qb", qi)], out=qb[qi][:, :T], in_=qf[:, :T], func=AF.Copy)
            t1, k1 = tmpf()
            E("dve", "tensor_tensor", r=[qfk, "cosb"], w=[k1], out=t1[:, :T], in0=qf[:, :T], in1=cosb[:, :T], op=ALU.mult)
            return (qi, t1, k1, dst, dkey)

        def rope_finish(st):
            qi, t1, k1, dst, dkey = st
            rb, rk = bank()
            E("pe", "matmul", r=[("qb", qi), "cbf"], w=[rk],
              out=rb[:, :T], lhsT=rot_m, rhs=qb[qi][:, :T], start=True, stop=True)
            t2, k2 = tmpf()
            E("dve", "tensor_tensor", r=[rk, "sinb"], w=[k2], out=t2[:, :T], in0=rb[:, :T], in1=sinb[:, :T], op=ALU.mult)
            E("dve", "tensor_tensor", r=[k1, k2], w=[dkey], out=dst, in0=t1[:, :T], in1=t2[:, :T], op=ALU.add)

        if ti > 0 and not tiles[ti - 1][0] < HALO:
            finish_sq(ti - 1)
        pend = None
        for j in range(0, 5):
            wap, wkey, coff = win_chunk(j)
            pb, pk = proj(wap, wkey, 8, coff, un_r, un_k, T)
            if pend is not None:
                rope_finish(pend)
            if j == 0:
                pend = rope_start(pb, pk, 0, kr[:, :T], "kr")
            else:
                pend = rope_start(pb, pk, j, qr[:, j - 1, :T], ("qr", j - 1))
        wap, wkey, coff = win_chunk(5)
        vb, vk = bank()
        for i in range(NB):
            for k in range(8):
                E("pe", "matmul", r=[wkey, ("un", k)], w=[vk],
                  out=vb[:, i * 128:(i + 1) * 128], lhsT=un[:, k, i * 128:(i + 1) * 128],
                  rhs=wap[:, k, coff:coff + 128], start=(k == 0), stop=(k == 7))
        rope_finish(pend)
        for i in range(NB):
            E("dve", "tensor_tensor", r=[vk, "cols"], w=[("vcur", i)],
              out=vcur[:, i, :, 0:64], in0=vb[:, i * 128:(i + 1) * 128].rearrange("p (a b) -> p a b", a=2),
              in1=cols[:, C_BV:C_BV + 128].rearrange("p (a b) -> p a b", a=2), op=ALU.add)
        if ti > 0 and not tiles[ti - 1][0] < HALO:
            finish_tile(ti - 1)

        pts = {}

        def attn_A(i):
            gb = gb0 + i
            for hh in range(2):
                for kb in range(2):
                    if kb == 0:
                        if i == 0:
                            ksrc, kkey = kprev[hh * 64:(hh + 1) * 64, :], "kprev"
                        else:
                            ksrc, kkey = kr[hh * 64:(hh + 1) * 64, (i - 1) * 128:i * 128], "kr"
                        if gb == 0:
                            msk, mkey = mzero[:, :], "mzero"
                        elif gb == HALO // 128:
                            msk, mkey = mfirst, "cbf"
                        else:
                            msk, mkey = mprev, "cbf"
                    else:
                        ksrc, kkey = kr[hh * 64:(hh + 1) * 64, i * 128:(i + 1) * 128], "kr"
                        msk, mkey = mcur, "cbf"
                    sb, sk = bank()
                    E("pe", "matmul", r=[kkey] + [("qr", j) for j in range(4)], w=[sk],
                      out=sb[:, :], lhsT=ksrc, rhs=qr[hh * 64:(hh + 1) * 64, :, i * 128:(i + 1) * 128],
                      start=True, stop=True)
                    pi = rot("pt", NPT)
                    E("act", "activation", r=[sk], w=[("pt", pi)],
                      out=pt[pi][:, :], in_=sb[:, :], func=AF.Exp, scale=HD ** -0.5)
                    E("dve", "tensor_tensor", r=[("pt", pi), mkey], w=[("pt", pi)],
                      out=pt[pi][:, :], in0=pt[pi][:, :], in1=msk, op=ALU.mult)
                    pts[(i, hh, kb)] = pi

        def attn_B(i):
            for hh in range(2):
                ob, ok = bank()
                for kb in range(2):
                    if kb == 0:
                        if i == 0:
                            vsrc, vkey = vprev[:, hh, :], "vprev"
                        else:
                            vsrc, vkey = vcur[:, i - 1, hh, :], ("vcur", i - 1)
                    else:
                        vsrc, vkey = vcur[:, i, hh, :], ("vcur", i)
                    pi = pts[(i, hh, kb)]
                    E("pe", "matmul", r=[vkey, ("pt", pi)], w=[ok],
                      out=ob[:, :], lhsT=vsrc, rhs=pt[pi][:, :], start=(kb == 0), stop=False)
                E("pe", "matmul", r=["cbf", "sinkrow"], w=[ok],
                  out=ob[:, :], lhsT=cbf[0:1, B_SEL:B_SEL + 128], rhs=sinkrow[0:1, hh, :], start=False, stop=True)
                d1, dk1 = tmpf()
                E("act", "activation", r=[ok], w=[dk1], out=d1[0:64, :], in_=ob[64:128, :], func=AF.Ln)
                d2, dk2 = tmpf()
                E("act", "activation", r=[dk1], w=[dk2], out=d2[0:64, :], in_=d1[0:64, :], func=AF.Exp, scale=-1.0)
                E("dve", "tensor_tensor", r=[ok, dk2], w=[("ao", hh, i)],
                  out=ao[hh * 64:(hh + 1) * 64, :, i * 128:(i + 1) * 128],
                  in0=ob[0:64, :].rearrange("p (a b) -> p a b", a=4),
                  in1=d2[0:64, :].rearrange("p (a b) -> p a b", a=4), op=ALU.mult)

        work = [[] for _ in range(NB + 2)]
        for i in range(NB):
            work[i].append((attn_A, i))
            work[i + 2].append((attn_B, i))

        def conv_b_part(c, a2, k2):
            jb = CONV_J[("cb", c)]
            wap, wkey, coff = win_chunk(jb)
            pbb, pbk = proj(wap, wkey, 8, coff, un_r, un_k, T)
            E("dve", "scalar_tensor_tensor", r=[pbk, k2, "cols"], w=[("bc", c)],
              out=bc[:, c, :T], in0=pbb[:, :T], scalar=col(C_BIN + jb), in1=a2[:, :T], op0=ALU.add, op1=ALU.mult)

        pend_b = None
        for c in range(8):
            jc, jx = CONV_J[("cc", c)], CONV_J[("cx", c)]
            wap, wkey, coff = win_chunk(jc)
            pc, pck = proj(wap, wkey, 8, coff, un_r, un_k, T)
            cs, csk = tmpf()
            E("act", "activation", r=[pck, "cols"], w=[csk],
              out=cs[:, :T], in_=pc[:, :T], func=AF.Identity, bias=col(C_BIN + jc), scale=1.0)
            wap, wkey, coff = win_chunk(jx)
            px, pxk = proj(wap, wkey, 8, coff, un_r, un_k, T)
            xi = rot("cxb", 2)
            cx = cxb[xi]
            cxk = ("cxb", xi)
            E("pool", "tensor_copy", r=[("cxcarry", c)], w=[cxk], out=cx[:, PFX - 2:PFX], in_=cxcarry[:, c, :])
            E("dve", "scalar_tensor_tensor", r=[pxk, csk, "cols", cxk], w=[cxk],
              out=cx[:, PFX:PFX + T], in0=px[:, :T], scalar=col(C_BIN + jx), in1=cs[:, :T], op0=ALU.add, op1=ALU.mult)
            if last_halo:
                E("pool", "tensor_scalar", r=[cxk, "cols"], w=[("cxcarry", c)],
                  out=cxcarry[:, c, :], in0=cx[:, PFX + T - 2:PFX + T], scalar1=col(C_FLAG), scalar2=None, op0=ALU.mult)
            else:
                E("pool", "tensor_copy", r=[cxk], w=[("cxcarry", c)], out=cxcarry[:, c, :], in_=cx[:, PFX + T - 2:PFX + T])
            a0, k0 = tmpf()
            E("dve", "tensor_scalar", r=[cxk, "cols"], w=[k0],
              out=a0[:, :T], in0=cx[:, PFX:PFX + T], scalar1=col(C_CVW + 3 * c + 2), scalar2=None, op0=ALU.mult)
            a1, k1 = tmpf()
            E("dve", "scalar_tensor_tensor", r=[cxk, k0, "cols"], w=[k1],
              out=a1[:, :T], in0=cx[:, PFX - 1:PFX - 1 + T], scalar=col(C_CVW + 3 * c + 1), in1=a0[:, :T],
              op0=ALU.mult, op1=ALU.add)
            ci = rot("cva", 2)
            E("dve", "scalar_tensor_tensor", r=[cxk, k1, "cols"], w=[("silu", ci)],
              out=cva[ci][:, :T], in0=cx[:, PFX - 2:PFX - 2 + T], scalar=col(C_CVW + 3 * c), in1=a1[:, :T],
              op0=ALU.mult, op1=ALU.add)
            if pend_b is not None:
                conv_b_part(*pend_b)
            pend_b = (c, cva[ci], ("silu", ci))
            if work:
                for fn, arg in work.pop(0):
                    fn(arg)
        conv_b_part(*pend_b)
        while work:
            for fn, arg in work.pop(0):
                fn(arg)

        for c in range(8):
            for g, dst, dk in ((0, sgc, "sgc"), (1, sga, "sga")):
                j = 30 + 2 * c + g
                wap, wkey, coff = win_chunk(j)
                pg, pgk = proj(wap, wkey, 8, coff, un_r, un_k, T)
                E("act", "activation", r=[pgk, "cols"], w=[(dk, c)],
                  out=dst[:, c, :T], in_=pg[:, :T], func=AF.Sigmoid, bias=col(C_BIN + j), scale=1.0)
        ws_release()
        cur["w"] = None

        E("dve", "tensor_copy", r=["kr"], w=["kprev"], out=kprev[:, :], in_=kr[:, T - 128:T])
        E("dve", "tensor_copy", r=[("vcur", NB - 1)], w=["vprev"], out=vprev[:, :, :], in_=vcur[:, NB - 1, :, :])

        wco0, wco0k = ws_acquire()
        wao_, waok = ws_acquire()
        wco1 = wco1k = None
        aokeys = [("ao", hh, i) for hh in range(2) for i in range(NB)]
        for c in range(8):
            if c == 4:
                ws_release()
                wco1, wco1k = ws_acquire()
            wc, wck = (wco0, wco0k) if c < 4 else (wco1, wco1k)
            yc, yck = proj(wc, wck, 8, (c % 4) * 128, lambda k: bc[:, k, :T], lambda k: ("bc", k), T)
            ya, yak = bank()
            for j in range(4):
                E("pe", "matmul", r=[waok] + aokeys, w=[yak],
                  out=ya[:, :T], lhsT=wao_[:, j, c * 128:(c + 1) * 128], rhs=ao[:, j, :T],
                  start=(j == 0), stop=(j == 3))
            m1, mk1 = tmpf()
            E("dve", "tensor_tensor", r=[yck, ("sgc", c)], w=[mk1],
              out=m1[:, :T], in0=yc[:, :T], in1=sgc[:, c, :T], op=ALU.mult)
            m2, mk2 = tmpf()
            E("dve", "scalar_tensor_tensor", r=[yak, ("sga", c), "cols"], w=[mk2],
              out=m2[:, :T], in0=ya[:, :T], scalar=col(C_BAO + c), in1=sga[:, c, :T], op0=ALU.add, op1=ALU.mult)
            E("dve", "tensor_tensor", r=[mk1, mk2], w=[("mg", c)],
              out=mg[:, c, :T], in0=m1[:, :T], in1=m2[:, :T], op=ALU.add)
        ws_release()
        ws_release()

        wm = wmk = None
        for c in range(8):
            if c % 4 == 0:
                if c == 4:
                    ws_release()
                wm, wmk = ws_acquire()
            pm, pmk = proj(wm, wmk, 8, (c % 4) * 128, lambda k: mg[:, k, :T], lambda k: ("mg", k), T)
            E("dve", "tensor_tensor", r=[pmk, (xkey, c)], w=[(xkey, c)],
              out=xb[:, c, :T], in0=xb[:, c, :T], in1=pm[:, :T], op=ALU.add)
        ws_release()

        if ti + 1 < len(tiles):
            load_x(ti + 1)

        norm_stats(xb, xkey, T, 1)
        norm_apply(xb, xkey, T, 1, C_NFW, un, "un")

        wunits = {}
        pre = {}

        def ffn_up_part(f):
            g = f // 2
            if g not in wunits:
                wunits[g] = ws_acquire()
            wu, wuk = wunits[g]
            lo = (f % 2) * 128
            if ("up", f) in pre:
                pu, puk = pre[("up", f)]
            else:
                pu, puk = proj(wu, wuk, 8, lo, un_r, un_k, T)
            ui = rot("upb", 3)
            ub = upb[ui]
            ubk = ("upb", ui)
            E("pool", "tensor_copy", r=[("upcarry", f)], w=[ubk], out=ub[:, PFX - 2:PFX], in_=upcarry[:, f, :])
            E("act", "activation", r=[puk, ubk], w=[ubk], out=ub[:, PFX:PFX + T], in_=pu[:, :T], func=AF.Copy)
            a0, k0 = tmpf()
            E("act", "activation", r=[puk, "cols"], w=[k0], out=a0[:, :T], in_=pu[:, :T], func=AF.Identity,
              scale=col(C_FCW + 3 * f + 2), bias=col(C_FCB + f))
            if last_halo:
                E("pool", "tensor_scalar", r=[ubk, "cols"], w=[("upcarry", f)],
                  out=upcarry[:, f, :], in0=ub[:, PFX + T - 2:PFX + T], scalar1=col(C_FLAG), scalar2=None, op0=ALU.mult)
            else:
                E("pool", "tensor_copy", r=[ubk], w=[("upcarry", f)], out=upcarry[:, f, :], in_=ub[:, PFX + T - 2:PFX + T])
            a1, k1 = tmpf()
            E("dve", "scalar_tensor_tensor", r=[ubk, k0, "cols"], w=[k1],
              out=a1[:, :T], in0=ub[:, PFX - 1:PFX - 1 + T], scalar=col(C_FCW + 3 * f + 1), in1=a0[:, :T],
              op0=ALU.mult, op1=ALU.add)
            a2, k2 = tmpf()
            E("dve", "scalar_tensor_tensor", r=[ubk, k1, "cols"], w=[k2],
              out=a2[:, :T], in0=ub[:, PFX - 2:PFX - 2 + T], scalar=col(C_FCW + 3 * f), in1=a1[:, :T],
              op0=ALU.mult, op1=ALU.add)
            return (a2, k2)

        def ffn_gate_part(f, a2k):
            a2, k2 = a2k
            si = rot("silu", 2)
            E("act", "activation", r=[k2], w=[("silu", si)], out=silub[si][:, :T], in_=a2[:, :T], func=AF.Silu)
            wu, wuk = wunits[f // 2]
            lo = (f % 2) * 128
            if ("gate", f) in pre:
                pg, pgk = pre[("gate", f)]
            else:
                pg, pgk = proj(wu, wuk, 8, 256 + lo, un_r, un_k, T)
            E("dve", "tensor_tensor", r=[pgk, ("silu", si)], w=[("hb", f)],
              out=hb[:, f, :T], in0=pg[:, :T], in1=silub[si][:, :T], op=ALU.mult)
            if f % 2 == 1:
                ws_release()

        if is_halo:
            for f in range(NFC):
                g = f // 2
                if f % 2 == 0:
                    wu, wuk = ws_acquire()
                pu, puk = proj(wu, wuk, 8, (f % 2) * 128, un_r, un_k, T)
                ui = rot("upb", 3)
                ub = upb[ui]
                ubk = ("upb", ui)
                E("act", "activation", r=[puk], w=[ubk], out=ub[:, PFX:PFX + T], in_=pu[:, :T], func=AF.Copy)
                if last_halo:
                    E("pool", "tensor_scalar", r=[ubk, "cols"], w=[("upcarry", f)],
                      out=upcarry[:, f, :], in0=ub[:, PFX + T - 2:PFX + T], scalar1=col(C_FLAG), scalar2=None,
                      op0=ALU.mult)
                else:
                    E("pool", "tensor_copy", r=[ubk], w=[("upcarry", f)], out=upcarry[:, f, :],
                      in_=ub[:, PFX + T - 2:PFX + T])
                if f % 2 == 1:
                    ws_release()
            if ti + 1 < len(tiles):
                prep(ti + 1)
        else:
            wunits[0] = ws_acquire()
            wu0, wuk0 = wunits[0]
            pbanks = [bank() for _ in range(4)]
            for k in range(8):
                for j in range(4):
                    E("pe", "matmul", r=[wuk0, ("un", k)], w=[pbanks[j][1]],
                      out=pbanks[j][0][:, :T], lhsT=wu0[:, k, j * 128:(j + 1) * 128], rhs=un[:, k, :T],
                      start=(k == 0), stop=(k == 7))
            pre[("up", 0)], pre[("up", 1)], pre[("gate", 0)], pre[("gate", 1)] = pbanks
            prev = None
            for f in range(NFC):
                si = ffn_up_part(f)
                if prev is not None:
                    ffn_gate_part(*prev)
                prev = (f, si)
            ffn_gate_part(*prev)

            for c in range(8):
                wd, wdk = ws_acquire()
                pd, pdk = proj(wd, wdk, NFC, 0, lambda k: hb[:, k, :T], lambda k: ("hb", k), T)
                ws_release()
                if c == 2 and ti + 1 < len(tiles):
                    prep(ti + 1)
                E("dve", "tensor_tensor", r=[pdk, (xkey, c)], w=[(xkey, c)],
                  out=xb[:, c, :T], in0=xb[:, c, :T], in1=pd[:, :T], op=ALU.add)

        if ti + 1 == len(tiles) and not is_halo:
            finish_sq(ti)
            finish_tile(ti)

    E("sp", None, w=[("x0", c) for c in range(8)] + [("x1", c) for c in range(8)]
      + [("ring", i) for i in range(NSLOT)] + ["cbf", "posi", "cols"])
    P.emit()
    P.close()
    return nc, P


def _perm_in():
    cb0, cc0, cx0, q0, k0, v0, gc0, ga0 = 0, 1024, 2048, 3072, 3584, 3712, 3840, 4864
    perm = []
    perm += list(range(k0, k0 + 128))
    for j in range(4):
        perm += list(range(q0 + j * 64, q0 + (j + 1) * 64))
        perm += list(range(q0 + (4 + j) * 64, q0 + (5 + j) * 64))
    perm += list(range(v0, v0 + 128))
    base = {"cc": cc0, "cx": cx0, "cb": cb0}
    for kind, c in CONV_ORDER:
        perm += list(range(base[kind] + c * 128, base[kind] + (c + 1) * 128))
    for c in range(8):
        perm += list(range(gc0 + c * 128, gc0 + (c + 1) * 128))
        perm += list(range(ga0 + c * 128, ga0 + (c + 1) * 128))
    assert len(perm) == INW
    return np.array(perm, dtype=np.int64)


def _pkc(w):
    K, N = w.shape
    return np.ascontiguousarray(w.reshape(K // 128, 128, N).transpose(1, 0, 2))


def _colchunks(v):
    return np.ascontiguousarray(v.reshape(-1, 128).T)


def prepare_inputs(x, positions, norm_mix_w, w_in, b_in, conv_mix_w, w_conv_out, w_attn_out, b_attn_out, sinks,
                   w_mix_out, norm_ffn_w, w_ffn_up, ffn_conv_w, ffn_conv_b, w_ffn_down, norm_final_w):
    f32 = np.float32
    perm = _perm_in()
    win = _pkc(np.asarray(w_in[0], f32)[:, perm])
    wco = _pkc(np.asarray(w_conv_out[0], f32))
    arow = []
    for j in range(4):
        arow += list(range(j * 64, (j + 1) * 64)) + list(range((4 + j) * 64, (5 + j) * 64))
    wao = _pkc(np.asarray(w_attn_out[0], f32)[np.array(arow)])
    wmo = _pkc(np.asarray(w_mix_out[0], f32))
    uperm = []
    for g in range(NFC // 2):
        uperm += list(range(2 * g * 128, (2 * g + 2) * 128))
        uperm += list(range(DFF + 2 * g * 128, DFF + (2 * g + 2) * 128))
    wup = _pkc(np.asarray(w_ffn_up[0], f32)[:, np.array(uperm)])
    wd = np.asarray(w_ffn_down[0], f32).reshape(NFC, 128, 8, 128)
    wdn = np.ascontiguousarray(wd.transpose(1, 2, 0, 3))

    cols = np.zeros((128, NCOLS), f32)
    cols[:, C_NMW:C_NMW + 8] = _colchunks(np.asarray(norm_mix_w[0], f32))
    cols[:, C_NFW:C_NFW + 8] = _colchunks(np.asarray(norm_ffn_w[0], f32))
    cols[:, C_NLW:C_NLW + 8] = _colchunks(np.asarray(norm_final_w, f32))
    cols[:, C_BIN:C_BIN + 46] = _colchunks(np.asarray(b_in[0], f32)[perm])
    cols[:, C_BAO:C_BAO + 8] = _colchunks(np.asarray(b_attn_out[0], f32))
    cw = np.asarray(conv_mix_w[0], f32)
    cols[:, C_CVW:C_CVW + 24] = cw.reshape(3, 8, 128).transpose(2, 1, 0).reshape(128, 24)
    fw = np.asarray(ffn_conv_w[0], f32)
    cols[:, C_FCW:C_FCW + 66] = fw.reshape(3, NFC, 128).transpose(2, 1, 0).reshape(128, 66)
    cols[:, C_FCB:C_FCB + NFC] = _colchunks(np.asarray(ffn_conv_b[0], f32))
    inv_freq = (10000.0 ** (-np.arange(0, HD, 2, dtype=f32) / HD)).astype(f32)
    cols[:, C_INVF] = inv_freq[np.arange(128) % 32]
    cols[:, C_SINK:C_SINK + 8] = np.asarray(sinks[0], f32)[None, :]
    cols[:, C_BV:C_BV + 128] = np.asarray(b_in[0], f32)[3712:3840][None, :]

    cbf = np.zeros((128, NBCOLS), f32)
    cbf[:, B_ONES:B_ONES + 128] = 1.0
    rm = np.zeros((128, 128), f32)
    for m in range(128):
        if m % 64 < 32:
            rm[m + 32, m] = -1.0
        else:
            rm[m - 32, m] = 1.0
    cbf[:, B_ROT:B_ROT + 128] = rm
    kk = np.arange(128)[:, None]
    qq = np.arange(128)[None, :]
    mp = (kk > qq).astype(f32)
    mc = (kk <= qq).astype(f32)
    cbf[:, B_MPREV:B_MPREV + 512] = np.tile(mp, (1, 4))
    cbf[:, B_MCUR:B_MCUR + 512] = np.tile(mc, (1, 4))
    cbf[0, B_SEL + 64:B_SEL + 128] = 1.0

    x = np.asarray(x, f32)
    positions = np.asarray(positions, np.int32)
    in_maps = []
    for core in range(NCORE):
        b, half = core // 2, core % 2
        t0 = half * TOK
        xT = np.zeros((D, NTOK), f32)
        xT[:, HALO:] = x[b, t0:t0 + TOK, :].T
        pos = np.zeros((NTOK,), np.int32)
        pos[HALO:] = positions[b, t0:t0 + TOK]
        ccols = cols.copy()
        ccbf = cbf.copy()
        if half == 1:
            xT[:, :HALO] = x[b, t0 - HALO:t0, :].T
            pos[:HALO] = positions[b, t0 - HALO:t0]
            ccols[:, C_FLAG] = 1.0
            ccbf[:, B_MFIRST:B_MFIRST + 512] = np.tile(mp, (1, 4))
        in_maps.append({
            "xT": xT, "posb": np.ascontiguousarray(np.broadcast_to(pos[None, :], (128, NTOK))),
            "cols": ccols, "cbf": ccbf, "win": win, "wco": wco, "wao": wao, "wmo": wmo, "wup": wup, "wdn": wdn,
        })
    return in_maps


def kernel(**inputs):
    in_maps = prepare_inputs(**inputs)
    nc, _ = build_program()
    res = run_bass_kernel_spmd(nc, in_maps, core_ids=list(range(NCORE)))
    out = np.empty((BATCH, SEQ, D), np.float32)
    for core in range(NCORE):
        b, half = core // 2, core % 2
        out[b, half * TOK:(half + 1) * TOK, :] = res.results[core]["outT"].T
    return out
```

```python
from contextlib import ExitStack
import math
import os
import numpy as np
import concourse.bass as bass
import concourse.mybir as mybir
from concourse.bass_utils import run_bass_kernel_spmd

F32 = mybir.dt.float32
BF16 = mybir.dt.bfloat16
I32 = mybir.dt.int32
AF = mybir.ActivationFunctionType
ALU = mybir.AluOpType

D = 1024
SEQ = 8192
BATCH = 4
DFF = 2816
NQ = 8
NKV = 2
HD = 64
EPS = 1e-5
NCORE = 8
TOK = 4096
HALO = 256
NTOK = TOK + HALO
INW = 5888
NFC = DFF // 128
TILES = [(0, 256)] + [(256 + 512 * i, 512) for i in range(8)]
NSLOT = 4
NWBG = 6
PFX = 8
NBANK = int(os.environ.get('KNBANK', 8))
SLOT_ELEMS = 4096
TWO_PI = 2.0 * math.pi
PI_SAFE = 3.1415925

CONV_ORDER = [("cc", 0), ("cx", 0)]
for _c in range(1, 8):
    CONV_ORDER += [("cc", _c), ("cx", _c), ("cb", _c - 1)]
CONV_ORDER += [("cb", 7)]
CONV_J = {k: 6 + i for i, k in enumerate(CONV_ORDER)}

C_NMW = 0
C_NFW = 8
C_NLW = 16
C_BIN = 24
C_BAO = 70
C_CVW = 78
C_FCW = 102
C_FCB = 168
C_INVF = 190
C_FLAG = 191
C_SINK = 192
C_BV = 200
NCOLS = 328

B_ONES = 0
B_ROT = 128
B_MPREV = 256
B_MCUR = 768
B_MFIRST = 1280
B_SEL = 1792
NBCOLS = 1920


ENGS = ("pe", "act", "dve", "pool", "sp")


class Op:
    __slots__ = ("eng", "fn", "deps", "signal", "sig_idx", "dma_sem", "dma_cnt", "pos")

    def __init__(self, eng, fn):
        self.eng = eng
        self.fn = fn
        self.deps = []
        self.signal = False
        self.sig_idx = 0
        self.dma_sem = None
        self.dma_cnt = 0


class Prog:
    def __init__(self, nc):
        self.nc = nc
        self.es = ExitStack()
        self.ops = {e: [] for e in ENGS}
        self.state = {}
        self.dma_counts = {}
        self.sems = {}

    def sbuf(self, name, shape, dtype):
        return self.es.enter_context(self.nc.sbuf_tensor("sb_" + name, list(shape), dtype))

    def psum(self, name, shape, dtype):
        return self.es.enter_context(self.nc.psum_tensor(name, list(shape), dtype))

    def sem(self, name):
        s = self.es.enter_context(self.nc.semaphore("sm_" + name))
        self.sems[name] = s
        self.dma_counts[name] = 0
        return name

    def op(self, eng, fn, r=(), w=(), dma=None):
        self.nrec = getattr(self, "nrec", 0) + 1
        if self.nrec > getattr(self, "limit", 10 ** 9) and fn[0] is not None:
            return None
        o = Op(eng, fn)
        deps = {}
        for k in r:
            st = self.state.get(k)
            if st is not None and st[0] is not None:
                deps[id(st[0])] = st[0]
        for k in w:
            st = self.state.get(k)
            if st is None:
                continue
            if st[0] is not None and (st[0].eng != eng or st[0].dma_sem is not None or eng != "pe"):
                deps[id(st[0])] = st[0]
            for rd in st[1]:
                if rd.eng != eng or rd.dma_sem is not None or eng != "pe":
                    deps[id(rd)] = rd
        for k in r:
            st = self.state.setdefault(k, [None, []])
            st[1].append(o)
        for k in w:
            self.state[k] = [o, []]
        best = {}
        for d in deps.values():
            if d.dma_sem is not None:
                key = ("d", d.dma_sem)
                if key not in best or d.dma_cnt > best[key].dma_cnt:
                    best[key] = d
            else:
                key = ("e", d.eng)
                if key not in best or d.pos > best[key].pos:
                    best[key] = d
        for d in best.values():
            if d.dma_sem is None:
                d.signal = True
        o.deps = list(best.values())
        o.pos = len(self.ops[eng])
        if dma is not None:
            o.dma_sem = dma
            self.dma_counts[dma] += 16
            o.dma_cnt = self.dma_counts[dma]
        self.ops[eng].append(o)
        return o

    def emit(self):
        nc = self.nc
        engsem = {e: self.es.enter_context(nc.semaphore("es_" + e)) for e in ENGS}
        for e in ENGS:
            c = 0
            for o in self.ops[e]:
                if o.signal and o.dma_sem is None:
                    c += 1
                    o.sig_idx = c
        block = self.es.enter_context(nc.Block())
        stats = {}

        def run(e, handle):
            waited = {}
            nw = 0
            for o in self.ops[e]:
                for d in o.deps:
                    if d.dma_sem is not None:
                        key, sem, val = "d_" + d.dma_sem, self.sems[d.dma_sem], d.dma_cnt
                    else:
                        key, sem, val = "e_" + d.eng, engsem[d.eng], d.sig_idx
                    if waited.get(key, 0) >= val:
                        continue
                    waited[key] = val
                    handle.wait_ge(sem, val)
                    nw += 1
                meth, kw = o.fn
                ins = getattr(handle, meth)(**kw) if meth is not None else None
                if o.dma_sem is not None:
                    ins.then_inc(self.sems[o.dma_sem], 16)
                elif o.signal:
                    ins.then_inc(engsem[e], 1)
            stats[e] = (len(self.ops[e]), nw)

        @block.sync
        def _(h):
            run("sp", h)

        @block.tensor
        def _(h):
            run("pe", h)

        @block.scalar
        def _(h):
            run("act", h)

        @block.vector
        def _(h):
            run("dve", h)

        @block.gpsimd
        def _(h):
            run("pool", h)

        self.stats = stats

    def close(self):
        self.es.close()


def build_program(tiles=TILES):
    nc = bass.Bass("TRN2", target_bir_lowering=False)
    XT = nc.dram_tensor("xT", [D, NTOK], F32, kind="ExternalInput").ap()
    POS = nc.dram_tensor("posb", [128, NTOK], I32, kind="ExternalInput").ap()
    COLS = nc.dram_tensor("cols", [128, NCOLS], F32, kind="ExternalInput").ap()
    CBF = nc.dram_tensor("cbf", [128, NBCOLS], F32, kind="ExternalInput").ap()
    WIN = nc.dram_tensor("win", [128, 8, INW], F32, kind="ExternalInput").ap()
    WCO = nc.dram_tensor("wco", [128, 8, D], F32, kind="ExternalInput").ap()
    WAO = nc.dram_tensor("wao", [128, 4, D], F32, kind="ExternalInput").ap()
    WMO = nc.dram_tensor("wmo", [128, 8, D], F32, kind="ExternalInput").ap()
    WUP = nc.dram_tensor("wup", [128, 8, 2 * DFF], F32, kind="ExternalInput").ap()
    WDN = nc.dram_tensor("wdn", [128, 8, NFC, 128], F32, kind="ExternalInput").ap()
    OUT = nc.dram_tensor("outT", [D, TOK], F32, kind="ExternalOutput").ap()
    NUNIT = 36
    SCR = nc.dram_tensor("wscr", [NUNIT, 128, SLOT_ELEMS], BF16, kind="Internal").ap()
    XT3 = XT.rearrange("(c p) t -> p c t", p=128)
    OUT3 = OUT.rearrange("(c p) t -> p c t", p=128)

    P = Prog(nc)
    P.limit = int(os.environ.get("KLIMIT", 10 ** 9))
    P.nrec = 0
    bank_touch = [0] * 8

    def E(eng, meth, r=(), w=(), dma=None, **kw):
        for k in list(r) + list(w):
            if isinstance(k, tuple) and k[0] == "ps":
                bank_touch[k[1]] = P.nrec + 1
        return P.op(eng, (meth, kw), r=r, w=w, dma=dma)

    TM = 512
    xs = [P.sbuf(f"xs{i}", [128, 8, TM], F32) for i in range(2)]
    cols = P.sbuf("cols", [128, NCOLS], F32)
    cbf = P.sbuf("cbf", [128, NBCOLS], BF16)
    ring = [P.sbuf(f"ring{i}", [128, SLOT_ELEMS], BF16) for i in range(NSLOT)]
    sq = [P.sbuf(f"sq{i}", [128, TM], BF16) for i in range(2)]
    rstd = [P.sbuf(f"rstd{i}", [128, TM], F32) for i in range(3)]
    un = P.sbuf("un", [128, 8, TM], BF16)
    NTMP = 8
    tmp = [P.sbuf(f"tmp{i}", [128, TM], F32) for i in range(NTMP)]
    tmpi = P.sbuf("tmpi", [128, TM], I32)
    cosb = P.sbuf("cosb", [128, TM], F32)
    sinb = P.sbuf("sinb", [128, TM], F32)
    posi = P.sbuf("posi", [128, TM], I32)
    qb = [P.sbuf(f"qb{i}", [128, TM], BF16) for i in range(2)]
    qr = P.sbuf("qr", [128, 4, TM], BF16)
    kr = P.sbuf("kr", [128, TM], BF16)
    kprev = P.sbuf("kprev", [128, 128], BF16)
    vcur = P.sbuf("vcur", [128, 4, 2, 128], BF16)
    vprev = P.sbuf("vprev", [128, 2, 128], BF16)
    cxb = [P.sbuf(f"cxb{i}", [128, TM + PFX], F32) for i in range(2)]
    cxcarry = P.sbuf("cxcarry", [128, 8, 2], F32)
    bc = P.sbuf("bc", [128, 8, TM], BF16)
    sgc = P.sbuf("sgc", [128, 8, TM], BF16)
    sga = P.sbuf("sga", [128, 8, TM], BF16)
    NPT = 12
    pt = [P.sbuf(f"pt{i}", [128, TM], BF16) for i in range(NPT)]
    mzero = P.sbuf("mzero", [128, TM], BF16)
    sinkrow = P.sbuf("sinkrow", [1, 2, TM], BF16)
    sinktmp = P.sbuf("sinktmp", [128, 8], F32)
    zf = P.sbuf("zf", [128, 128], F32)
    ao = P.sbuf("ao", [128, 4, TM], BF16)
    mg = P.sbuf("mg", [128, 8, TM], BF16)
    upb = [P.sbuf(f"upb{i}", [128, TM + PFX], F32) for i in range(3)]
    upcarry = P.sbuf("upcarry", [128, NFC, 2], F32)
    silub = [P.sbuf(f"silub{i}", [128, TM], F32) for i in range(2)]
    cva = silub
    hb = P.sbuf("hb", [128, NFC, TM], BF16)
    banks = [P.psum(f"ps{i}", [128, TM], F32) for i in range(8)]

    s_ring = [P.sem(f"ring{i}") for i in range(NSLOT)]
    s_wb = [P.sem(f"wb{i}") for i in range(NSLOT)]
    s_ld = [P.sem(f"ld{i}") for i in range(NSLOT)]
    s_x = [P.sem(f"x{i}") for i in range(2)]
    s_o = [P.sem(f"o{i}") for i in range(2)]
    s_pos = P.sem("pos")
    s_pos0 = P.sem("pos0")
    s_x0 = P.sem("x0first")
    s_cols = P.sem("cols")
    s_cbf = P.sem("cbf")

    ctr = {"bank": 0, "tmp": 0, "sq": 0, "qb": 0, "pt": 0, "cxb": 0, "upb": 0, "silu": 0, "cva": 0}

    def rot(name, n):
        i = ctr[name]
        ctr[name] = (i + 1) % n
        return i

    def bank():
        b = min(range(NBANK), key=lambda i: bank_touch[i])
        bank_touch[b] = P.nrec + 1
        return banks[b], ("ps", b)

    def tmpf():
        i = rot("tmp", NTMP)
        return tmp[i], ("tmp", i)

    units = []
    NHALO_UNITS = 0
    for _ti, (_tok0, _T) in enumerate(tiles):
        _halo = _tok0 < HALO
        for u in range(12):
            n = min(512, INW - u * 512)
            units.append((WIN[:, :, u * 512:u * 512 + n], (8, n)))
        units.append((WCO[:, :, 0:512], (8, 512)))
        units.append((WAO[:, :, :], (4, 1024)))
        units.append((WCO[:, :, 512:1024], (8, 512)))
        units.append((WMO[:, :, 0:512], (8, 512)))
        units.append((WMO[:, :, 512:1024], (8, 512)))
        for u in range(11):
            if _halo:
                units.append((WUP[:, :, u * 512:u * 512 + 256], (8, 256)))
            else:
                units.append((WUP[:, :, u * 512:(u + 1) * 512], (8, 512)))
        if not _halo:
            for c in range(8):
                units.append((WDN[:, c, :, :], (NFC, 128)))
        if _halo:
            NHALO_UNITS = len(units)
    ws = {"issued": 0, "acq": 0}

    def ws_issue():
        n = ws["issued"]
        if n >= len(units):
            return
        src, (a, b) = units[n]
        slot = n % NSLOT
        dst = ring[slot][:, 0:a * b].rearrange("p (a b) -> p a b", a=a)
        if n < NHALO_UNITS:
            t, u, g = 0, n, 0
        else:
            t, u = 1 + (n - NHALO_UNITS) // NUNIT, (n - NHALO_UNITS) % NUNIT
            g = u % NWBG
        if t >= g + 2:
            E("sp", "dma_start", r=[("scr", u)], w=[("ring", slot)], dma=s_ld[slot],
              out=ring[slot][:, 0:a * b], in_=SCR[u, :, 0:a * b])
        else:
            E("pool", "dma_start", w=[("ring", slot)], dma=s_ring[slot], out=dst, in_=src)
            if t == g + 1:
                E("sp", "dma_start", r=[("ring", slot)], w=[("scr", u)], dma=s_wb[slot],
                  out=SCR[u, :, 0:a * b], in_=ring[slot][:, 0:a * b])
        ws["issued"] = n + 1

    def ws_acquire():
        n = ws["acq"]
        assert n < ws["issued"], "weight ring underflow"
        ws["acq"] = n + 1
        _, (a, b) = units[n]
        slot = n % NSLOT
        return ring[slot][:, 0:a * b].rearrange("p (a b) -> p a b", a=a), ("ring", slot)

    def ws_release():
        ws_issue()

    E("sp", "dma_start", w=["cols"], dma=s_cols, out=cols[:], in_=COLS)
    E("pool", "dma_start", w=["cbf"], dma=s_cbf, out=cbf[:], in_=CBF)
    ones_m = cbf[:, B_ONES:B_ONES + 128]
    rot_m = cbf[:, B_ROT:B_ROT + 128]
    mprev = cbf[:, B_MPREV:B_MPREV + 512]
    mcur = cbf[:, B_MCUR:B_MCUR + 512]
    mfirst = cbf[:, B_MFIRST:B_MFIRST + 512]

    def col(i):
        return cols[:, i:i + 1]

    E("dve", "memset", w=["mzero"], ap=mzero[:], constant=0.0)
    E("dve", "memset", w=["kprev"], ap=kprev[:], constant=0.0)
    E("dve", "memset", w=["vprev"], ap=vprev[:], constant=0.0)
    E("dve", "memset", w=["vcur"], ap=vcur[:], constant=1.0)
    E("dve", "memset", w=["cxcarry"], ap=cxcarry[:], constant=0.0)
    E("dve", "memset", w=["upcarry"], ap=upcarry[:], constant=0.0)
    E("dve", "memset", w=["zf"], ap=zf[:], constant=0.0)
    E("act", "activation", r=["cols"], w=["sinktmp"], out=sinktmp[:], in_=cols[:, C_SINK:C_SINK + 8], func=AF.Exp)
    for hh in range(2):
        for j in range(4):
            E("dve", "tensor_scalar", r=["zf", "sinktmp"], w=["sinkrow"],
              out=sinkrow[0:1, hh, j * 128:(j + 1) * 128], in0=zf[0:1, :],
              scalar1=sinktmp[0:1, hh * 4 + j:hh * 4 + j + 1], scalar2=None, op0=ALU.add)

    def norm_stats(xb, xkey, T, ridx):
        pb, pk = bank()
        for c in range(8):
            si = rot("sq", 2)
            E("act", "activation", r=[(xkey, c)], w=[("sq", si)],
              out=sq[si][:, :T], in_=xb[:, c, :T], func=AF.Square, scale=1.0 / 32.0)
            E("pe", "matmul", r=[("sq", si), "cbf"], w=[pk],
              out=pb[:, :T], lhsT=ones_m, rhs=sq[si][:, :T], start=(c == 0), stop=(c == 7))
        tb, tk = tmpf()
        E("act", "activation", r=[pk], w=[tk], out=tb[:, :T], in_=pb[:, :T], func=AF.Ln, bias=EPS, scale=1.0)
        E("act", "activation", r=[tk], w=[("rstd", ridx)], out=rstd[ridx][:, :T], in_=tb[:, :T], func=AF.Exp, scale=-0.5)

    def norm_apply(xb, xkey, T, ridx, wc0, dst, dkey):
        for c in range(8):
            E("dve", "scalar_tensor_tensor", r=[(xkey, c), ("rstd", ridx), "cols"], w=[(dkey, c)],
              out=dst[:, c, :T], in0=xb[:, c, :T], scalar=col(wc0 + c), in1=rstd[ridx][:, :T],
              op0=ALU.mult, op1=ALU.mult)

    def proj(wap, wkey, nk, coff, rhs_fn, rkeys, T):
        pb, pk = bank()
        for k in range(nk):
            E("pe", "matmul", r=[wkey, rkeys(k)], w=[pk],
              out=pb[:, :T], lhsT=wap[:, k, coff:coff + 128], rhs=rhs_fn(k), start=(k == 0), stop=(k == nk - 1))
        return pb, pk

    def load_x(ti):
        tok0, T = tiles[ti]
        xb = xs[ti % 2]
        xkey = "x%d" % (ti % 2)
        E("sp" if ti == 0 else "pool", "dma_start", w=[(xkey, c) for c in range(8)], dma=s_x0 if ti == 0 else s_x[ti % 2],
          out=xb[:, :, :T], in_=XT3[:, :, tok0:tok0 + T])

    prep_sq_done = set()

    def prep_sq(ti):
        tok0, T = tiles[ti]
        xb = xs[ti % 2]
        xkey = "x%d" % (ti % 2)
        for c in range(8):
            E("act", "activation", r=[(xkey, c)], w=[("mg", c)],
              out=mg[:, c, :T], in_=xb[:, c, :T], func=AF.Square, scale=1.0 / 32.0)
        prep_sq_done.add(ti)

    def prep(ti):
        tok0, T = tiles[ti]
        xb = xs[ti % 2]
        xkey = "x%d" % (ti % 2)
        if ti > 0:
            E("pool", "dma_start", w=["posi"], dma=s_pos, out=posi[:, :T], in_=POS[:, tok0:tok0 + T])

        if ti in prep_sq_done:
            pb, pk = bank()
            for c in range(8):
                E("pe", "matmul", r=[("mg", c), "cbf"], w=[pk],
                  out=pb[:, :T], lhsT=ones_m, rhs=mg[:, c, :T], start=(c == 0), stop=(c == 7))
            tb, tk = tmpf()
            E("act", "activation", r=[pk], w=[tk], out=tb[:, :T], in_=pb[:, :T], func=AF.Ln, bias=EPS, scale=1.0)
            E("act", "activation", r=[tk], w=[("rstd", 0)], out=rstd[0][:, :T], in_=tb[:, :T], func=AF.Exp, scale=-0.5)
        else:
            norm_stats(xb, xkey, T, 0)
        norm_apply(xb, xkey, T, 0, C_NMW, un, "un")

        a_b, a_k = tmpf()
        E("dve", "tensor_copy", r=["posi"], w=[a_k], out=a_b[:, :T], in_=posi[:, :T])
        ang_b, ang_k = tmpf()
        E("dve", "tensor_scalar", r=[a_k, "cols"], w=[ang_k],
          out=ang_b[:, :T], in0=a_b[:, :T], scalar1=col(C_INVF), scalar2=None, op0=ALU.mult)
        for which in range(2):
            shift = 0.0 if which == 0 else 0.5 * math.pi
            dstb = sinb if which == 0 else cosb
            dk = "sinb" if which == 0 else "cosb"
            E("dve", "tensor_scalar", r=[ang_k], w=["tmpi"],
              out=tmpi[:, :T], in0=ang_b[:, :T], scalar1=shift, scalar2=1.0 / TWO_PI, op0=ALU.add, op1=ALU.mult)
            kf_b, kf_k = tmpf()
            E("dve", "tensor_copy", r=["tmpi"], w=[kf_k], out=kf_b[:, :T], in_=tmpi[:, :T])
            r_b, r_k = tmpf()
            E("dve", "scalar_tensor_tensor", r=[kf_k, ang_k], w=[r_k],
              out=r_b[:, :T], in0=kf_b[:, :T], scalar=-TWO_PI, in1=ang_b[:, :T], op0=ALU.mult, op1=ALU.add)
            E("dve", "tensor_scalar", r=[r_k], w=[r_k],
              out=r_b[:, :T], in0=r_b[:, :T], scalar1=PI_SAFE - shift, scalar2=-PI_SAFE - shift,
              op0=ALU.min, op1=ALU.max)
            E("act", "activation", r=[r_k], w=[dk], out=dstb[:, :T], in_=r_b[:, :T], func=AF.Sin, bias=shift, scale=1.0)


    def finish_sq(ti):
        tok0, T = tiles[ti]
        xb = xs[ti % 2]
        xkey = "x%d" % (ti % 2)
        for c in range(8):
            E("act", "activation", r=[(xkey, c)], w=[("hb", c)],
              out=hb[:, c, :T], in_=xb[:, c, :T], func=AF.Square, scale=1.0 / 32.0)

    def finish_tile(ti):
        tok0, T = tiles[ti]
        xb = xs[ti % 2]
        xkey = "x%d" % (ti % 2)
        xkeys = [(xkey, c) for c in range(8)]
        pb, pk = bank()
        for c in range(8):
            E("pe", "matmul", r=[("hb", c), "cbf"], w=[pk],
              out=pb[:, :T], lhsT=ones_m, rhs=hb[:, c, :T], start=(c == 0), stop=(c == 7))
        tb, tk = tmpf()
        E("act", "activation", r=[pk], w=[tk], out=tb[:, :T], in_=pb[:, :T], func=AF.Ln, bias=EPS, scale=1.0)
        E("act", "activation", r=[tk], w=[("rstd", 2)], out=rstd[2][:, :T], in_=tb[:, :T], func=AF.Exp, scale=-0.5)
        norm_apply(xb, xkey, T, 2, C_NLW, xb, xkey)
        o0 = tok0 - HALO
        E("pool", "dma_start", r=xkeys, dma=s_o[ti % 2], out=OUT3[:, :, o0:o0 + T], in_=xb[:, :, :T])

    load_x(0)
    E("sp", "dma_start", w=["posi"], dma=s_pos0, out=posi[:, :tiles[0][1]], in_=POS[:, tiles[0][0]:tiles[0][0] + tiles[0][1]])
    for _ in range(NSLOT):
        ws_issue()
    prep(0)

    for ti, (tok0, T) in enumerate(tiles):
        NB = T // 128
        gb0 = tok0 // 128
        is_halo = tok0 < HALO
        last_halo = is_halo and (tok0 + T == HALO)
        xb = xs[ti % 2]
        xkey = "x%d" % (ti % 2)
        xkeys = [(xkey, c) for c in range(8)]

        un_r = lambda k, T=T: un[:, k, :T]
        un_k = lambda k: ("un", k)
        cur = {"w": None, "k": None, "n": -1}

        def win_chunk(j):
            u = j // 4
            if cur["w"] is None or cur["n"] != u:
                if cur["w"] is not None:
                    ws_release()
                cur["w"], cur["k"] = ws_acquire()
                cur["n"] = u
            return cur["w"], cur["k"], (j % 4) * 128

        def rope_start(pb, pk, j, dst, dkey):
            qi = rot("qb", 2)
            qf, qfk = tmpf()
            E("act", "activation", r=[pk, "cols"], w=[qfk],
              out=qf[:, :T], in_=pb[:, :T], func=AF.Identity, bias=col(C_BIN + j), scale=1.0)
            E("act", "activation", r=[qfk], w=[("qb", qi)], out=qb[qi][:, :T], in_=qf[:, :T], func=AF.Copy)
            t1, k1 = tmpf()
            E("dve", "tensor_tensor", r=[qfk, "cosb"], w=[k1], out=t1[:, :T], in0=qf[:, :T], in1=cosb[:, :T], op=ALU.mult)
            return (qi, t1, k1, dst, dkey)

        def rope_finish(st):
            qi, t1, k1, dst, dkey = st
            rb, rk = bank()
            E("pe", "matmul", r=[("qb", qi), "cbf"], w=[rk],
              out=rb[:, :T], lhsT=rot_m, rhs=qb[qi][:, :T], start=True, stop=True)
            t2, k2 = tmpf()
            E("dve", "tensor_tensor", r=[rk, "sinb"], w=[k2], out=t2[:, :T], in0=rb[:, :T], in1=sinb[:, :T], op=ALU.mult)
            E("dve", "tensor_tensor", r=[k1, k2], w=[dkey], out=dst, in0=t1[:, :T], in1=t2[:, :T], op=ALU.add)

        if ti > 0 and not tiles[ti - 1][0] < HALO:
            finish_sq(ti - 1)
        pend = None
        for j in range(0, 5):
            wap, wkey, coff = win_chunk(j)
            pb, pk = proj(wap, wkey, 8, coff, un_r, un_k, T)
            if pend is not None:
                rope_finish(pend)
            if j == 0:
                pend = rope_start(pb, pk, 0, kr[:, :T], "kr")
            else:
                pend = rope_start(pb, pk, j, qr[:, j - 1, :T], ("qr", j - 1))
        wap, wkey, coff = win_chunk(5)
        vb, vk = bank()
        for i in range(NB):
            for k in range(8):
                E("pe", "matmul", r=[wkey, ("un", k)], w=[vk],
                  out=vb[:, i * 128:(i + 1) * 128], lhsT=un[:, k, i * 128:(i + 1) * 128],
                  rhs=wap[:, k, coff:coff + 128], start=(k == 0), stop=(k == 7))
        rope_finish(pend)
        for i in range(NB):
            E("dve", "tensor_tensor", r=[vk, "cols"], w=[("vcur", i)],
              out=vcur[:, i, :, 0:64], in0=vb[:, i * 128:(i + 1) * 128].rearrange("p (a b) -> p a b", a=2),
              in1=cols[:, C_BV:C_BV + 128].rearrange("p (a b) -> p a b", a=2), op=ALU.add)
        if ti > 0 and not tiles[ti - 1][0] < HALO:
            finish_tile(ti - 1)

        pts = {}

        def attn_A(i):
            gb = gb0 + i
            for hh in range(2):
                for kb in range(2):
                    if kb == 0:
                        if i == 0:
                            ksrc, kkey = kprev[hh * 64:(hh + 1) * 64, :], "kprev"
                        else:
                            ksrc, kkey = kr[hh * 64:(hh + 1) * 64, (i - 1) * 128:i * 128], "kr"
                        if gb == 0:
                            msk, mkey = mzero[:, :], "mzero"
                        elif gb == HALO // 128:
                            msk, mkey = mfirst, "cbf"
                        else:
                            msk, mkey = mprev, "cbf"
                    else:
                        ksrc, kkey = kr[hh * 64:(hh + 1) * 64, i * 128:(i + 1) * 128], "kr"
                        msk, mkey = mcur, "cbf"
                    sb, sk = bank()
                    E("pe", "matmul", r=[kkey] + [("qr", j) for j in range(4)], w=[sk],
                      out=sb[:, :], lhsT=ksrc, rhs=qr[hh * 64:(hh + 1) * 64, :, i * 128:(i + 1) * 128],
                      start=True, stop=True)
                    pi = rot("pt", NPT)
                    E("act", "activation", r=[sk], w=[("pt", pi)],
                      out=pt[pi][:, :], in_=sb[:, :], func=AF.Exp, scale=HD ** -0.5)
                    E("dve", "tensor_tensor", r=[("pt", pi), mkey], w=[("pt", pi)],
                      out=pt[pi][:, :], in0=pt[pi][:, :], in1=msk, op=ALU.mult)
                    pts[(i, hh, kb)] = pi

        def attn_B(i):
            for hh in range(2):
                ob, ok = bank()
                for kb in range(2):
                    if kb == 0:
                        if i == 0:
                            vsrc, vkey = vprev[:, hh, :], "vprev"
                        else:
                            vsrc, vkey = vcur[:, i - 1, hh, :], ("vcur", i - 1)
                    else:
                        vsrc, vkey = vcur[:, i, hh, :], ("vcur", i)
                    pi = pts[(i, hh, kb)]
                    E("pe", "matmul", r=[vkey, ("pt", pi)], w=[ok],
                      out=ob[:, :], lhsT=vsrc, rhs=pt[pi][:, :], start=(kb == 0), stop=False)
                E("pe", "matmul", r=["cbf", "sinkrow"], w=[ok],
                  out=ob[:, :], lhsT=cbf[0:1, B_SEL:B_SEL + 128], rhs=sinkrow[0:1, hh, :], start=False, stop=True)
                d1, dk1 = tmpf()
                E("act", "activation", r=[ok], w=[dk1], out=d1[0:64, :], in_=ob[64:128, :], func=AF.Ln)
                d2, dk2 = tmpf()
                E("act", "activation", r=[dk1], w=[dk2], out=d2[0:64, :], in_=d1[0:64, :], func=AF.Exp, scale=-1.0)
                E("dve", "tensor_tensor", r=[ok, dk2], w=[("ao", hh, i)],
                  out=ao[hh * 64:(hh + 1) * 64, :, i * 128:(i + 1) * 128],
                  in0=ob[0:64, :].rearrange("p (a b) -> p a b", a=4),
                  in1=d2[0:64, :].rearrange("p (a b) -> p a b", a=4), op=ALU.mult)

        work = [[] for _ in range(NB + 2)]
        for i in range(NB):
            work[i].append((attn_A, i))
            work[i + 2].append((attn_B, i))

        def conv_b_part(c, a2, k2):
            jb = CONV_J[("cb", c)]
            wap, wkey, coff = win_chunk(jb)
            pbb, pbk = proj(wap, wkey, 8, coff, un_r, un_k, T)
            E("dve", "scalar_tensor_tensor", r=[pbk, k2, "cols"], w=[("bc", c)],
              out=bc[:, c, :T], in0=pbb[:, :T], scalar=col(C_BIN + jb), in1=a2[:, :T], op0=ALU.add, op1=ALU.mult)

        pend_b = None
        for c in range(8):
            jc, jx = CONV_J[("cc", c)], CONV_J[("cx", c)]
            wap, wkey, coff = win_chunk(jc)
            pc, pck = proj(wap, wkey, 8, coff, un_r, un_k, T)
            cs, csk = tmpf()
            E("act", "activation", r=[pck, "cols"], w=[csk],
              out=cs[:, :T], in_=pc[:, :T], func=AF.Identity, bias=col(C_BIN + jc), scale=1.0)
            wap, wkey, coff = win_chunk(jx)
            px, pxk = proj(wap, wkey, 8, coff, un_r, un_k, T)
            xi = rot("cxb", 2)
            cx = cxb[xi]
            cxk = ("cxb", xi)
            E("pool", "tensor_copy", r=[("cxcarry", c)], w=[cxk], out=cx[:, PFX - 2:PFX], in_=cxcarry[:, c, :])
            E("dve", "scalar_tensor_tensor", r=[pxk, csk, "cols", cxk], w=[cxk],
              out=cx[:, PFX:PFX + T], in0=px[:, :T], scalar=col(C_BIN + jx), in1=cs[:, :T], op0=ALU.add, op1=ALU.mult)
            if last_halo:
                E("pool", "tensor_scalar", r=[cxk, "cols"], w=[("cxcarry", c)],
                  out=cxcarry[:, c, :], in0=cx[:, PFX + T - 2:PFX + T], scalar1=col(C_FLAG), scalar2=None, op0=ALU.mult)
            else:
                E("pool", "tensor_copy", r=[cxk], w=[("cxcarry", c)], out=cxcarry[:, c, :], in_=cx[:, PFX + T - 2:PFX + T])
            a0, k0 = tmpf()
            E("dve", "tensor_scalar", r=[cxk, "cols"], w=[k0],
              out=a0[:, :T], in0=cx[:, PFX:PFX + T], scalar1=col(C_CVW + 3 * c + 2), scalar2=None, op0=ALU.mult)
            a1, k1 = tmpf()
            E("dve", "scalar_tensor_tensor", r=[cxk, k0, "cols"], w=[k1],
              out=a1[:, :T], in0=cx[:, PFX - 1:PFX - 1 + T], scalar=col(C_CVW + 3 * c + 1), in1=a0[:, :T],
              op0=ALU.mult, op1=ALU.add)
            ci = rot("cva", 2)
            E("dve", "scalar_tensor_tensor", r=[cxk, k1, "cols"], w=[("silu", ci)],
              out=cva[ci][:, :T], in0=cx[:, PFX - 2:PFX - 2 + T], scalar=col(C_CVW + 3 * c), in1=a1[:, :T],
              op0=ALU.mult, op1=ALU.add)
            if pend_b is not None:
                conv_b_part(*pend_b)
            pend_b = (c, cva[ci], ("silu", ci))
            if work:
                for fn, arg in work.pop(0):
                    fn(arg)
        conv_b_part(*pend_b)
        while work:
            for fn, arg in work.pop(0):
                fn(arg)

        for c in range(8):
            for g, dst, dk in ((0, sgc, "sgc"), (1, sga, "sga")):
                j = 30 + 2 * c + g
                wap, wkey, coff = win_chunk(j)
                pg, pgk = proj(wap, wkey, 8, coff, un_r, un_k, T)
                E("act", "activation", r=[pgk, "cols"], w=[(dk, c)],
                  out=dst[:, c, :T], in_=pg[:, :T], func=AF.Sigmoid, bias=col(C_BIN + j), scale=1.0)
        ws_release()
        cur["w"] = None

        E("dve", "tensor_copy", r=["kr"], w=["kprev"], out=kprev[:, :], in_=kr[:, T - 128:T])
        E("dve", "tensor_copy", r=[("vcur", NB - 1)], w=["vprev"], out=vprev[:, :, :], in_=vcur[:, NB - 1, :, :])

        wco0, wco0k = ws_acquire()
        wao_, waok = ws_acquire()
        wco1 = wco1k = None
        aokeys = [("ao", hh, i) for hh in range(2) for i in range(NB)]
        for c in range(8):
            if c == 4:
                ws_release()
                wco1, wco1k = ws_acquire()
            wc, wck = (wco0, wco0k) if c < 4 else (wco1, wco1k)
            yc, yck = proj(wc, wck, 8, (c % 4) * 128, lambda k: bc[:, k, :T], lambda k: ("bc", k), T)
            ya, yak = bank()
            for j in range(4):
                E("pe", "matmul", r=[waok] + aokeys, w=[yak],
                  out=ya[:, :T], lhsT=wao_[:, j, c * 128:(c + 1) * 128], rhs=ao[:, j, :T],
                  start=(j == 0), stop=(j == 3))
            m1, mk1 = tmpf()
            E("dve", "tensor_tensor", r=[yck, ("sgc", c)], w=[mk1],
              out=m1[:, :T], in0=yc[:, :T], in1=sgc[:, c, :T], op=ALU.mult)
            m2, mk2 = tmpf()
            E("dve", "scalar_tensor_tensor", r=[yak, ("sga", c), "cols"], w=[mk2],
              out=m2[:, :T], in0=ya[:, :T], scalar=col(C_BAO + c), in1=sga[:, c, :T], op0=ALU.add, op1=ALU.mult)
            E("dve", "tensor_tensor", r=[mk1, mk2], w=[("mg", c)],
              out=mg[:, c, :T], in0=m1[:, :T], in1=m2[:, :T], op=ALU.add)
        ws_release()
        ws_release()

        wm = wmk = None
        sb2, sk2 = bank()
        held = sk2[1]
        pend_mm = None
        for c in range(8):
            if c % 4 == 0:
                if c == 4:
                    ws_release()
                wm, wmk = ws_acquire()
            bank_touch[held] = 10 ** 12
            pm, pmk = proj(wm, wmk, 8, (c % 4) * 128, lambda k: mg[:, k, :T], lambda k: ("mg", k), T)
            if pend_mm is not None:
                E("pe", "matmul", r=[("bc", pend_mm), "cbf"], w=[sk2],
                  out=sb2[:, :T], lhsT=ones_m, rhs=bc[:, pend_mm, :T], start=(pend_mm == 0), stop=False)
                bank_touch[held] = 10 ** 12
            E("dve", "tensor_tensor", r=[pmk, (xkey, c)], w=[(xkey, c)],
              out=xb[:, c, :T], in0=xb[:, c, :T], in1=pm[:, :T], op=ALU.add)
            E("act", "activation", r=[(xkey, c)], w=[("bc", c)],
              out=bc[:, c, :T], in_=xb[:, c, :T], func=AF.Square, scale=1.0 / 32.0)
            pend_mm = c
        ws_release()

        if ti + 1 < len(tiles):
            load_x(ti + 1)

        E("pe", "matmul", r=[("bc", 7), "cbf"], w=[sk2],
          out=sb2[:, :T], lhsT=ones_m, rhs=bc[:, 7, :T], start=False, stop=True)
        tb2, tk2 = tmpf()
        E("act", "activation", r=[sk2], w=[tk2], out=tb2[:, :T], in_=sb2[:, :T], func=AF.Ln, bias=EPS, scale=1.0)
        E("act", "activation", r=[tk2], w=[("rstd", 1)], out=rstd[1][:, :T], in_=tb2[:, :T], func=AF.Exp, scale=-0.5)
        norm_apply(xb, xkey, T, 1, C_NFW, un, "un")

        wunits = {}
        pre = {}

        def ffn_up_part(f):
            g = f // 2
            if g not in wunits:
                wunits[g] = ws_acquire()
            wu, wuk = wunits[g]
            lo = (f % 2) * 128
            if ("up", f) in pre:
                pu, puk = pre[("up", f)]
            else:
                pu, puk = proj(wu, wuk, 8, lo, un_r, un_k, T)
            ui = rot("upb", 3)
            ub = upb[ui]
            ubk = ("upb", ui)
            E("pool", "tensor_copy", r=[("upcarry", f)], w=[ubk], out=ub[:, PFX - 2:PFX], in_=upcarry[:, f, :])
            E("act", "activation", r=[puk, ubk], w=[ubk], out=ub[:, PFX:PFX + T], in_=pu[:, :T], func=AF.Copy)
            a0, k0 = tmpf()
            E("act", "activation", r=[puk, "cols"], w=[k0], out=a0[:, :T], in_=pu[:, :T], func=AF.Identity,
              scale=col(C_FCW + 3 * f + 2), bias=col(C_FCB + f))
            if last_halo:
                E("pool", "tensor_scalar", r=[ubk, "cols"], w=[("upcarry", f)],
                  out=upcarry[:, f, :], in0=ub[:, PFX + T - 2:PFX + T], scalar1=col(C_FLAG), scalar2=None, op0=ALU.mult)
            else:
                E("pool", "tensor_copy", r=[ubk], w=[("upcarry", f)], out=upcarry[:, f, :], in_=ub[:, PFX + T - 2:PFX + T])
            a1, k1 = tmpf()
            E("dve", "scalar_tensor_tensor", r=[ubk, k0, "cols"], w=[k1],
              out=a1[:, :T], in0=ub[:, PFX - 1:PFX - 1 + T], scalar=col(C_FCW + 3 * f + 1), in1=a0[:, :T],
              op0=ALU.mult, op1=ALU.add)
            a2, k2 = tmpf()
            E("dve", "scalar_tensor_tensor", r=[ubk, k1, "cols"], w=[k2],
              out=a2[:, :T], in0=ub[:, PFX - 2:PFX - 2 + T], scalar=col(C_FCW + 3 * f), in1=a1[:, :T],
              op0=ALU.mult, op1=ALU.add)
            return (a2, k2)

        def ffn_gate_part(f, a2k):
            a2, k2 = a2k
            si = rot("silu", 2)
            E("act", "activation", r=[k2], w=[("silu", si)], out=silub[si][:, :T], in_=a2[:, :T], func=AF.Silu)
            wu, wuk = wunits[f // 2]
            lo = (f % 2) * 128
            if ("gate", f) in pre:
                pg, pgk = pre[("gate", f)]
            else:
                pg, pgk = proj(wu, wuk, 8, 256 + lo, un_r, un_k, T)
            E("dve", "tensor_tensor", r=[pgk, ("silu", si)], w=[("hb", f)],
              out=hb[:, f, :T], in0=pg[:, :T], in1=silub[si][:, :T], op=ALU.mult)
            if f % 2 == 1:
                ws_release()

        if is_halo:
            for f in range(NFC):
                g = f // 2
                if f % 2 == 0:
                    wu, wuk = ws_acquire()
                pu, puk = proj(wu, wuk, 8, (f % 2) * 128, un_r, un_k, T)
                ui = rot("upb", 3)
                ub = upb[ui]
                ubk = ("upb", ui)
                E("act", "activation", r=[puk], w=[ubk], out=ub[:, PFX:PFX + T], in_=pu[:, :T], func=AF.Copy)
                if last_halo:
                    E("pool", "tensor_scalar", r=[ubk, "cols"], w=[("upcarry", f)],
                      out=upcarry[:, f, :], in0=ub[:, PFX + T - 2:PFX + T], scalar1=col(C_FLAG), scalar2=None,
                      op0=ALU.mult)
                else:
                    E("pool", "tensor_copy", r=[ubk], w=[("upcarry", f)], out=upcarry[:, f, :],
                      in_=ub[:, PFX + T - 2:PFX + T])
                if f % 2 == 1:
                    ws_release()
            if ti + 1 < len(tiles):
                prep(ti + 1)
        else:
            wunits[0] = ws_acquire()
            wu0, wuk0 = wunits[0]
            pbanks = [bank() for _ in range(4)]
            for k in range(8):
                for j in range(4):
                    E("pe", "matmul", r=[wuk0, ("un", k)], w=[pbanks[j][1]],
                      out=pbanks[j][0][:, :T], lhsT=wu0[:, k, j * 128:(j + 1) * 128], rhs=un[:, k, :T],
                      start=(k == 0), stop=(k == 7))
            pre[("up", 0)], pre[("up", 1)], pre[("gate", 0)], pre[("gate", 1)] = pbanks
            prev = None
            for f in range(NFC):
                si = ffn_up_part(f)
                if prev is not None:
                    ffn_gate_part(*prev)
                prev = (f, si)
            ffn_gate_part(*prev)

            for c in range(8):
                if c == 0 and ti + 1 < len(tiles):
                    prep_sq(ti + 1)
                wd, wdk = ws_acquire()
                pd, pdk = proj(wd, wdk, NFC, 0, lambda k: hb[:, k, :T], lambda k: ("hb", k), T)
                ws_release()
                if c == 2 and ti + 1 < len(tiles):
                    prep(ti + 1)
                E("dve", "tensor_tensor", r=[pdk, (xkey, c)], w=[(xkey, c)],
                  out=xb[:, c, :T], in0=xb[:, c, :T], in1=pd[:, :T], op=ALU.add)

        if ti + 1 == len(tiles) and not is_halo:
            finish_sq(ti)
            finish_tile(ti)

    E("sp", None, w=[("x0", c) for c in range(8)] + [("x1", c) for c in range(8)]
      + [("ring", i) for i in range(NSLOT)] + ["cbf", "posi", "cols"])
    P.emit()
    P.close()
    return nc, P


def _perm_in():
    cb0, cc0, cx0, q0, k0, v0, gc0, ga0 = 0, 1024, 2048, 3072, 3584, 3712, 3840, 4864
    perm = []
    perm += list(range(k0, k0 + 128))
    for j in range(4):
        perm += list(range(q0 + j * 64, q0 + (j + 1) * 64))
        perm += list(range(q0 + (4 + j) * 64, q0 + (5 + j) * 64))
    perm += list(range(v0, v0 + 128))
    base = {"cc": cc0, "cx": cx0, "cb": cb0}
    for kind, c in CONV_ORDER:
        perm += list(range(base[kind] + c * 128, base[kind] + (c + 1) * 128))
    for c in range(8):
        perm += list(range(gc0 + c * 128, gc0 + (c + 1) * 128))
        perm += list(range(ga0 + c * 128, ga0 + (c + 1) * 128))
    assert len(perm) == INW
    return np.array(perm, dtype=np.int64)


def _pkc(w):
    K, N = w.shape
    return np.ascontiguousarray(w.reshape(K // 128, 128, N).transpose(1, 0, 2))


def _colchunks(v):
    return np.ascontiguousarray(v.reshape(-1, 128).T)


def prepare_inputs(x, positions, norm_mix_w, w_in, b_in, conv_mix_w, w_conv_out, w_attn_out, b_attn_out, sinks,
                   w_mix_out, norm_ffn_w, w_ffn_up, ffn_conv_w, ffn_conv_b, w_ffn_down, norm_final_w):
    f32 = np.float32
    perm = _perm_in()
    win = _pkc(np.asarray(w_in[0], f32)[:, perm])
    wco = _pkc(np.asarray(w_conv_out[0], f32))
    arow = []
    for j in range(4):
        arow += list(range(j * 64, (j + 1) * 64)) + list(range((4 + j) * 64, (5 + j) * 64))
    wao = _pkc(np.asarray(w_attn_out[0], f32)[np.array(arow)])
    wmo = _pkc(np.asarray(w_mix_out[0], f32))
    uperm = []
    for g in range(NFC // 2):
        uperm += list(range(2 * g * 128, (2 * g + 2) * 128))
        uperm += list(range(DFF + 2 * g * 128, DFF + (2 * g + 2) * 128))
    wup = _pkc(np.asarray(w_ffn_up[0], f32)[:, np.array(uperm)])
    wd = np.asarray(w_ffn_down[0], f32).reshape(NFC, 128, 8, 128)
    wdn = np.ascontiguousarray(wd.transpose(1, 2, 0, 3))

    cols = np.zeros((128, NCOLS), f32)
    cols[:, C_NMW:C_NMW + 8] = _colchunks(np.asarray(norm_mix_w[0], f32))
    cols[:, C_NFW:C_NFW + 8] = _colchunks(np.asarray(norm_ffn_w[0], f32))
    cols[:, C_NLW:C_NLW + 8] = _colchunks(np.asarray(norm_final_w, f32))
    cols[:, C_BIN:C_BIN + 46] = _colchunks(np.asarray(b_in[0], f32)[perm])
    cols[:, C_BAO:C_BAO + 8] = _colchunks(np.asarray(b_attn_out[0], f32))
    cw = np.asarray(conv_mix_w[0], f32)
    cols[:, C_CVW:C_CVW + 24] = cw.reshape(3, 8, 128).transpose(2, 1, 0).reshape(128, 24)
    fw = np.asarray(ffn_conv_w[0], f32)
    cols[:, C_FCW:C_FCW + 66] = fw.reshape(3, NFC, 128).transpose(2, 1, 0).reshape(128, 66)
    cols[:, C_FCB:C_FCB + NFC] = _colchunks(np.asarray(ffn_conv_b[0], f32))
    inv_freq = (10000.0 ** (-np.arange(0, HD, 2, dtype=f32) / HD)).astype(f32)
    cols[:, C_INVF] = inv_freq[np.arange(128) % 32]
    cols[:, C_SINK:C_SINK + 8] = np.asarray(sinks[0], f32)[None, :]
    cols[:, C_BV:C_BV + 128] = np.asarray(b_in[0], f32)[3712:3840][None, :]

    cbf = np.zeros((128, NBCOLS), f32)
    cbf[:, B_ONES:B_ONES + 128] = 1.0
    rm = np.zeros((128, 128), f32)
    for m in range(128):
        if m % 64 < 32:
            rm[m + 32, m] = -1.0
        else:
            rm[m - 32, m] = 1.0
    cbf[:, B_ROT:B_ROT + 128] = rm
    kk = np.arange(128)[:, None]
    qq = np.arange(128)[None, :]
    mp = (kk > qq).astype(f32)
    mc = (kk <= qq).astype(f32)
    cbf[:, B_MPREV:B_MPREV + 512] = np.tile(mp, (1, 4))
    cbf[:, B_MCUR:B_MCUR + 512] = np.tile(mc, (1, 4))
    cbf[0, B_SEL + 64:B_SEL + 128] = 1.0

    x = np.asarray(x, f32)
    positions = np.asarray(positions, np.int32)
    in_maps = []
    for core in range(NCORE):
        b, half = core // 2, core % 2
        t0 = half * TOK
        xT = np.zeros((D, NTOK), f32)
        xT[:, HALO:] = x[b, t0:t0 + TOK, :].T
        pos = np.zeros((NTOK,), np.int32)
        pos[HALO:] = positions[b, t0:t0 + TOK]
        ccols = cols.copy()
        ccbf = cbf.copy()
        if half == 1:
            xT[:, :HALO] = x[b, t0 - HALO:t0, :].T
            pos[:HALO] = positions[b, t0 - HALO:t0]
            ccols[:, C_FLAG] = 1.0
            ccbf[:, B_MFIRST:B_MFIRST + 512] = np.tile(mp, (1, 4))
        in_maps.append({
            "xT": xT, "posb": np.ascontiguousarray(np.broadcast_to(pos[None, :], (128, NTOK))),
            "cols": ccols, "cbf": ccbf, "win": win, "wco": wco, "wao": wao, "wmo": wmo, "wup": wup, "wdn": wdn,
        })
    return in_maps


def kernel(**inputs):
    in_maps = prepare_inputs(**inputs)
    nc, _ = build_program()
    res = run_bass_kernel_spmd(nc, in_maps, core_ids=list(range(NCORE)))
    out = np.empty((BATCH, SEQ, D), np.float32)
    for core in range(NCORE):
        b, half = core // 2, core % 2
        out[b, half * TOK:(half + 1) * TOK, :] = res.results[core]["outT"].T
    return out
```
